# Optimizing a Trainium2 kernel written in Bass

```python
import math
import jax, jax.numpy as jnp
from jax import lax
import numpy as np

D_MODEL = 1024
BATCH = 32
SEQ = 2048
DEPTH = 1
DEC_BATCH = 128
DEC_SEQ = 4
PAST_LEN = 8192
PAGE_SIZE = 128

HEAD_DIM = 64
GROUP_HEADS = 4
ATT_GROUPS = ((128, 1), (512, 4), (2048, 16))
N_GROUPS = len(ATT_GROUPS)
N_ATT_HEADS = N_GROUPS * GROUP_HEADS
ATT_W = N_ATT_HEADS * HEAD_DIM
ATT_OUT_W = GROUP_HEADS * HEAD_DIM
BLK = 128
N_BUCKETS = 32
MAX_DISTANCE = 2048
SGU_CHUNK = 128
SGU_GROUPS = 4
SGU_GROUP_W = 128
SGU_W = SGU_GROUPS * SGU_GROUP_W
D_FF = 2816
LN_EPS = 1e-5
ALPHA = (2 * DEPTH) ** 0.25
BETA = (8 * DEPTH) ** -0.25
IN_W = 3 * ATT_W + 2 * SGU_W + 2 * D_MODEL
NEG = -1e30
SCALE = HEAD_DIM ** -0.5

kernel_name = 'dilated_attn_gmlp_gated_macaron_step'


def _t5_bucket(dist):
    dist = np.asarray(dist, np.int64)
    max_exact = N_BUCKETS // 2
    large = max_exact + (np.log(np.maximum(dist, max_exact) / max_exact)
                         / np.log(MAX_DISTANCE / max_exact) * (N_BUCKETS - max_exact)).astype(np.int64)
    large = np.minimum(large, N_BUCKETS - 1)
    return np.where(dist < max_exact, dist, large).astype(np.int32)


def layer_norm(x, g, b):
    xf = x.astype(jnp.float32)
    mu = jnp.mean(xf, axis=-1, keepdims=True)
    var = jnp.mean(jnp.square(xf - mu), axis=-1, keepdims=True)
    return ((xf - mu) * lax.rsqrt(var + LN_EPS) * g.astype(jnp.float32) + b.astype(jnp.float32)).astype(x.dtype)


def swiglu(x, w1, w3, w2):
    return (jax.nn.silu(x @ w1) * (x @ w3)) @ w2


def dilated_attn_prompt(q, k, v, table, window, dil):
    B, S, H, E = q.shape
    L = S // dil
    band = window // dil
    assert band <= BLK
    nb = -(-L // BLK)
    Lp = nb * BLK

    def sub(a):
        return a.reshape(B, L, dil, H, E).transpose(0, 2, 3, 1, 4)

    qs = jnp.pad(sub(q), ((0, 0), (0, 0), (0, 0), (0, Lp - L), (0, 0))).reshape(B, dil, H, nb, BLK, E)

    def kblocks(a):
        ap = jnp.pad(sub(a), ((0, 0), (0, 0), (0, 0), (BLK, Lp - L), (0, 0)))
        prev = ap[:, :, :, :Lp].reshape(B, dil, H, nb, BLK, E)
        cur = ap[:, :, :, BLK:].reshape(B, dil, H, nb, BLK, E)
        return jnp.concatenate([prev, cur], axis=-2)

    kb, vb = kblocks(k), kblocks(v)
    rel = np.arange(BLK)[:, None] - np.arange(2 * BLK)[None, :] + BLK
    valid = (rel >= 0) & (rel <= band)
    bias = table.astype(jnp.float32)[_t5_bucket(np.clip(rel, 0, band) * dil)].transpose(2, 0, 1)
    kvalid = ~((np.arange(nb)[:, None] == 0) & (np.arange(2 * BLK)[None, :] < BLK))
    mask = valid[None] & kvalid[:, None, :]
    s = jnp.einsum('bdhnqe,bdhnke->bdhnqk', qs.astype(jnp.float32), kb.astype(jnp.float32)) * SCALE + bias[:, None]
    s = jnp.where(mask, s, NEG)
    m = jnp.max(s, axis=-1, keepdims=True)
    p = jnp.exp(s - m)
    l = jnp.sum(p, axis=-1, keepdims=True)
    o = jnp.einsum('bdhnqk,bdhnke->bdhnqe', p, vb.astype(jnp.float32)) / l
    lse = (m + jnp.log(l))[..., 0]
    o = o.reshape(B, dil, H, Lp, E)[:, :, :, :L].transpose(0, 3, 1, 2, 4).reshape(B, S, H, E)
    lse = lse.reshape(B, dil, H, Lp)[..., :L].transpose(0, 3, 1, 2).reshape(B, S, H)
    return o, lse


def dilated_attn_sample(q, k, v, kv_buf, table, window, dil):
    Bd, T, H, E = q.shape
    Lg = kv_buf.shape[1]
    band = window // dil
    full = jnp.concatenate([kv_buf, jnp.stack([k, v], axis=2).astype(kv_buf.dtype)], axis=1)
    steps = np.arange(band + 1)
    idx = Lg + np.arange(T)[:, None] - steps[None, :] * dil
    valid = idx >= 0
    kvg = full[:, np.maximum(idx, 0)]
    bias = table.astype(jnp.float32)[_t5_bucket(steps * dil)]
    s = jnp.einsum('bthe,btkhe->bthk', q.astype(jnp.float32), kvg[:, :, :, 0].astype(jnp.float32)) * SCALE + bias.T
    s = jnp.where(valid[:, None, :], s, NEG)
    m = jnp.max(s, axis=-1, keepdims=True)
    p = jnp.exp(s - m)
    l = jnp.sum(p, axis=-1, keepdims=True)
    o = jnp.einsum('bthk,btkhe->bthe', p, kvg[:, :, :, 1].astype(jnp.float32)) / l
    lse = (m + jnp.log(l))[..., 0]
    keep = min(window, full.shape[1])
    return o, lse, full[:, full.shape[1] - keep:]


def combine_groups(outs, lses):
    w = jax.nn.softmax(jnp.stack(lses, axis=-1), axis=-1)
    o = jnp.sum(jnp.stack(outs, axis=-2) * w[..., None], axis=-2)
    return o.reshape(*o.shape[:-2], ATT_OUT_W)


def sgu_branch(uv, ln_g, ln_b, ws, sb):
    Bn, L, _ = uv.shape
    z = jax.nn.gelu(uv, approximate=False)
    u, vr = z[..., :SGU_W], z[..., SGU_W:]
    vn = layer_norm(vr, ln_g, ln_b)
    cl = min(SGU_CHUNK, L)
    vc = vn.reshape(Bn, L // cl, cl, SGU_GROUPS, SGU_GROUP_W)
    wm = jnp.tril(ws)[:, :cl, :cl]
    mixed = jnp.einsum('gpq,bnqgc->bnpgc', wm, vc) + sb[:, :cl].T[:, :, None]
    return u * mixed.reshape(Bn, L, SGU_W), vn


def trunk_layer(x, rel_bias, kv_bufs, ln1_g, ln1_b, f1_w1, f1_w3, f1_w2, w_in, b_in,
                sgu_ln_g, sgu_ln_b, sgu_ws, sgu_b, w_oa, w_ob, w_out,
                ln2_g, ln2_b, f2_w1, f2_w3, f2_w2, ln3_g, ln3_b):
    h = layer_norm(ALPHA * x + 0.5 * swiglu(x, f1_w1, f1_w3, f1_w2), ln1_g, ln1_b)
    z = h @ w_in + b_in
    lead = z.shape[:-1]
    q = z[..., :ATT_W].reshape(*lead, N_GROUPS, GROUP_HEADS, HEAD_DIM)
    k = z[..., ATT_W:2 * ATT_W].reshape(*lead, N_GROUPS, GROUP_HEADS, HEAD_DIM)
    v = z[..., 2 * ATT_W:3 * ATT_W].reshape(*lead, N_GROUPS, GROUP_HEADS, HEAD_DIM)
    o_uv = 3 * ATT_W
    uv = z[..., o_uv:o_uv + 2 * SGU_W]
    ga = z[..., o_uv + 2 * SGU_W:o_uv + 2 * SGU_W + D_MODEL]
    gb = z[..., o_uv + 2 * SGU_W + D_MODEL:]
    outs, lses, new_bufs = [], [], []
    for g, (window, dil) in enumerate(ATT_GROUPS):
        tab = rel_bias[:, g * GROUP_HEADS:(g + 1) * GROUP_HEADS]
        qg, kg, vg = q[..., g, :, :], k[..., g, :, :], v[..., g, :, :]
        if kv_bufs is None:
            o, lse = dilated_attn_prompt(qg, kg, vg, tab, window, dil)
            S = x.shape[1]
            keep = min(window, S)
            nbuf = jnp.stack([kg, vg], axis=2)[:, S - keep:]
        else:
            o, lse, nbuf = dilated_attn_sample(qg, kg, vg, kv_bufs[g], tab, window, dil)
        outs.append(o)
        lses.append(lse)
        new_bufs.append(nbuf)
    att = combine_groups(outs, lses).astype(h.dtype)
    yb, vrows = sgu_branch(uv, sgu_ln_g, sgu_ln_b, sgu_ws, sgu_b)
    mix = (jax.nn.sigmoid(ga) * (att @ w_oa) + jax.nn.sigmoid(gb) * (yb @ w_ob)) @ w_out
    h = layer_norm(ALPHA * h + mix, ln2_g, ln2_b)
    h = layer_norm(ALPHA * h + 0.5 * swiglu(h, f2_w1, f2_w3, f2_w2), ln3_g, ln3_b)
    return h, new_bufs, vrows


def setup_inputs(seed: int = 0) -> dict:
    key = jax.random.key(seed)
    ks = iter(jax.random.split(key, 40))

    def nrm(shape, scale):
        return jax.random.normal(next(ks), shape, jnp.float32) * scale

    def gain(shape):
        return 1.0 + nrm(shape, 0.05)

    inp = {}
    inp['x_prompt'] = nrm((BATCH, SEQ, D_MODEL), 1.0)
    inp['x_sample'] = nrm((DEC_BATCH, DEC_SEQ, D_MODEL), 1.0)
    for g, (window, dil) in enumerate(ATT_GROUPS):
        inp['state_win%d' % g] = nrm((DEPTH, DEC_BATCH, min(window, PAST_LEN), 2, GROUP_HEADS, HEAD_DIM), 1.0)
    inp['rel_bias'] = nrm((N_BUCKETS, N_ATT_HEADS), 0.3)
    inp['ln1_g'] = gain((DEPTH, D_MODEL))
    inp['ln1_b'] = nrm((DEPTH, D_MODEL), 0.05)
    inp['f1_w1'] = nrm((DEPTH, D_MODEL, D_FF), D_MODEL ** -0.5)
    inp['f1_w3'] = nrm((DEPTH, D_MODEL, D_FF), D_MODEL ** -0.5)
    inp['f1_w2'] = nrm((DEPTH, D_FF, D_MODEL), BETA * D_FF ** -0.5)
    inp['w_in'] = nrm((DEPTH, D_MODEL, IN_W), D_MODEL ** -0.5)
    inp['b_in'] = nrm((DEPTH, IN_W), 0.02)
    inp['sgu_ln_g'] = gain((DEPTH, SGU_W))
    inp['sgu_ln_b'] = nrm((DEPTH, SGU_W), 0.05)
    inp['sgu_ws'] = nrm((DEPTH, SGU_GROUPS, SGU_CHUNK, SGU_CHUNK), 0.5 * SGU_CHUNK ** -0.5)
    inp['sgu_b'] = gain((DEPTH, SGU_GROUPS, SGU_CHUNK))
    inp['w_oa'] = nrm((DEPTH, ATT_OUT_W, D_MODEL), BETA * ATT_OUT_W ** -0.5)
    inp['w_ob'] = nrm((DEPTH, SGU_W, D_MODEL), BETA * SGU_W ** -0.5)
    inp['w_out'] = nrm((DEPTH, D_MODEL, D_MODEL), BETA * D_MODEL ** -0.5)
    inp['ln2_g'] = gain((DEPTH, D_MODEL))
    inp['ln2_b'] = nrm((DEPTH, D_MODEL), 0.05)
    inp['f2_w1'] = nrm((DEPTH, D_MODEL, D_FF), D_MODEL ** -0.5)
    inp['f2_w3'] = nrm((DEPTH, D_MODEL, D_FF), D_MODEL ** -0.5)
    inp['f2_w2'] = nrm((DEPTH, D_FF, D_MODEL), BETA * D_FF ** -0.5)
    inp['ln3_g'] = gain((DEPTH, D_MODEL))
    inp['ln3_b'] = nrm((DEPTH, D_MODEL), 0.05)
    return inp


def reference(x_prompt, x_sample, state_win0, state_win1, state_win2, rel_bias,
              ln1_g, ln1_b, f1_w1, f1_w3, f1_w2, w_in, b_in,
              sgu_ln_g, sgu_ln_b, sgu_ws, sgu_b, w_oa, w_ob, w_out,
              ln2_g, ln2_b, f2_w1, f2_w3, f2_w2, ln3_g, ln3_b):
    yp, ys = x_prompt, x_sample
    pw0, pw1, pw2, sw0, sw1, sw2, sv = [], [], [], [], [], [], []
    for l in range(DEPTH):
        lw = (ln1_g[l], ln1_b[l], f1_w1[l], f1_w3[l], f1_w2[l], w_in[l], b_in[l],
              sgu_ln_g[l], sgu_ln_b[l], sgu_ws[l], sgu_b[l], w_oa[l], w_ob[l], w_out[l],
              ln2_g[l], ln2_b[l], f2_w1[l], f2_w3[l], f2_w2[l], ln3_g[l], ln3_b[l])
        yp, nbp, _ = trunk_layer(yp, rel_bias, None, *lw)
        ys, nbs, vrows = trunk_layer(ys, rel_bias, (state_win0[l], state_win1[l], state_win2[l]), *lw)
        pw0.append(nbp[0]); pw1.append(nbp[1]); pw2.append(nbp[2])
        sw0.append(nbs[0]); sw1.append(nbs[1]); sw2.append(nbs[2])
        sv.append(vrows)
    return (yp, ys, jnp.stack(pw0), jnp.stack(pw1), jnp.stack(pw2),
            jnp.stack(sw0), jnp.stack(sw1), jnp.stack(sw2), jnp.stack(sv))
```

```python
import math
from contextlib import ExitStack

import numpy as np
import ml_dtypes
import concourse.bass as bass
import concourse.mybir as mybir
from concourse.bass_utils import run_bass_kernel_spmd

F32 = mybir.dt.float32
BF16 = mybir.dt.bfloat16
AF = mybir.ActivationFunctionType
ALU = mybir.AluOpType

D = 1024
DC = 8
DFF = 2816
FC = 22
SEQ = 2048
NCORES = 8
NSEQ = 4
T = 512
NTL = SEQ // T
SB = 16
TS = 64
WIN = (128, 512, 2048)
DIL = (1, 4, 16)
ALPHA = 2.0 ** 0.25
LN_EPS = 1e-5
EPS2 = LN_EPS / (ALPHA * ALPHA)
SCALE = 0.125
NEGB = -30000.0
CH = 16
CHW = CH * 128
NSLOT = 6
NPOOL = 24

OQ, OK_, OV, OU, OVV, OGA, OGB = 0, 768, 1536, 2304, 2816, 3328, 4352


def _stream_plan():
    blocks = []
    idx = {}

    def add(key, name, r0, c0):
        idx[key] = len(blocks)
        blocks.append((name, r0, c0))

    for pre, (w1, w3, w2) in (("f1", ("f1_w1", "f1_w3", "f1_w2")),):
        for fc in range(FC):
            for dc in range(DC):
                add((pre + "w1", fc, dc), w1, dc * 128, fc * 128)
            for dc in range(DC):
                add((pre + "w3", fc, dc), w3, dc * 128, fc * 128)
        for oc in range(DC):
            for fc in range(FC):
                add((pre + "w2", oc, fc), w2, fc * 128, oc * 128)
    fm_cols = [OQ + i * 128 for i in range(6)] + [OK_ + i * 128 for i in range(6)] + [OU + i * 128 for i in range(4)]
    for cc, c0 in enumerate(fm_cols):
        for dc in range(DC):
            add(("wq", cc, dc), "w_in", dc * 128, c0)
    while len(blocks) % 32:
        blocks.append(None)
    for cg in (3, 0, 1, 2):
        if cg < 3:
            cols = [OK_ + cg * 256, OK_ + cg * 256 + 128, OV + cg * 256, OV + cg * 256 + 128]
        else:
            cols = [OVV + i * 128 for i in range(4)]
        for dc in range(DC):
            for i, c0 in enumerate(cols):
                add(("wtm", cg, dc, i), "w_in", dc * 128, c0)
    for oc in range(DC):
        for dc in range(DC):
            add(("ga", oc, dc), "w_in", dc * 128, OGA + oc * 128)
        for kc in range(2):
            add(("oa", oc, kc), "w_oa", kc * 128, oc * 128)
        for dc in range(DC):
            add(("gb", oc, dc), "w_in", dc * 128, OGB + oc * 128)
        for kc in range(4):
            add(("ob", oc, kc), "w_ob", kc * 128, oc * 128)
    for oc in range(DC):
        for kc in range(DC):
            add(("wo", oc, kc), "w_out", kc * 128, oc * 128)
    for pre, (w1, w3, w2) in (("f2", ("f2_w1", "f2_w3", "f2_w2")),):
        for fc in range(FC):
            for dc in range(DC):
                add((pre + "w1", fc, dc), w1, dc * 128, fc * 128)
            for dc in range(DC):
                add((pre + "w3", fc, dc), w3, dc * 128, fc * 128)
        for oc in range(DC):
            for fc in range(FC):
                add((pre + "w2", oc, fc), w2, fc * 128, oc * 128)
    return blocks, idx


BLOCKS, BIDX = _stream_plan()
NBLK = len(BLOCKS)
NCH = (NBLK + CH - 1) // CH


def _t5_bucket(dist):
    dist = np.asarray(dist, np.int64)
    max_exact = 16
    large = max_exact + (np.log(np.maximum(dist, max_exact) / max_exact) / np.log(2048 / max_exact) * 16).astype(np.int64)
    large = np.minimum(large, 31)
    return np.where(dist < max_exact, dist, large).astype(np.int64)


class _Res:
    __slots__ = ("w", "rd")

    def __init__(self):
        self.w = None
        self.rd = {}


class Ctx:
    def __init__(self, nc):
        self.nc = nc
        self.es = ExitStack()
        self.h = {"pe": nc.tensor, "act": nc.scalar, "dve": nc.vector, "pool": nc.gpsimd, "sp": nc.sync}
        self.sems = {}
        self.cnt = {}
        for name in self.h:
            self.sems[name] = self.es.enter_context(nc.semaphore("s_" + name))
            self.cnt[name] = 0
        self.pool = []
        for i in range(NPOOL):
            k = "d%d" % i
            self.sems[k] = self.es.enter_context(nc.semaphore("s_" + k))
            self.cnt[k] = 0
            self.pool.append(k)
        self.pool_i = 0
        self.res = {}
        self.seen = {name: {} for name in self.h}
        self.nins = {name: 0 for name in self.h}
        self.banks = []
        self.bank_rot = list(range(8))
        self.bank_i = 0

    def sb(self, name, shape, dt, es=None):
        return (es or self.es).enter_context(self.nc.sbuf_tensor("sb_" + name, shape, dt))

    def _wait(self, eng, tok):
        k, v = tok
        if self.seen[eng].get(k, 0) < v:
            self.h[eng].wait_ge(self.sems[k], v)
            self.seen[eng][k] = v

    def _deps(self, eng, R, W):
        need = {}

        def add(tok, raw):
            k, v = tok
            if k == eng and (not raw or eng == "pe"):
                return
            if need.get(k, 0) < v:
                need[k] = v

        for key in R:
            r = self.res.get(key)
            if r is not None and r.w is not None:
                add(r.w, True)
        for key in W:
            r = self.res.get(key)
            if r is not None:
                if r.w is not None:
                    add(r.w, False)
                for k, v in r.rd.items():
                    add((k, v), False)
        for k, v in need.items():
            self._wait(eng, (k, v))

    def _record(self, tok, R, W):
        k, v = tok
        for key in R:
            r = self.res.get(key)
            if r is None:
                r = self.res[key] = _Res()
            if r.rd.get(k, 0) < v:
                r.rd[k] = v
        for key in W:
            r = self.res.get(key)
            if r is None:
                r = self.res[key] = _Res()
            r.w = tok
            r.rd = {}

    def op(self, eng, fn, R=(), W=()):
        self._deps(eng, R, W)
        ins = fn(self.h[eng])
        self.cnt[eng] += 1
        ins.then_inc(self.sems[eng], 1)
        tok = (eng, self.cnt[eng])
        self._record(tok, R, W)
        self.nins[eng] += 1
        return tok

    def mm(self, items, W=()):
        allR = []
        ins = None
        first = True
        for fn, R in items:
            self._deps("pe", R, W if first else ())
            first = False
            ins = fn(self.h["pe"])
            self.nins["pe"] += 1
            for k in R:
                if k not in allR:
                    allR.append(k)
        self.cnt["pe"] += 1
        ins.then_inc(self.sems["pe"], 1)
        tok = ("pe", self.cnt["pe"])
        self._record(tok, allR, W)
        return tok

    def dma(self, out, in_, R=(), W=(), q="pool"):
        ch = self.pool[self.pool_i]
        self.pool_i = (self.pool_i + 1) % NPOOL
        if self.cnt[ch] > 0:
            self._wait(q, (ch, self.cnt[ch]))
        self._deps(q, R, W)
        ins = self.h[q].dma_start(out=out, in_=in_)
        self.cnt[ch] += 16
        ins.then_inc(self.sems[ch], 16)
        tok = (ch, self.cnt[ch])
        self._record(tok, R, W)
        self.nins[q] += 1
        return tok

    def barrier(self):
        for e in self.h:
            for k in list(self.cnt):
                if k != e and self.cnt[k] > 0:
                    self._wait(e, (k, self.cnt[k]))

    def finish(self, q="sp"):
        for k in self.cnt:
            if k != q and self.cnt[k] > 0:
                self._wait(q, (k, self.cnt[k]))

    def bank(self):
        i = self.bank_rot[self.bank_i % len(self.bank_rot)]
        self.bank_i += 1
        return self.banks[i], ("pb", i)


def build(nseq=NSEQ, do_sample=True, debug=False):
    nc = bass.Bass("TRN2", target_bir_lowering=False)
    c = Ctx(nc)
    dbg = {}
    if debug:
        for nm in ("r1", "h", "r2", "h2"):
            dbg[nm] = nc.dram_tensor("dbg_" + nm, [128, DC, T], F32, kind="ExternalOutput").ap()

    def din(name, shape, dt=F32):
        return nc.dram_tensor(name, list(shape), dt, kind="ExternalInput").ap()

    def dout(name, shape):
        return nc.dram_tensor(name, list(shape), F32, kind="ExternalOutput").ap()

    xp = din("xp", [nseq, 128, DC, SEQ])
    xs = din("xs", [128, DC, TS])
    wst = din("wst", [128, NCH * CHW])
    pp_d = din("pp", [128, 80])
    NBT = 2048 + 512 + 512 + 512
    bt_d = din("bt", [128, NBT])
    bts_d = din("bts", [128, 1024])
    wsT_d = din("wsT", [128, 4, 128])
    wsTs_d = din("wsTs", [128, 4, 64])
    relb_d = din("relb", [32, 12])
    cf_d = din("cf", [128, 128 + 128 + 64 + 127 + 256])
    cb_d = din("cb", [128, 128 + 64 + 128 + 127], BF16)
    oh_d = din("oh", [32, 3 * 383])
    mka_d = din("mka", [4, 3 * 383])
    ohA_d = din("ohA", [32, 12 * 128])
    mkA_d = din("mkA", [128, 48])
    ohB_d = din("ohB", [32, 12 * 64])
    mkB_d = din("mkB", [64, 48])
    st_d = [din("st%d" % g, [SB, WIN[g], 512]) for g in range(3)]

    yp = dout("yp", [nseq, 128, DC, SEQ])
    ys = dout("ys", [128, DC, TS])
    pw = [dout("pw%d" % g, [nseq, WIN[g], 512]) for g in range(3)]
    sw = [dout("sw%d" % g, [SB, WIN[g], 512]) for g in range(3)]
    sv = dout("sv", [TS, 512])

    wbf_t = nc.dram_tensor("wbf", [NCH, 128, CHW], BF16, kind="Internal")
    wbf = wbf_t.ap()
    ebs_t = nc.dram_tensor("ebs", [12, 383], F32, kind="Internal")
    qsc_t = nc.dram_tensor("qsc", [TS, 768], F32, kind="Internal")

    for i in range(8):
        c.banks.append(c.es.enter_context(nc.psum_tensor("pb%d" % i, [128, 512], F32)))
    ws = [c.sb("ws%d" % i, [128, CHW], BF16) for i in range(NSLOT)]
    pp = c.sb("pp", [128, 80], F32)
    bt = c.sb("bt", [128, NBT], F32)
    cf = c.sb("cf", [128, 128 + 128 + 64 + 127 + 256], F32)
    cb = c.sb("cb", [128, 128 + 64 + 128 + 127], BF16)
    EB = c.sb("EB", [128, 12, 256], BF16)
    wm = c.sb("wm", [128, 4, 128], BF16)
    wms = c.sb("wms", [128, 4, 64], BF16)
    relb = c.sb("relb", [32, 12], F32)
    epsc = c.sb("epsc", [128, 2], F32)
    dmy = c.sb("dmy", [128, 2], F32)

    identf = cf[:, 0:128]
    triu = cf[:, 128:256]
    masks = cf[:, 256:320]
    zsel = cf[:, 320:447]
    gsel = cf[:, 447:703]
    onesb = cb[:, 0:128]
    ones64 = cb[:, 128:192]
    Jb = cb[:, 192:320]
    zselb = cb[:, 320:447]
    PLN = {1: (0, 8), 2: (16, 24), 3: (32, 40)}
    PBQ, PBK, PBU, PBGA, PBGB = 48, 54, 60, 64, 72
    BKV, BSG, BSB, BSGB = 0, 2048, 2560, 3072
    BQ, BSGBS = 0, 768

    c.dma(pp[:], pp_d[:, :], W=["pp"])
    c.dma(bt[:], bt_d[:, :], W=["bt"])
    c.dma(cf[:], cf_d[:, :], W=["cf"])
    c.dma(cb[:], cb_d[:, :], W=["cb"])
    c.dma(relb[:], relb_d[:, :], W=["relb"])
    c.op("dve", lambda e: e.memset(epsc[:, 0:1], EPS2), W=["epsc"])
    c.op("dve", lambda e: e.memset(epsc[:, 1:2], LN_EPS), R=["epsc"], W=["epsc"])

    if do_sample:
        for g in range(3):
            n = (WIN[g] - 4) * 512
            for b in range(SB):
                src = bass.AP(tensor=st_d[g].tensor, offset=b * WIN[g] * 512 + 4 * 512, ap=[[n // 16, 16], [1, n // 16]])
                dst = bass.AP(tensor=sw[g].tensor, offset=b * WIN[g] * 512, ap=[[n // 16, 16], [1, n // 16]])
                c.dma(dst, src, W=[("swcopy", g, b)])

    with ExitStack() as p0:
        oh = c.sb("oh", [32, 3 * 383], F32, p0)
        mka = c.sb("mka", [4, 3 * 383], F32, p0)
        ext = c.sb("ext", [4, 3 * 383], F32, p0)
        hank = c.sb("hank", [128, 12, 256], F32, p0)
        hankb = c.sb("hankb", [128, 12, 256], BF16, p0)
        wsT = c.sb("wsT", [128, 4, 128], F32, p0)
        wsTs = c.sb("wsTs", [128, 4, 64], F32, p0)
        c.dma(oh[:], oh_d[:, :], W=["oh"])
        c.dma(mka[:], mka_d[:, :], W=["mka"])
        c.dma(wsT[:], wsT_d[:, :, :], W=["wsT"])
        c.dma(wsTs[:], wsTs_d[:, :, :], W=["wsTs"])
        for g in range(3):
            pb_, pk = c.bank()
            c.mm([(lambda e, g=g, pb_=pb_: e.matmul(pb_[0:4, 0:383], lhsT=relb[:, 4 * g:4 * g + 4], rhs=oh[:, g * 383:(g + 1) * 383],
                                                  start=True, stop=True), ["relb", "oh"])], W=[pk])
            c.op("dve", lambda e, g=g, pb_=pb_: e.tensor_tensor(out=ext[:, g * 383:(g + 1) * 383], in0=pb_[0:4, 0:383],
                                                             in1=mka[:, g * 383:(g + 1) * 383], op=ALU.add), R=[pk, "mka"], W=[("ext", g)])
            c.op("act", lambda e, g=g: e.activation(out=ext[:, g * 383:(g + 1) * 383], in_=ext[:, g * 383:(g + 1) * 383], func=AF.Exp),
                 R=[("ext", g)], W=[("ext", g)])
            c.dma(ebs_t.ap()[4 * g:4 * g + 4, :], ext[:, g * 383:(g + 1) * 383], R=[("ext", g)], W=[("ebs", g)])
        src = bass.AP(tensor=ebs_t, offset=0, ap=[[1, 128], [383, 12], [1, 256]])
        c.dma(hank[:, :, :], src, R=[("ebs", g) for g in range(3)], W=["hank"])
        c.op("dve", lambda e: e.tensor_copy(out=hankb[:], in_=hank[:]), R=["hank"], W=["hankb"])
        for hh in range(12):
            pb_, pk = c.bank()
            c.mm([(lambda e, hh=hh, pb_=pb_: e.matmul(pb_[:, 0:256], lhsT=Jb, rhs=hankb[:, hh, :], start=True, stop=True), ["cb", "hankb"])], W=[pk])
            c.op("act", lambda e, hh=hh, pb_=pb_: e.copy(out=EB[:, hh, :], in_=pb_[:, 0:256]), R=[pk], W=["EB"])
        c.op("dve", lambda e: e.tensor_tensor(out=wm[:], in0=wsT[:], in1=triu.unsqueeze(1).to_broadcast([128, 4, 128]), op=ALU.mult),
             R=["wsT", "cf"], W=["wm"])
        c.op("dve", lambda e: e.tensor_tensor(out=wms[0:64], in0=wsTs[0:64], in1=masks[0:64].unsqueeze(1).to_broadcast([64, 4, 64]), op=ALU.mult),
             R=["wsTs", "cf"], W=["wms"])
        c.barrier()

    S = [c.sb("S0", [128, DC, T], F32), None]
    Sb = c.sb("Sb", [128, DC, T], BF16)
    t8 = c.sb("t8", [128, DC, T], BF16)
    uT = c.sb("uT", [128, 4, T], BF16)
    vn = c.sb("vn", [128, 4, 512], BF16)
    vt = [c.sb("vt%d" % i, [128, 512], F32) for i in range(4)]
    attb = c.sb("attb", [128, 2, T], BF16)
    rL = c.sb("rL", [128, T], F32)
    yb = c.sb("yb", [128, 4, T], BF16)
    rstd = c.sb("rstd", [128, T], F32)
    sil = [c.sb("sil%d" % i, [128, T], F32) for i in range(2)]
    sg = [c.sb("sg%d" % i, [128, T], F32) for i in range(2)]
    m1 = [c.sb("m1_0", [128, T], F32)]
    mean_s = m1[0]
    st6 = c.sb("st6", [128, 4, 6], F32)
    mv = c.sb("mv", [128, 4, 4], F32)
    ln_pending = []
    sgu_pending = []
    pm = ExitStack()
    S[1] = c.sb("S1", [128, DC, T], F32, pm)
    bufs = {"h1": c.sb("h1", [128, FC, T], BF16, pm)}
    qT = c.sb("qT", [128, 6, T], BF16, pm)
    kT01 = [c.sb("kT%d" % g, [128, 2, 2, T], BF16, pm) for g in range(2)]
    kT2 = c.sb("kT2", [128, 2, SEQ], BF16, pm)
    Vc01 = [c.sb("Vc%d" % g, [128, 2, 4, 256], BF16, pm) for g in range(2)]
    Vc2 = c.sb("Vc2", [128, 16, 256], BF16, pm)
    kvst = [c.sb("kvst%d" % i, [128, 512], F32, pm) for i in range(4)]
    tmpE = [c.sb("tmpE%d" % i, [128, 512], F32, pm) for i in range(2)]
    PT = [c.sb("PT%d" % i, [128, 512], BF16, pm) for i in range(2)]

    cnt = {"kv": 0, "vt": 0, "te": 0, "sil": 0, "sg": 0}

    wstate = {"issued": 0}
    total_tiles = nseq * NTL + (1 if do_sample else 0)

    def ws_ensure(G):
        lim = min(G + NSLOT - 3, total_tiles * NCH - 1)
        while wstate["issued"] <= lim:
            g_ = wstate["issued"]
            if g_ < NCH:
                c.dma(ws[g_ % NSLOT][:], wst[:, g_ * CHW:(g_ + 1) * CHW], W=[("ws", g_ % NSLOT)], q="pool")
                c.dma(wbf[g_], ws[g_ % NSLOT][:], R=[("ws", g_ % NSLOT)], W=[("wbf", g_)], q="sp")
            else:
                c.dma(ws[g_ % NSLOT][:], wbf[g_ % NCH], R=[("wbf", g_ % NCH)], W=[("ws", g_ % NSLOT)], q="sp")
            wstate["issued"] += 1

    def W_(ti, key, width=128):
        b = BIDX[key]
        G = ti * NCH + b // CH
        ws_ensure(G)
        off = (b % CH) * 128
        slot = G % NSLOT
        return ws[slot][:, off:off + width], ("ws", slot)

    def cast_stream(si, n, dcs=None):
        for dc in range(DC):
            if dc % 2 == 0:
                c.op("act", lambda e, dc=dc: e.copy(out=Sb[:, dc, :n], in_=S[si][:, dc, :n]), R=[("S", si, dc)], W=[("Sb", dc)])
            else:
                c.op("dve", lambda e, dc=dc: e.tensor_copy(out=Sb[:, dc, :n], in_=S[si][:, dc, :n]), R=[("S", si, dc)], W=[("Sb", dc)])

    def preload(func):
        c.op("act", lambda e: e.activation(out=dmy[:, 0:1], in_=epsc[:, 0:1], func=func), R=["epsc"], W=["dmy"])

    def ffn(ti, si, n, pre, alt=False):
        for fc in range(FC):
            pa, ka = c.bank()
            pb_, kb = c.bank()
            ia, ib = [], []
            for dc in range(DC):
                wa, wk = W_(ti, (pre + "w1", fc, dc))
                ia.append((lambda e, wa=wa, dc=dc, pa=pa: e.matmul(pa[:, :n], lhsT=wa, rhs=Sb[:, dc, :n], start=(dc == 0), stop=(dc == DC - 1)),
                           [wk, ("Sb", dc)]))
            c.mm(ia, W=[ka])
            for dc in range(DC):
                wb, wk = W_(ti, (pre + "w3", fc, dc))
                ib.append((lambda e, wb=wb, dc=dc, pb_=pb_: e.matmul(pb_[:, :n], lhsT=wb, rhs=Sb[:, dc, :n], start=(dc == 0), stop=(dc == DC - 1)),
                           [wk, ("Sb", dc)]))
            c.mm(ib, W=[kb])
            sb_i = cnt["sil"] % 2
            cnt["sil"] += 1
            c.op("act", lambda e, pa=pa, sb_i=sb_i: e.activation(out=sil[sb_i][:, :n], in_=pa[:, :n], func=AF.Silu), R=[ka], W=[("sil", sb_i)])
            c.op("dve", lambda e, pb_=pb_, sb_i=sb_i, fc=fc: e.tensor_tensor(out=bufs['h1'][:, fc, :n], in0=pb_[:, :n], in1=sil[sb_i][:, :n], op=ALU.mult),
                 R=[kb, ("sil", sb_i)], W=[("h1", fc)])
            if ln_pending:
                ln_pending.pop(0)()
        while ln_pending:
            ln_pending.pop(0)()
        preload(AF.Ln)
        for oc in range(DC):
            po, ko = c.bank()
            items = []
            for fc in range(FC):
                w2, wk = W_(ti, (pre + "w2", oc, fc))
                items.append((lambda e, w2=w2, fc=fc, po=po: e.matmul(po[:, :n], lhsT=w2, rhs=bufs['h1'][:, fc, :n], start=(fc == 0), stop=(fc == FC - 1)),
                              [wk, ("h1", fc)]))
            c.mm(items, W=[ko])
            resid(si, n, oc, po, ko, 0.5 / ALPHA, alt)

    def xbuf(dc, n):
        return (uT[:, dc, :n], ("uT", dc)) if dc < 4 else (yb[:, dc - 4, :n], ("yb", dc - 4))

    def resid(si, n, oc, po, ko, coef, alt=False):
        c.op("dve", lambda e: e.scalar_tensor_tensor(out=S[si][:, oc, :n], in0=po[:, :n], scalar=coef, in1=S[si][:, oc, :n],
                                                      op0=ALU.mult, op1=ALU.add), R=[ko, ("S", si, oc)], W=[("S", si, oc)])
        dst, dkey = xbuf(oc, n) if alt else (Sb[:, oc, :n], ("Sb", oc))
        c.op("act", lambda e: e.copy(out=dst, in_=S[si][:, oc, :n]), R=[("S", si, oc)], W=[dkey])
        c.op("act", lambda e: e.activation(out=t8[:, oc, :n], in_=S[si][:, oc, :n], func=AF.Square), R=[("S", si, oc)], W=[("t8", oc)])

    def layer_norm(si, n, which, need_bf16=True, defer=None):
        g0, b0 = PLN[which]
        npool = 0
        st = {}

        def stats():
            pm, km = c.bank()
            pe2, ke = c.bank()
            st["pm"], st["km"] = pm, km
            srcs = [xbuf(dc, n) if defer is not None else (Sb[:, dc, :n], ("Sb", dc)) for dc in range(DC)]
            c.mm([(lambda e, dc=dc: e.matmul(pm[:, :n], lhsT=onesb, rhs=srcs[dc][0], start=(dc == 0), stop=(dc == DC - 1)), ["cb", srcs[dc][1]])
                  for dc in range(DC)], W=[km])
            c.mm([(lambda e, dc=dc: e.matmul(pe2[:, :n], lhsT=onesb, rhs=t8[:, dc, :n], start=(dc == 0), stop=(dc == DC - 1)), ["cb", ("t8", dc)])
                  for dc in range(DC)], W=[ke])
            c.op("act", lambda e: e.activation(out=rstd[:, :n], in_=pm[:, :n], func=AF.Square), R=[km], W=["rstd"])
            if defer is not None or npool:
                c.op("act", lambda e: e.copy(out=mean_s[:, :n], in_=pm[:, :n]), R=[km], W=[("m1", 0)])
            c.op("dve", lambda e: e.tensor_tensor(out=rstd[:, :n], in0=pe2[:, :n], in1=rstd[:, :n], op=ALU.subtract), R=[ke, "rstd"], W=["rstd"])
            c.op("act", lambda e: e.activation(out=rstd[:, :n], in_=rstd[:, :n], func=AF.Ln, bias=epsc[:, 0:1]), R=["rstd", "epsc"], W=["rstd"])
            c.op("act", lambda e: e.activation(out=rstd[:, :n], in_=rstd[:, :n], func=AF.Exp, scale=-0.5), R=["rstd"], W=["rstd"])

        def csub(dc, eng="dve"):
            if defer is not None or eng == "pool":
                c.op(eng, lambda e: e.tensor_tensor(out=S[si][:, dc, :n], in0=S[si][:, dc, :n], in1=mean_s[:, :n], op=ALU.subtract),
                     R=[("S", si, dc), ("m1", 0)], W=[("S", si, dc)])
            else:
                c.op(eng, lambda e: e.tensor_tensor(out=S[si][:, dc, :n], in0=S[si][:, dc, :n], in1=st["pm"][:, :n], op=ALU.subtract),
                     R=[("S", si, dc), st["km"]], W=[("S", si, dc)])

        def cmul(dc, eng="dve"):
            c.op(eng, lambda e: e.tensor_tensor(out=S[si][:, dc, :n], in0=S[si][:, dc, :n], in1=rstd[:, :n], op=ALU.mult),
                 R=[("S", si, dc), "rstd"], W=[("S", si, dc)])

        def chunk(dc, eng="dve"):
            csub(dc, eng)
            cmul(dc, eng)

        def tobf(dc):
            c.op("act", lambda e: e.activation(out=Sb[:, dc, :n], in_=S[si][:, dc, :n], func=AF.Identity,
                                               scale=pp[:, g0 + dc:g0 + dc + 1], bias=pp[:, b0 + dc:b0 + dc + 1]),
                 R=[("S", si, dc), "pp"], W=[("Sb", dc)])

        def affine(dc):
            c.op("act", lambda e: e.activation(out=S[si][:, dc, :n], in_=S[si][:, dc, :n], func=AF.Identity,
                                               scale=pp[:, g0 + dc:g0 + dc + 1], bias=pp[:, b0 + dc:b0 + dc + 1]),
                 R=[("S", si, dc), "pp"], W=[("S", si, dc)])

        if defer is None:
            stats()
            for dc in range(4):
                csub(dc)
            for dc in range(DC):
                cmul(dc)
                if need_bf16:
                    tobf(dc)
                if dc + 4 < DC:
                    csub(dc + 4)
            for dc in range(DC):
                affine(dc)
        else:
            ln_pending.append(stats)
            for dc in range(DC):
                ln_pending.append(lambda dc=dc: (chunk(dc), affine(dc)))
            ln_pending.append(defer)

    def proj_fm(ti, n, key, idx, evac):
        p, k = c.bank()
        items = []
        for dc in range(DC):
            w, wk = W_(ti, (key, idx, dc))
            items.append((lambda e, w=w, dc=dc: e.matmul(p[:, :n], lhsT=w, rhs=Sb[:, dc, :n], start=(dc == 0), stop=(dc == DC - 1)), [wk, ("Sb", dc)]))
        c.mm(items, W=[k])
        evac(p, k)

    def sgu_rows(p, k, m, tb, sample):
        vi = tb
        v_ = vt[vi]
        c.op("dve", lambda e: e.tensor_tensor(out=v_[:m], in0=p[:m, :], in1=bt[:m, BKV + 1536:BKV + 2048], op=ALU.add), R=[k, "bt"], W=[("vt", vi)])
        c.op("act", lambda e: e.activation(out=v_[:m], in_=v_[:m], func=AF.Gelu), R=[("vt", vi)], W=[("vt", vi)])
        steps = [
            lambda: c.op("dve", lambda e: e.bn_stats(out=st6[:m, tb, :], in_=v_[:m]), R=[("vt", vi)], W=[("st6", tb)]),
            lambda: c.op("dve", lambda e: e.bn_aggr(out=mv[:m, tb, 0:2], in_=st6[:m, tb, :]), R=[("st6", tb)], W=[("mv", tb)]),
            lambda: c.op("act", lambda e: e.activation(out=mv[:m, tb, 2:3], in_=mv[:m, tb, 1:2], func=AF.Ln, bias=epsc[:m, 1:2]),
                         R=[("mv", tb), "epsc"], W=[("mv2", tb)]),
            lambda: c.op("act", lambda e: e.activation(out=mv[:m, tb, 3:4], in_=mv[:m, tb, 2:3], func=AF.Exp, scale=-0.5), R=[("mv2", tb)], W=[("mv3", tb)]),
            lambda: c.op("dve", lambda e: e.tensor_scalar(out=v_[:m], in0=v_[:m], scalar1=mv[:m, tb, 0:1], scalar2=mv[:m, tb, 3:4],
                                                          op0=ALU.subtract, op1=ALU.mult), R=[("vt", vi), ("mv", tb), ("mv3", tb)], W=[("vt", vi)]),
            lambda: c.op("dve", lambda e: e.tensor_tensor(out=v_[:m], in0=v_[:m], in1=bt[:m, BSG:BSG + 512], op=ALU.mult), R=[("vt", vi), "bt"], W=[("vt", vi)]),
            lambda: c.op("dve", lambda e: e.tensor_tensor(out=v_[:m], in0=v_[:m], in1=bt[:m, BSB:BSB + 512], op=ALU.add), R=[("vt", vi), "bt"], W=[("vt", vi)]),
            lambda: c.op("act", lambda e: e.copy(out=vn[:m, tb, :], in_=v_[:m]), R=[("vt", vi)], W=[("vn", tb)]),
        ]
        if sample:
            steps.append(lambda: c.dma(sv[:, :], v_[:m], R=[("vt", vi)]))
        return steps

    def sgu_mix(n, sample):
        for gg in range(4):
            p, k = c.bank()
            items = []
            if not sample:
                for tb in range(4):
                    items.append((lambda e, tb=tb: e.matmul(p[:, tb * 128:(tb + 1) * 128], lhsT=vn[:, tb, gg * 128:(gg + 1) * 128], rhs=wm[:, gg, :],
                                                            start=True, stop=True), [("vn", tb), "wm"]))
            else:
                items.append((lambda e: e.matmul(p[:, 0:64], lhsT=vn[0:64, 0, gg * 128:(gg + 1) * 128], rhs=wms[0:64, gg, :], start=True, stop=True),
                              [("vn", 0), "wms"]))
            c.mm(items, W=[k])
            vi = cnt["vt"] % 4
            cnt["vt"] += 1
            v_ = vt[vi]
            if not sample:
                bia = bt[:, BSGB + gg * 128:BSGB + (gg + 1) * 128].unsqueeze(1).to_broadcast([128, 4, 128])
                c.op("dve", lambda e: e.tensor_tensor(out=v_[:, :].rearrange("p (a b) -> p a b", a=4), in0=p[:, :].rearrange("p (a b) -> p a b", a=4),
                                                      in1=bia, op=ALU.add), R=[k, "bt"], W=[("vt", vi)])
            else:
                c.op("dve", lambda e: e.tensor_tensor(out=v_[:, :n], in0=p[:, :n], in1=bufs["bts"][:, BSGBS + gg * 64:BSGBS + (gg + 1) * 64], op=ALU.add),
                     R=[k, "bts"], W=[("vt", vi)])
            c.op("dve", lambda e: e.tensor_tensor(out=yb[:, gg, :n], in0=v_[:, :n], in1=uT[:, gg, :n], op=ALU.mult),
                 R=[("vt", vi), ("uT", gg)], W=[("yb", gg)])

    def mix_stage(ti, si, n):
        for oc in range(DC):
            def ev_ga(p, k, oc=oc):
                i = cnt["sg"] % 2
                cnt["sg"] += 1
                c.op("act", lambda e: e.activation(out=sg[i][:, :n], in_=p[:, :n], func=AF.Sigmoid, bias=pp[:, PBGA + oc:PBGA + oc + 1]),
                     R=[k, "pp"], W=[("sg", i)])
                ev_ga.i = i
            proj_fm(ti, n, "ga", oc, ev_ga)
            pA, kA = c.bank()
            items = []
            for kc in range(2):
                w, wk = W_(ti, ("oa", oc, kc))
                items.append((lambda e, w=w, kc=kc: e.matmul(pA[:, :n], lhsT=w, rhs=attb[:, kc, :n], start=(kc == 0), stop=(kc == 1)), [wk, ("attb", kc)]))
            c.mm(items, W=[kA])
            ia = ev_ga.i
            mi = 0
            c.op("dve", lambda e: e.tensor_tensor(out=m1[mi][:, :n], in0=pA[:, :n], in1=sg[ia][:, :n], op=ALU.mult), R=[kA, ("sg", ia)], W=[("m1", mi)])

            def ev_gb(p, k, oc=oc):
                i = cnt["sg"] % 2
                cnt["sg"] += 1
                c.op("act", lambda e: e.activation(out=sg[i][:, :n], in_=p[:, :n], func=AF.Sigmoid, bias=pp[:, PBGB + oc:PBGB + oc + 1]),
                     R=[k, "pp"], W=[("sg", i)])
                ev_gb.i = i
            proj_fm(ti, n, "gb", oc, ev_gb)
            pB, kB = c.bank()
            items = []
            for kc in range(4):
                w, wk = W_(ti, ("ob", oc, kc))
                items.append((lambda e, w=w, kc=kc: e.matmul(pB[:, :n], lhsT=w, rhs=yb[:, kc, :n], start=(kc == 0), stop=(kc == 3)), [wk, ("yb", kc)]))
            c.mm(items, W=[kB])
            ib = ev_gb.i
            c.op("dve", lambda e: e.tensor_tensor(out=sg[ib][:, :n], in0=pB[:, :n], in1=sg[ib][:, :n], op=ALU.mult), R=[kB, ("sg", ib)], W=[("sg", ib)])
            c.op("dve", lambda e: e.tensor_tensor(out=t8[:, oc, :n], in0=m1[mi][:, :n], in1=sg[ib][:, :n], op=ALU.add),
                 R=[("m1", mi), ("sg", ib)], W=[("t8", oc)])
        preload(AF.Ln)
        for oc in range(DC):
            po, ko = c.bank()
            items = []
            for kc in range(DC):
                w, wk = W_(ti, ("wo", oc, kc))
                items.append((lambda e, w=w, kc=kc: e.matmul(po[:, :n], lhsT=w, rhs=t8[:, kc, :n], start=(kc == 0), stop=(kc == DC - 1)), [wk, ("t8", kc)]))
            c.mm(items, W=[ko])
            mix_stage.pend.append((oc, po, ko))
        for oc, po, ko in mix_stage.pend:
            resid(si, n, oc, po, ko, 1.0 / ALPHA)
        mix_stage.pend = []

    mix_stage.pend = []

    def prompt_attention(j):
        pr = j % 2
        c.bank_rot = [0, 1, 2, 3]
        NUM = [(c.banks[4], ("pb", 4)), (c.banks[5], ("pb", 5))]
        LB = [(c.banks[6], ("pb", 6)), (c.banks[7], ("pb", 7))]
        started = set()

        def exp_mul(p, k, eb_ap, m, lo, hi, view):
            i = cnt["te"] % 2
            cnt["te"] += 1
            c.op("act", lambda e: e.activation(out=tmpE[i][:m, lo:hi], in_=p[:m, lo:hi], func=AF.Exp, scale=SCALE), R=[k], W=[("tmpE", i)])
            c.op("dve", lambda e: e.tensor_tensor(out=view(PT[i][:m, lo:hi]), in0=view(tmpE[i][:m, lo:hi]), in1=eb_ap, op=ALU.mult),
                 R=[("tmpE", i), "EB"], W=[("PT", i)])
            return PT[i], ("PT", i)

        def numl(items_spec, ptk, h):
            hc, hr = h // 2, (h % 2) * 64
            items = []
            for vap, vkey, rhs, osel, m in items_spec:
                for kind in (0, 1):
                    tgt, tk = (NUM if kind == 0 else LB)[hc]
                    st = (kind, h) not in started
                    started.add((kind, h))
                    lhs = vap if kind == 0 else ones64[:m, :]
                    items.append((lambda e, lhs=lhs, rhs=rhs, tgt=tgt, st=st, osel=osel: e.matmul(osel(tgt[hr:hr + 64, :]), lhsT=lhs, rhs=rhs, start=st,
                                                                                                  stop=False, skip_group_check=True),
                                  [vkey, ptk, "cb"]))
            c.mm(items, W=[NUM[hc][1], LB[hc][1]])

        jobs = []
        for g in range(2):
            for h in range(4):
                for which in (0, 1):
                    if which == 1 and j == 0 and g == 1:
                        continue

                    def st_fn(g=g, h=h, which=which):
                        kT, Vc = kT01[g], Vc01[g]
                        hc, hr = h // 2, (h % 2) * 64
                        qc = g * 2 + hc
                        p, k = c.bank()
                        items = []
                        specs = []
                        lo = 128 if (which == 1 and j == 0) else 0
                        for sbk in range(4):
                            if g == 0:
                                qcols = slice(sbk * 128, (sbk + 1) * 128)
                                if which == 0:
                                    kap = kT[hr:hr + 64, hc, pr, sbk * 128:(sbk + 1) * 128]
                                    kkey = ("kT", 0, hc, pr)
                                    vap, vkey = Vc[:, pr, sbk, h * 64:(h + 1) * 64], ("Vc", 0, pr, sbk)
                                elif sbk > 0:
                                    kap = kT[hr:hr + 64, hc, pr, (sbk - 1) * 128:sbk * 128]
                                    kkey = ("kT", 0, hc, pr)
                                    vap, vkey = Vc[:, pr, sbk - 1, h * 64:(h + 1) * 64], ("Vc", 0, pr, sbk - 1)
                                else:
                                    if j == 0:
                                        continue
                                    kap = kT[hr:hr + 64, hc, 1 - pr, 384:512]
                                    kkey = ("kT", 0, hc, 1 - pr)
                                    vap, vkey = Vc[:, 1 - pr, 3, h * 64:(h + 1) * 64], ("Vc", 0, 1 - pr, 3)
                            else:
                                qcols = slice(sbk, T, 4)
                                pp_ = pr if which == 0 else 1 - pr
                                kap = kT[hr:hr + 64, hc, pp_, sbk:T:4]
                                kkey = ("kT", 1, hc, pp_)
                                vap, vkey = Vc[:, pp_, sbk, h * 64:(h + 1) * 64], ("Vc", 1, pp_, sbk)
                            osel = (lambda t_, qcols=qcols: t_[:, qcols])
                            items.append((lambda e, kap=kap, qcols=qcols, sbk=sbk: e.matmul(p[:, sbk * 128:(sbk + 1) * 128], lhsT=kap,
                                                                                            rhs=qT[hr:hr + 64, qc, qcols], start=True, stop=True),
                                          [kkey, ("qT", qc)]))
                            specs.append((vap, vkey, sbk, osel))
                        c.mm(items, W=[k])
                        return p, k, specs, lo

                    def post_fn(state, g=g, h=h, which=which):
                        p, k, specs, lo = state
                        eb = EB[:, g * 4 + h, :]
                        nb_ = (512 - lo) // 128
                        ebs = eb[:, which * 128:(which + 1) * 128].unsqueeze(1).to_broadcast([128, nb_, 128])
                        ptb, ptk = exp_mul(p, k, ebs, 128, lo, 512, lambda a, nb_=nb_: a.rearrange("p (a b) -> p a b", a=nb_))
                        numl([(vap, vkey, ptb[:, sbk * 128:(sbk + 1) * 128], osel, 128) for vap, vkey, sbk, osel in specs], ptk, h)

                    jobs.append((st_fn, post_fn))
        M = 32 * (j + 1)
        for h in range(4):
            def st_fn(h=h):
                hc, hr = h // 2, (h % 2) * 64
                qc = 4 + hc
                p, k = c.bank()
                items = []
                for r in range(16):
                    items.append((lambda e, r=r: e.matmul(p[:M, r * 32:(r + 1) * 32], lhsT=kT2[hr:hr + 64, hc, r:512 * (j + 1):16],
                                                          rhs=qT[hr:hr + 64, qc, r:T:16], start=True, stop=True),
                                  [("kT", 2, hc, jj) for jj in range(j + 1)] + [("qT", qc)]))
                c.mm(items, W=[k])
                return p, k

            def post_fn(state, h=h):
                p, k = state
                ebs = EB[:M, 8 + h, 32 * j:32 * (j + 1)].unsqueeze(1).to_broadcast([M, 16, 32])
                ptb, ptk = exp_mul(p, k, ebs, M, 0, 512, lambda a: a.rearrange("p (a b) -> p a b", a=16))
                numl([(Vc2[:M, r, h * 64:(h + 1) * 64], ("Vc", 2, r), ptb[:M, r * 32:(r + 1) * 32], (lambda t_, r=r: t_[:, r:T:16]), M)
                      for r in range(16)], ptk, h)

            jobs.append((st_fn, post_fn))
        states = [None] * len(jobs)
        states[0] = jobs[0][0]()
        for i in range(len(jobs)):
            if i + 1 < len(jobs):
                states[i + 1] = jobs[i + 1][0]()
            jobs[i][1](states[i])
            if sgu_pending:
                sgu_pending.pop(0)()
        for hc in range(2):
            c.op("dve", lambda e, hc=hc: e.reciprocal(out=rL[:, :], in_=LB[hc][0][:, :]), R=[LB[hc][1]], W=["rL"])
            c.op("dve", lambda e, hc=hc: e.tensor_tensor(out=attb[:, hc, :], in0=NUM[hc][0][:, :], in1=rL[:, :], op=ALU.mult),
                 R=[NUM[hc][1], "rL"], W=[("attb", hc)])
        c.bank_rot = list(range(8))
        c.bank_i = 0

    def prompt_tile(ti, s, j):
        si = ti % 2
        n = T
        pr = j % 2
        cast_stream(si, n)
        ffn(ti, si, n, "f1")
        if ti + 1 < len(tiles):
            load_x(ti + 1)
        if debug and ti == 0:
            c.dma(dbg["r1"][:, :, :], S[si][:, :, :], R=[("S", si, dc) for dc in range(DC)])
        layer_norm(si, n, 1)
        if debug and ti == 0:
            c.dma(dbg["h"][:, :, :], S[si][:, :, :], R=[("S", si, dc) for dc in range(DC)])
        for cc in range(16):
            if cc < 6:
                def ev(p, k, cc=cc):
                    c.op("act", lambda e: e.activation(out=qT[:, cc, :], in_=p[:, :], func=AF.Identity, bias=pp[:, PBQ + cc:PBQ + cc + 1]),
                         R=[k, "pp"], W=[("qT", cc)])
            elif cc < 12:
                def ev(p, k, cc=cc):
                    g, hc = (cc - 6) // 2, (cc - 6) % 2
                    if g < 2:
                        dst, key = kT01[g][:, hc, pr, :], ("kT", g, hc, pr)
                    else:
                        dst, key = kT2[:, hc, j * T:(j + 1) * T], ("kT", 2, hc, j)
                    c.op("act", lambda e: e.activation(out=dst, in_=p[:, :], func=AF.Identity, bias=pp[:, PBK + cc - 6:PBK + cc - 5]),
                         R=[k, "pp"], W=[key])
            else:
                def ev(p, k, cc=cc):
                    c.op("act", lambda e: e.activation(out=uT[:, cc - 12, :], in_=p[:, :], func=AF.Gelu, bias=pp[:, PBU + cc - 12:PBU + cc - 11]),
                         R=[k, "pp"], W=[("uT", cc - 12)])
            proj_fm(ti, n, "wq", cc, ev)
        for cg in (3, 0, 1, 2):
            tb_steps = []
            for mb in range(4):
                if cg == 1:
                    lcols = lambda dc, mb=mb: Sb[:, dc, mb:T:4]
                else:
                    lcols = lambda dc, mb=mb: Sb[:, dc, mb * 128:(mb + 1) * 128]
                p, k = c.bank()
                items = []
                need_k = cg >= 2 or (cg == 1 and j == NTL - 1) or (cg == 0 and j == NTL - 1 and mb == 3)
                c0 = 0 if need_k else 256
                for dc in range(DC):
                    w, wk = W_(ti, ("wtm", cg, dc, 0), 512)
                    items.append((lambda e, w=w, dc=dc, lcols=lcols, c0=c0: e.matmul(p[:, c0:512], lhsT=lcols(dc), rhs=w[:, c0:512],
                                                                                     start=(dc == 0), stop=(dc == DC - 1)),
                                  [wk, ("Sb", dc)]))
                c.mm(items, W=[k])
                if cg == 3:
                    tb_steps.append(sgu_rows(p, k, 128, mb, False))
                    continue
                g = cg
                ki = cnt["kv"] % 4
                cnt["kv"] += 1
                kv = kvst[ki]
                c.op("dve", lambda e: e.tensor_tensor(out=kv[:, c0:512], in0=p[:, c0:512], in1=bt[:, BKV + g * 512 + c0:BKV + (g + 1) * 512], op=ALU.add),
                     R=[k, "bt"], W=[("kvst", ki)])
                if g < 2:
                    c.op("act", lambda e: e.copy(out=Vc01[g][:, pr, mb, :], in_=kv[:, 256:512]), R=[("kvst", ki)], W=[("Vc", g, pr, mb)])
                if g == 0 and j == NTL - 1 and mb == 3:
                    c.dma(pw[0][s, :, :], kv[:], R=[("kvst", ki)])
                if g == 1 and j == NTL - 1:
                    c.dma(pw[1][s, mb:512:4, :], kv[:], R=[("kvst", ki)])
                if g == 2:
                    c.dma(pw[2][s, T * j + mb * 128:T * j + (mb + 1) * 128, :], kv[:], R=[("kvst", ki)], W=[("pw2", mb)])
            if cg == 2:
                src = bass.AP(tensor=pw[2].tensor, offset=(s * SEQ + T * j) * 512 + 256, ap=[[16 * 512, 32], [512, 16], [1, 256]])
                c.dma(Vc2[32 * j:32 * (j + 1), :, :], src, R=[("pw2", mb) for mb in range(4)], W=[("Vc", 2, r) for r in range(16)])
            if cg == 3:
                preload(AF.Exp)
                for st_i in range(len(tb_steps[0])):
                    sgu_pending.append(lambda st_i=st_i, tbs=tb_steps: [tbs[tb][st_i]() for tb in range(4)])
        prompt_attention(j)
        while sgu_pending:
            sgu_pending.pop(0)()
        sgu_mix(n, False)
        mix_stage(ti, si, n)
        if debug and ti == 0:
            c.dma(dbg["r2"][:, :, :], S[si][:, :, :], R=[("S", si, dc) for dc in range(DC)])
        layer_norm(si, n, 2)
        if debug and ti == 0:
            c.dma(dbg["h2"][:, :, :], S[si][:, :, :], R=[("S", si, dc) for dc in range(DC)])
        ffn(ti, si, n, "f2", alt=True)
        layer_norm(si, n, 3, need_bf16=False,
                   defer=lambda: c.dma(yp[s, :, :, j * T:(j + 1) * T], S[si][:, :, :], R=[("S", si, dc) for dc in range(DC)]))

    tiles = [(s, j) for s in range(nseq) for j in range(NTL)]

    def load_x(ti):
        s, j = tiles[ti]
        c.dma(S[ti % 2][:, :, :], xp[s, :, :, j * T:(j + 1) * T], W=[("S", ti % 2, dc) for dc in range(DC)])

    load_x(0)
    for ti, (s, j) in enumerate(tiles):
        prompt_tile(ti, s, j)
    while ln_pending:
        ln_pending.pop(0)()

    if do_sample:
        c.barrier()
        pm.close()
        bufs["h1"] = c.sb("h1s", [128, FC, TS], BF16)
        qbs = [c.sb("qb%d" % i, [128, 4, 768], BF16) for i in range(2)]
        KVa = [c.sb("KVa0", [128, 1, 512], BF16), c.sb("KVa1", [128, 4, 512], BF16), c.sb("KVa2", [128, 4, 512], BF16)]
        prod = c.sb("prod", [128, 4, 256], BF16)
        PVt = c.sb("PVt", [128, 4, 260], BF16)
        prodB = c.sb("prodB", [128, 4, 256], F32)
        PVB = c.sb("PVB", [128, 4, 260], F32)
        sA = c.sb("sA", [128, 16], F32)
        biasA = c.sb("biasA", [128, 48], F32)
        biasB = c.sb("biasB", [128, 48], F32)
        KVn = c.sb("KVn", [128, 3, 512], F32)
        qB4 = c.sb("qB4", [128, 4, 768], F32)
        qtok = c.sb("qtok", [128, 768], F32)
        att = c.sb("att", [128, 256], F32)
        racc = c.sb("racc", [128, 4], F32)
        ohA = c.sb("ohA", [32, 12 * 128], F32)
        ohB = c.sb("ohB", [32, 12 * 64], F32)
        mkA = c.sb("mkA", [128, 48], F32)
        mkB = c.sb("mkB", [64, 48], F32)
        bts = c.sb("bts", [128, 1024], F32)
        bufs["bts"] = bts
        c.dma(bts[:], bts_d[:, :], W=["bts"])
        c.dma(ohA[:], ohA_d[:, :], W=["ohA"])
        c.dma(ohB[:], ohB_d[:, :], W=["ohB"])
        c.dma(mkA[:], mkA_d[:, :], W=["mkA"])
        c.dma(mkB[:], mkB_d[:, :], W=["mkB"])
        pA, kA = c.bank()
        c.mm([(lambda e, i=i: e.matmul(pA[:, 4 * i:4 * i + 4], lhsT=ohA[:, 128 * i:128 * (i + 1)], rhs=relb[:, 4 * (i // 4):4 * (i // 4) + 4],
                                       start=True, stop=True), ["ohA", "relb"]) for i in range(12)], W=[kA])
        c.op("dve", lambda e: e.tensor_tensor(out=biasA[:, :], in0=pA[:, 0:48], in1=mkA[:, :], op=ALU.add), R=[kA, "mkA"], W=["biasA"])
        pB, kB = c.bank()
        c.mm([(lambda e, i=i: e.matmul(pB[0:64, 4 * i:4 * i + 4], lhsT=ohB[:, 64 * i:64 * (i + 1)], rhs=relb[:, 4 * (i // 4):4 * (i // 4) + 4],
                                       start=True, stop=True), ["ohB", "relb"]) for i in range(12)], W=[kB])
        c.op("dve", lambda e: e.tensor_tensor(out=biasB[0:64, :], in0=pB[0:64, 0:48], in1=mkB[:, :], op=ALU.add), R=[kB, "mkB"], W=["biasB"])

        ti = len(tiles)
        si, n = 0, TS
        c.dma(S[0][:, :, :n], xs[:, :, :], W=[("S", 0, dc) for dc in range(DC)])
        cast_stream(si, n)
        ffn(ti, si, n, "f1")
        layer_norm(si, n, 1)
        pq = [c.bank(), c.bank()]
        for cc in range(6):
            p, k = pq[cc // 4]
            items = []
            for dc in range(DC):
                w, wk = W_(ti, ("wq", cc, dc))
                items.append((lambda e, w=w, dc=dc, cc=cc, p=p: e.matmul(p[0:n, (cc % 4) * 128:(cc % 4 + 1) * 128], lhsT=Sb[:, dc, 0:n], rhs=w,
                                                                       start=(dc == 0), stop=(dc == DC - 1)), [wk, ("Sb", dc)]))
            c.mm(items, W=[k])
        c.op("dve", lambda e: e.tensor_tensor(out=qtok[0:n, 0:512], in0=pq[0][0][0:n, :], in1=bts[0:n, BQ:BQ + 512], op=ALU.add), R=[pq[0][1], "bts"], W=["qtok"])
        c.op("dve", lambda e: e.tensor_tensor(out=qtok[0:n, 512:768], in0=pq[1][0][0:n, 0:256], in1=bts[0:n, BQ + 512:BQ + 768], op=ALU.add),
             R=[pq[1][1], "bts", "qtok"], W=["qtok"])
        c.dma(qsc_t.ap()[:, :], qtok[0:n, :], R=["qtok"], W=["qsc"])
        for cc in range(12, 16):
            def ev(p, k, cc=cc):
                c.op("act", lambda e: e.activation(out=uT[:, cc - 12, :n], in_=p[:, :n], func=AF.Gelu, bias=pp[:, PBU + cc - 12:PBU + cc - 11]),
                     R=[k, "pp"], W=[("uT", cc - 12)])
            proj_fm(ti, n, "wq", cc, ev)
        for cg in (3, 0, 1, 2):
            p, k = c.bank()
            items = []
            for dc in range(DC):
                w, wk = W_(ti, ("wtm", cg, dc, 0), 512)
                items.append((lambda e, w=w, dc=dc, p=p: e.matmul(p[0:n, :], lhsT=Sb[:, dc, 0:n], rhs=w, start=(dc == 0), stop=(dc == DC - 1)),
                              [wk, ("Sb", dc)]))
            c.mm(items, W=[k])
            if cg == 3:
                for st_ in sgu_rows(p, k, n, 0, True):
                    st_()
                continue
            g = cg
            c.op("dve", lambda e: e.tensor_tensor(out=KVn[0:n, g, :], in0=p[0:n, :], in1=bt[0:n, BKV + g * 512:BKV + (g + 1) * 512], op=ALU.add),
                 R=[k, "bt"], W=[("KVn", g)])
            for b in range(SB):
                c.dma(sw[g][b, WIN[g] - 4:WIN[g], :], KVn[4 * b:4 * b + 4, g, :], R=[("KVn", g)])
        acc, kacc = c.bank()
        first = [True]

        def att_block(m, Kap, Vap, qap, bias_ap, pv, sa, sel_fn, keysR, prod=prod):
            c.op("dve", lambda e: e.tensor_tensor(out=prod[:m], in0=Kap, in1=qap, op=ALU.mult), R=keysR, W=["prod"])
            c.op("dve", lambda e: e.tensor_reduce(out=sa[:m, :], in_=prod[:m].rearrange("p t (h e) -> p (t h) e", e=64), axis=mybir.AxisListType.X,
                                                  op=ALU.add), R=["prod"], W=["sA"])
            c.op("dve", lambda e: e.scalar_tensor_tensor(out=sa[:m, :], in0=sa[:m, :], scalar=SCALE, in1=bias_ap, op0=ALU.mult, op1=ALU.add),
                 R=["sA", "biasA", "biasB"], W=["sA"])
            c.op("act", lambda e: e.activation(out=pv[:m, :, 256:260], in_=sa[:m, :].rearrange("p (t h) -> p t h", t=4), func=AF.Exp),
                 R=["sA"], W=["PVp"])
            c.op("dve", lambda e: e.tensor_tensor(out=pv[:m, :, 0:256].rearrange("p t (h e) -> p t h e", e=64), in0=Vap,
                                                  in1=pv[:m, :, 256:260].unsqueeze(3).to_broadcast([m, 4, 4, 64]), op=ALU.mult),
                 R=keysR + ["PVp"], W=["PVv"])
            items = []
            for t in range(4):
                st = first[0]
                first[0] = False
                items.append((lambda e, t=t, st=st: e.matmul(acc[0:64, 0:260], lhsT=sel_fn(t), rhs=pv[:m, t, :], start=st, stop=False,
                                                             skip_group_check=True), ["PVp", "PVv", "cf"]))
            c.mm(items, W=[kacc])

        for b in range(SB):
            src = bass.AP(tensor=qsc_t, offset=b * 3072, ap=[[0, 128], [1, 3072]])
            qb = qbs[b % 2]
            c.dma(qb[:, :, :], src, R=["qsc"], W=[("qb", b % 2)])
            for g in range(3):
                nT = 1 if g == 0 else 4
                if g == 0:
                    c.dma(KVa[0][:, 0, :], st_d[0][b, :, :], W=[("KVa", 0)])
                else:
                    src = bass.AP(tensor=st_d[g].tensor, offset=b * WIN[g] * 512, ap=[[DIL[g] * 512, 128], [512, 4], [1, 512]])
                    c.dma(KVa[g][:, :, :], src, W=[("KVa", g)])
                if g == 0:
                    Kap = KVa[0][:, 0:1, 0:256].to_broadcast([128, 4, 256])
                    Vap = KVa[0][:, 0:1, 256:512].to_broadcast([128, 4, 256]).rearrange("p t (h e) -> p t h e", e=64)
                else:
                    Kap = KVa[g][:, :, 0:256]
                    Vap = KVa[g][:, :, 256:512].rearrange("p t (h e) -> p t h e", e=64)
                att_block(128, Kap, Vap, qb[:, :, g * 256:(g + 1) * 256], biasA[:, g * 16:(g + 1) * 16], PVt, sA,
                          lambda t, b=b: zselb[:, 63 - (4 * b + t):127 - (4 * b + t)], [("KVa", g), ("qb", b % 2)])
        for b in range(SB):
            src = bass.AP(tensor=qsc_t, offset=b * 3072, ap=[[0, 4], [1, 3072]])
            c.dma(qB4[4 * b:4 * b + 4, :, :], src, R=["qsc"], W=[("qB4", b)])
        for g in range(3):
            Kap = KVn[0:64, g:g + 1, 0:256].to_broadcast([64, 4, 256])
            Vap = KVn[0:64, g:g + 1, 256:512].to_broadcast([64, 4, 256]).rearrange("p t (h e) -> p t h e", e=64)
            att_block(64, Kap, Vap, qB4[0:64, :, g * 256:(g + 1) * 256], biasB[0:64, g * 16:(g + 1) * 16], PVB, sA,
                      lambda t: gsel[0:64, t * 64:(t + 1) * 64], [("KVn", g)] + [("qB4", b) for b in range(SB)], prod=prodB)
        c.op("dve", lambda e: e.reciprocal(out=racc[0:64, :], in_=acc[0:64, 256:260]), R=[kacc], W=["racc"])
        c.op("dve", lambda e: e.tensor_tensor(out=att[0:64, :].rearrange("p (h e) -> p h e", e=64), in0=acc[0:64, 0:256].rearrange("p (h e) -> p h e", e=64),
                                              in1=racc[0:64, :].unsqueeze(2).to_broadcast([64, 4, 64]), op=ALU.mult), R=[kacc, "racc"], W=["att"])
        for hc in range(2):
            ptr, ktr = c.bank()
            c.mm([(lambda e, hc=hc, ptr=ptr: e.transpose(ptr[:, 0:64], att[0:64, hc * 128:(hc + 1) * 128], identf[0:64, 0:64]), ["att", "cf"])], W=[ktr])
            c.op("act", lambda e, hc=hc, ptr=ptr: e.copy(out=attb[:, hc, 0:64], in_=ptr[:, 0:64]), R=[ktr], W=[("attb", hc)])
        sgu_mix(n, True)
        mix_stage(ti, si, n)
        layer_norm(si, n, 2)
        ffn(ti, si, n, "f2")
        layer_norm(si, n, 3, need_bf16=False)
        c.dma(ys[:, :, :], S[si][:, :, :n], R=[("S", si, dc) for dc in range(DC)])

    c.finish()
    c.es.close()
    return nc, c


def _host_consts():
    cf = np.zeros((128, 128 + 128 + 64 + 127 + 256), np.float32)
    cf[:, 0:128] = np.eye(128, dtype=np.float32)
    cf[:, 128:256] = np.triu(np.ones((128, 128), np.float32))
    m = np.zeros((64, 64), np.float32)
    for b in range(16):
        m[4 * b:4 * b + 4, 4 * b:4 * b + 4] = np.triu(np.ones((4, 4), np.float32))
    cf[0:64, 256:320] = m
    cf[:, 320 + 63] = 1.0
    gs = np.zeros((64, 4, 64), np.float32)
    for b in range(16):
        for t in range(4):
            gs[4 * b:4 * b + 4, t, 4 * b + t] = 1.0
    cf[0:64, 447:703] = gs.reshape(64, 256)
    cb = np.zeros((128, 447), np.float32)
    cb[:, 320 + 63] = 1.0
    cb[:, 0:128] = 1.0 / 1024.0
    cb[:, 128:192] = 1.0
    cb[:, 192:320] = np.eye(128, dtype=np.float32)[::-1]
    rel = np.arange(383) - 127
    oh = np.zeros((32, 3, 383), np.float32)
    mka = np.zeros((4, 3, 383), np.float32)
    for g in range(3):
        bk = _t5_bucket(np.clip(rel, 0, 128) * DIL[g])
        oh[bk, g, np.arange(383)] = 1.0
        mka[:, g, :] = np.where((rel >= 0) & (rel <= 128), 0.0, NEGB)[None]
    return cf, cb.astype(ml_dtypes.bfloat16), oh.reshape(32, 3 * 383), mka.reshape(4, 3 * 383)


def _fm(vec, n):
    return np.ascontiguousarray(np.asarray(vec, np.float32).reshape(n, 128).T)


def kernel(x_prompt, x_sample, state_win0, state_win1, state_win2, rel_bias,
           ln1_g, ln1_b, f1_w1, f1_w3, f1_w2, w_in, b_in,
           sgu_ln_g, sgu_ln_b, sgu_ws, sgu_b, w_oa, w_ob, w_out,
           ln2_g, ln2_b, f2_w1, f2_w3, f2_w2, ln3_g, ln3_b, _nseq=NSEQ, _ncores=NCORES, _do_sample=True, _debug=False):
    f32 = np.float32
    Wd = {"f1_w1": f1_w1[0], "f1_w3": f1_w3[0], "f1_w2": f1_w2[0], "w_in": w_in[0], "w_oa": w_oa[0], "w_ob": w_ob[0],
          "w_out": w_out[0], "f2_w1": f2_w1[0], "f2_w3": f2_w3[0], "f2_w2": f2_w2[0]}
    wst = np.zeros((128, NCH * CHW), f32)
    for i, blk in enumerate(BLOCKS):
        if blk is not None:
            name, r0, c0 = blk
            wst[:, i * 128:(i + 1) * 128] = Wd[name][r0:r0 + 128, c0:c0 + 128]
    bi = np.asarray(b_in[0], f32)
    pp = np.zeros((128, 80), f32)
    for i, v in enumerate((ln1_g, ln1_b, ln2_g, ln2_b, ln3_g, ln3_b)):
        pp[:, 8 * i:8 * i + 8] = _fm(v[0], 8)
    pp[:, 48:54] = _fm(bi[OQ:OQ + 768], 6)
    pp[:, 54:60] = _fm(bi[OK_:OK_ + 768], 6)
    pp[:, 60:64] = _fm(bi[OU:OU + 512], 4)
    pp[:, 64:72] = _fm(bi[OGA:OGA + 1024], 8)
    pp[:, 72:80] = _fm(bi[OGB:OGB + 1024], 8)
    rows = []
    for g in range(3):
        rows += [bi[OK_ + g * 256:OK_ + (g + 1) * 256], bi[OV + g * 256:OV + (g + 1) * 256]]
    rows += [bi[OVV:OVV + 512], sgu_ln_g[0], sgu_ln_b[0], np.asarray(sgu_b[0], f32).reshape(512)]
    btrow = np.concatenate([np.asarray(r, f32).reshape(-1) for r in rows])
    bt = np.ascontiguousarray(np.broadcast_to(btrow[None, :], (128, btrow.size)))
    btsrow = np.concatenate([bi[OQ:OQ + 768], np.tile(np.asarray(sgu_b[0], f32)[:, None, 0:4], (1, 16, 1)).reshape(256)])
    bts = np.ascontiguousarray(np.broadcast_to(btsrow[None, :], (128, 1024)))
    wsT = np.ascontiguousarray(np.transpose(np.asarray(sgu_ws[0], f32), (2, 0, 1)))
    wsTs = np.zeros((128, 4, 64), f32)
    wsTs[0:64] = np.tile(np.transpose(np.asarray(sgu_ws[0], f32)[:, 0:4, 0:4], (2, 0, 1)), (16, 1, 16))
    cf, cb, oh, mka = _host_consts()
    ohA = np.zeros((32, 12, 128), f32)
    mkA = np.zeros((128, 3, 4, 4), f32)
    ohB = np.zeros((32, 12, 64), f32)
    mkB = np.zeros((64, 3, 4, 4), f32)
    pidx = np.arange(128)
    for g in range(3):
        for t in range(4):
            if g == 0:
                dist = 128 + t - pidx
                valid = dist <= 128
            else:
                dist = (128 - pidx) * DIL[g]
                valid = np.ones(128, bool)
            ohA[_t5_bucket(np.clip(dist, 0, 128 * DIL[g])), g * 4 + t, pidx] = 1.0
            mkA[:, g, t, :] = np.where(valid, 0.0, NEGB)[:, None]
            tp = np.arange(64) % 4
            dB = t - tp
            vB = (dB >= 0) if g == 0 else (dB == 0)
            ohB[_t5_bucket(np.clip(dB, 0, 3) * DIL[g]), g * 4 + t, np.arange(64)] = 1.0
            mkB[:, g, t, :] = np.where(vB, 0.0, NEGB)[:, None]
    common = {"wst": wst, "pp": pp, "bt": bt, "bts": bts, "wsT": wsT, "wsTs": wsTs, "relb": np.asarray(rel_bias, f32), "cf": cf, "cb": cb,
              "oh": oh, "mka": mka,
              "ohA": ohA.reshape(32, 12 * 128), "mkA": mkA.reshape(128, 48), "ohB": ohB.reshape(32, 12 * 64), "mkB": mkB.reshape(64, 48)}
    xpT = np.asarray(x_prompt, f32).reshape(-1, SEQ, DC, 128)
    xsT = np.asarray(x_sample, f32).reshape(-1, 4, DC, 128)
    sts = [np.asarray(a, f32)[0].reshape(a.shape[1], a.shape[2], 512) for a in (state_win0, state_win1, state_win2)]
    in_maps = []
    for cid in range(_ncores):
        m = dict(common)
        m["xp"] = np.ascontiguousarray(np.transpose(xpT[cid * _nseq:(cid + 1) * _nseq], (0, 3, 2, 1)))
        m["xs"] = np.ascontiguousarray(np.transpose(xsT[cid * SB:(cid + 1) * SB].reshape(TS, DC, 128), (2, 1, 0)))
        for g in range(3):
            m["st%d" % g] = np.ascontiguousarray(sts[g][cid * SB:(cid + 1) * SB])
        in_maps.append(m)
    nc, _ = build(_nseq, _do_sample, _debug)
    res = run_bass_kernel_spmd(nc, in_maps, core_ids=list(range(_ncores)))
    R = res.results
    yp = np.concatenate([np.transpose(r["yp"], (0, 3, 2, 1)).reshape(_nseq, SEQ, D) for r in R], 0)
    ys = np.concatenate([np.transpose(r["ys"], (2, 1, 0)).reshape(SB, 4, D) for r in R], 0)
    outs = [yp, ys]
    for g in range(3):
        outs.append(np.concatenate([r["pw%d" % g].reshape(_nseq, WIN[g], 2, 4, 64) for r in R], 0)[None])
    for g in range(3):
        outs.append(np.concatenate([r["sw%d" % g].reshape(SB, WIN[g], 2, 4, 64) for r in R], 0)[None])
    outs.append(np.concatenate([r["sv"].reshape(SB, 4, 512) for r in R], 0)[None])
    return tuple(np.ascontiguousarray(o, dtype=np.float32) for o in outs)
```

```python
import math
from contextlib import ExitStack

import numpy as np
import ml_dtypes
import concourse.bass as bass
import concourse.mybir as mybir
from concourse.bass_utils import run_bass_kernel_spmd

F32 = mybir.dt.float32
BF16 = mybir.dt.bfloat16
AF = mybir.ActivationFunctionType
ALU = mybir.AluOpType

D = 1024
DC = 8
DFF = 2816
FC = 22
SEQ = 2048
NCORES = 8
NSEQ = 4
T = 512
NTL = SEQ // T
SB = 16
TS = 64
WIN = (128, 512, 2048)
DIL = (1, 4, 16)
ALPHA = 2.0 ** 0.25
LN_EPS = 1e-5
EPS2 = LN_EPS / (ALPHA * ALPHA)
SCALE = 0.125
NEGB = -30000.0
CH = 16
CHW = CH * 128
NSLOT = 6
NPOOL = 24

OQ, OK_, OV, OU, OVV, OGA, OGB = 0, 768, 1536, 2304, 2816, 3328, 4352


def _stream_plan():
    blocks = []
    idx = {}

    def add(key, name, r0, c0):
        idx[key] = len(blocks)
        blocks.append((name, r0, c0))

    for pre, (w1, w3, w2) in (("f1", ("f1_w1", "f1_w3", "f1_w2")),):
        for fc in range(FC):
            for dc in range(DC):
                add((pre + "w1", fc, dc), w1, dc * 128, fc * 128)
            for dc in range(DC):
                add((pre + "w3", fc, dc), w3, dc * 128, fc * 128)
        for oc in range(DC):
            for fc in range(FC):
                add((pre + "w2", oc, fc), w2, fc * 128, oc * 128)
    fm_cols = [OQ + i * 128 for i in range(6)] + [OK_ + i * 128 for i in range(6)] + [OU + i * 128 for i in range(4)]
    for cc, c0 in enumerate(fm_cols):
        for dc in range(DC):
            add(("wq", cc, dc), "w_in", dc * 128, c0)
    while len(blocks) % 32:
        blocks.append(None)
    for cg in (3, 0, 1, 2):
        if cg < 3:
            cols = [OK_ + cg * 256, OK_ + cg * 256 + 128, OV + cg * 256, OV + cg * 256 + 128]
        else:
            cols = [OVV + i * 128 for i in range(4)]
        for dc in range(DC):
            for i, c0 in enumerate(cols):
                add(("wtm", cg, dc, i), "w_in", dc * 128, c0)
    for oc in range(DC):
        for dc in range(DC):
            add(("ga", oc, dc), "w_in", dc * 128, OGA + oc * 128)
        for kc in range(2):
            add(("oa", oc, kc), "w_oa", kc * 128, oc * 128)
        for dc in range(DC):
            add(("gb", oc, dc), "w_in", dc * 128, OGB + oc * 128)
        for kc in range(4):
            add(("ob", oc, kc), "w_ob", kc * 128, oc * 128)
    for oc in range(DC):
        for kc in range(DC):
            add(("wo", oc, kc), "w_out", kc * 128, oc * 128)
    for pre, (w1, w3, w2) in (("f2", ("f2_w1", "f2_w3", "f2_w2")),):
        for fc in range(FC):
            for dc in range(DC):
                add((pre + "w1", fc, dc), w1, dc * 128, fc * 128)
            for dc in range(DC):
                add((pre + "w3", fc, dc), w3, dc * 128, fc * 128)
        for oc in range(DC):
            for fc in range(FC):
                add((pre + "w2", oc, fc), w2, fc * 128, oc * 128)
    return blocks, idx


BLOCKS, BIDX = _stream_plan()
NBLK = len(BLOCKS)
NCH = (NBLK + CH - 1) // CH


def _t5_bucket(dist):
    dist = np.asarray(dist, np.int64)
    max_exact = 16
    large = max_exact + (np.log(np.maximum(dist, max_exact) / max_exact) / np.log(2048 / max_exact) * 16).astype(np.int64)
    large = np.minimum(large, 31)
    return np.where(dist < max_exact, dist, large).astype(np.int64)


class _Res:
    __slots__ = ("w", "rd")

    def __init__(self):
        self.w = None
        self.rd = {}


class Ctx:
    def __init__(self, nc):
        self.nc = nc
        self.es = ExitStack()
        self.h = {"pe": nc.tensor, "act": nc.scalar, "dve": nc.vector, "pool": nc.gpsimd, "sp": nc.sync}
        self.sems = {}
        self.cnt = {}
        for name in self.h:
            self.sems[name] = self.es.enter_context(nc.semaphore("s_" + name))
            self.cnt[name] = 0
        self.pool = []
        for i in range(NPOOL):
            k = "d%d" % i
            self.sems[k] = self.es.enter_context(nc.semaphore("s_" + k))
            self.cnt[k] = 0
            self.pool.append(k)
        self.pool_i = 0
        self.res = {}
        self.seen = {name: {} for name in self.h}
        self.nins = {name: 0 for name in self.h}
        self.banks = []
        self.bank_rot = list(range(8))
        self.bank_i = 0

    def sb(self, name, shape, dt, es=None):
        return (es or self.es).enter_context(self.nc.sbuf_tensor("sb_" + name, shape, dt))

    def _wait(self, eng, tok):
        k, v = tok
        if self.seen[eng].get(k, 0) < v:
            self.h[eng].wait_ge(self.sems[k], v)
            self.seen[eng][k] = v

    def _deps(self, eng, R, W):
        need = {}

        def add(tok, raw):
            k, v = tok
            if k == eng and (not raw or eng == "pe"):
                return
            if need.get(k, 0) < v:
                need[k] = v

        for key in R:
            r = self.res.get(key)
            if r is not None and r.w is not None:
                add(r.w, True)
        for key in W:
            r = self.res.get(key)
            if r is not None:
                if r.w is not None:
                    add(r.w, False)
                for k, v in r.rd.items():
                    add((k, v), False)
        for k, v in need.items():
            self._wait(eng, (k, v))

    def _record(self, tok, R, W):
        k, v = tok
        for key in R:
            r = self.res.get(key)
            if r is None:
                r = self.res[key] = _Res()
            if r.rd.get(k, 0) < v:
                r.rd[k] = v
        for key in W:
            r = self.res.get(key)
            if r is None:
                r = self.res[key] = _Res()
            r.w = tok
            r.rd = {}

    def op(self, eng, fn, R=(), W=()):
        self._deps(eng, R, W)
        ins = fn(self.h[eng])
        self.cnt[eng] += 1
        ins.then_inc(self.sems[eng], 1)
        tok = (eng, self.cnt[eng])
        self._record(tok, R, W)
        self.nins[eng] += 1
        return tok

    def mm(self, items, W=()):
        allR = []
        ins = None
        first = True
        for fn, R in items:
            self._deps("pe", R, W if first else ())
            first = False
            ins = fn(self.h["pe"])
            self.nins["pe"] += 1
            for k in R:
                if k not in allR:
                    allR.append(k)
        self.cnt["pe"] += 1
        ins.then_inc(self.sems["pe"], 1)
        tok = ("pe", self.cnt["pe"])
        self._record(tok, allR, W)
        return tok

    def dma(self, out, in_, R=(), W=(), q="pool"):
        ch = self.pool[self.pool_i]
        self.pool_i = (self.pool_i + 1) % NPOOL
        if self.cnt[ch] > 0:
            self._wait(q, (ch, self.cnt[ch]))
        self._deps(q, R, W)
        ins = self.h[q].dma_start(out=out, in_=in_)
        self.cnt[ch] += 16
        ins.then_inc(self.sems[ch], 16)
        tok = (ch, self.cnt[ch])
        self._record(tok, R, W)
        self.nins[q] += 1
        return tok

    def barrier(self):
        for e in self.h:
            for k in list(self.cnt):
                if k != e and self.cnt[k] > 0:
                    self._wait(e, (k, self.cnt[k]))

    def finish(self, q="sp"):
        for k in self.cnt:
            if k != q and self.cnt[k] > 0:
                self._wait(q, (k, self.cnt[k]))

    def bank(self):
        i = self.bank_rot[self.bank_i % len(self.bank_rot)]
        self.bank_i += 1
        return self.banks[i], ("pb", i)


def build(nseq=NSEQ, do_sample=True, debug=False):
    nc = bass.Bass("TRN2", target_bir_lowering=False)
    c = Ctx(nc)
    dbg = {}
    if debug:
        for nm in ("r1", "h", "r2", "h2"):
            dbg[nm] = nc.dram_tensor("dbg_" + nm, [128, DC, T], F32, kind="ExternalOutput").ap()

    def din(name, shape, dt=F32):
        return nc.dram_tensor(name, list(shape), dt, kind="ExternalInput").ap()

    def dout(name, shape):
        return nc.dram_tensor(name, list(shape), F32, kind="ExternalOutput").ap()

    xp = din("xp", [nseq, 128, DC, SEQ])
    xs = din("xs", [128, DC, TS])
    wst = din("wst", [128, NCH * CHW])
    pp_d = din("pp", [128, 80])
    NBT = 2048 + 512 + 512 + 512
    bt_d = din("bt", [128, NBT])
    bts_d = din("bts", [128, 1024])
    wsT_d = din("wsT", [128, 4, 128])
    wsTs_d = din("wsTs", [128, 4, 64])
    relb_d = din("relb", [32, 12])
    cf_d = din("cf", [128, 128 + 128 + 64 + 127 + 256])
    cb_d = din("cb", [128, 128 + 64 + 128 + 127], BF16)
    oh_d = din("oh", [32, 3 * 383])
    mka_d = din("mka", [4, 3 * 383])
    ohA_d = din("ohA", [32, 12 * 128])
    mkA_d = din("mkA", [128, 48])
    ohB_d = din("ohB", [32, 12 * 64])
    mkB_d = din("mkB", [64, 48])
    st_d = [din("st%d" % g, [SB, WIN[g], 512]) for g in range(3)]

    yp = dout("yp", [nseq, 128, DC, SEQ])
    ys = dout("ys", [128, DC, TS])
    pw = [dout("pw%d" % g, [nseq, WIN[g], 512]) for g in range(3)]
    sw = [dout("sw%d" % g, [SB, WIN[g], 512]) for g in range(3)]
    sv = dout("sv", [TS, 512])

    wbf_t = nc.dram_tensor("wbf", [NCH, 128, CHW], BF16, kind="Internal")
    wbf = wbf_t.ap()
    ebs_t = nc.dram_tensor("ebs", [12, 383], F32, kind="Internal")
    qsc_t = nc.dram_tensor("qsc", [TS, 768], F32, kind="Internal")

    for i in range(8):
        c.banks.append(c.es.enter_context(nc.psum_tensor("pb%d" % i, [128, 512], F32)))
    ws = [c.sb("ws%d" % i, [128, CHW], BF16) for i in range(NSLOT)]
    pp = c.sb("pp", [128, 80], F32)
    bt = c.sb("bt", [128, NBT], F32)
    cf = c.sb("cf", [128, 128 + 128 + 64 + 127 + 256], F32)
    cb = c.sb("cb", [128, 128 + 64 + 128 + 127], BF16)
    EB = c.sb("EB", [128, 12, 256], BF16)
    wm = c.sb("wm", [128, 4, 128], BF16)
    wms = c.sb("wms", [128, 4, 64], BF16)
    relb = c.sb("relb", [32, 12], F32)
    epsc = c.sb("epsc", [128, 2], F32)
    dmy = c.sb("dmy", [128, 2], F32)

    identf = cf[:, 0:128]
    triu = cf[:, 128:256]
    masks = cf[:, 256:320]
    zsel = cf[:, 320:447]
    gsel = cf[:, 447:703]
    onesb = cb[:, 0:128]
    ones64 = cb[:, 128:192]
    Jb = cb[:, 192:320]
    zselb = cb[:, 320:447]
    PLN = {1: (0, 8), 2: (16, 24), 3: (32, 40)}
    PBQ, PBK, PBU, PBGA, PBGB = 48, 54, 60, 64, 72
    BKV, BSG, BSB, BSGB = 0, 2048, 2560, 3072
    BQ, BSGBS = 0, 768

    c.dma(pp[:], pp_d[:, :], W=["pp"])
    c.dma(bt[:], bt_d[:, :], W=["bt"])
    c.dma(cf[:], cf_d[:, :], W=["cf"])
    c.dma(cb[:], cb_d[:, :], W=["cb"])
    c.dma(relb[:], relb_d[:, :], W=["relb"])
    c.op("dve", lambda e: e.memset(epsc[:, 0:1], EPS2), W=["epsc"])
    c.op("dve", lambda e: e.memset(epsc[:, 1:2], LN_EPS), R=["epsc"], W=["epsc"])

    if do_sample:
        for g in range(3):
            n = (WIN[g] - 4) * 512
            for b in range(SB):
                src = bass.AP(tensor=st_d[g].tensor, offset=b * WIN[g] * 512 + 4 * 512, ap=[[n // 16, 16], [1, n // 16]])
                dst = bass.AP(tensor=sw[g].tensor, offset=b * WIN[g] * 512, ap=[[n // 16, 16], [1, n // 16]])
                c.dma(dst, src, W=[("swcopy", g, b)])

    with ExitStack() as p0:
        oh = c.sb("oh", [32, 3 * 383], F32, p0)
        mka = c.sb("mka", [4, 3 * 383], F32, p0)
        ext = c.sb("ext", [4, 3 * 383], F32, p0)
        hank = c.sb("hank", [128, 12, 256], F32, p0)
        hankb = c.sb("hankb", [128, 12, 256], BF16, p0)
        wsT = c.sb("wsT", [128, 4, 128], F32, p0)
        wsTs = c.sb("wsTs", [128, 4, 64], F32, p0)
        c.dma(oh[:], oh_d[:, :], W=["oh"])
        c.dma(mka[:], mka_d[:, :], W=["mka"])
        c.dma(wsT[:], wsT_d[:, :, :], W=["wsT"])
        c.dma(wsTs[:], wsTs_d[:, :, :], W=["wsTs"])
        for g in range(3):
            pb_, pk = c.bank()
            c.mm([(lambda e, g=g, pb_=pb_: e.matmul(pb_[0:4, 0:383], lhsT=relb[:, 4 * g:4 * g + 4], rhs=oh[:, g * 383:(g + 1) * 383],
                                                  start=True, stop=True), ["relb", "oh"])], W=[pk])
            c.op("dve", lambda e, g=g, pb_=pb_: e.tensor_tensor(out=ext[:, g * 383:(g + 1) * 383], in0=pb_[0:4, 0:383],
                                                             in1=mka[:, g * 383:(g + 1) * 383], op=ALU.add), R=[pk, "mka"], W=[("ext", g)])
            c.op("act", lambda e, g=g: e.activation(out=ext[:, g * 383:(g + 1) * 383], in_=ext[:, g * 383:(g + 1) * 383], func=AF.Exp),
                 R=[("ext", g)], W=[("ext", g)])
            c.dma(ebs_t.ap()[4 * g:4 * g + 4, :], ext[:, g * 383:(g + 1) * 383], R=[("ext", g)], W=[("ebs", g)])
        src = bass.AP(tensor=ebs_t, offset=0, ap=[[1, 128], [383, 12], [1, 256]])
        c.dma(hank[:, :, :], src, R=[("ebs", g) for g in range(3)], W=["hank"])
        c.op("dve", lambda e: e.tensor_copy(out=hankb[:], in_=hank[:]), R=["hank"], W=["hankb"])
        for hh in range(12):
            pb_, pk = c.bank()
            c.mm([(lambda e, hh=hh, pb_=pb_: e.matmul(pb_[:, 0:256], lhsT=Jb, rhs=hankb[:, hh, :], start=True, stop=True), ["cb", "hankb"])], W=[pk])
            c.op("act", lambda e, hh=hh, pb_=pb_: e.copy(out=EB[:, hh, :], in_=pb_[:, 0:256]), R=[pk], W=["EB"])
        c.op("dve", lambda e: e.tensor_tensor(out=wm[:], in0=wsT[:], in1=triu.unsqueeze(1).to_broadcast([128, 4, 128]), op=ALU.mult),
             R=["wsT", "cf"], W=["wm"])
        c.op("dve", lambda e: e.tensor_tensor(out=wms[0:64], in0=wsTs[0:64], in1=masks[0:64].unsqueeze(1).to_broadcast([64, 4, 64]), op=ALU.mult),
             R=["wsTs", "cf"], W=["wms"])
        c.barrier()

    S = [c.sb("S0", [128, DC, T], F32), None]
    Sb = c.sb("Sb", [128, DC, T], BF16)
    t8 = c.sb("t8", [128, DC, T], BF16)
    uT = c.sb("uT", [128, 4, T], BF16)
    vn = c.sb("vn", [128, 4, 512], BF16)
    vt = [c.sb("vt%d" % i, [128, 512], F32) for i in range(4)]
    attb = c.sb("attb", [128, 2, T], BF16)
    rL = c.sb("rL", [128, T], F32)
    yb = c.sb("yb", [128, 4, T], BF16)
    rstd = c.sb("rstd", [128, T], F32)
    sil = [c.sb("sil%d" % i, [128, T], F32) for i in range(2)]
    sg = [c.sb("sg%d" % i, [128, T], F32) for i in range(2)]
    m1 = [c.sb("m1_0", [128, T], F32)]
    mean_s = m1[0]
    st6 = c.sb("st6", [128, 4, 6], F32)
    mv = c.sb("mv", [128, 4, 4], F32)
    ln_pending = []
    sgu_pending = []
    pm = ExitStack()
    S[1] = c.sb("S1", [128, DC, T], F32, pm)
    bufs = {"h1": c.sb("h1", [128, FC, T], BF16, pm)}
    qT = c.sb("qT", [128, 6, T], BF16, pm)
    kT01 = [c.sb("kT%d" % g, [128, 2, 2, T], BF16, pm) for g in range(2)]
    kT2 = c.sb("kT2", [128, 2, SEQ], BF16, pm)
    Vc01 = [c.sb("Vc%d" % g, [128, 2, 4, 256], BF16, pm) for g in range(2)]
    Vc2 = c.sb("Vc2", [128, 16, 256], BF16, pm)
    kvst = [c.sb("kvst%d" % i, [128, 512], F32, pm) for i in range(4)]
    tmpE = [c.sb("tmpE%d" % i, [128, 512], F32, pm) for i in range(2)]
    PT = [c.sb("PT%d" % i, [128, 512], BF16, pm) for i in range(2)]

    cnt = {"kv": 0, "vt": 0, "te": 0, "sil": 0, "sg": 0}

    wstate = {"issued": 0}
    total_tiles = nseq * NTL + (1 if do_sample else 0)

    def ws_ensure(G):
        lim = min(G + NSLOT - 3, total_tiles * NCH - 1)
        while wstate["issued"] <= lim:
            g_ = wstate["issued"]
            if g_ < NCH:
                c.dma(ws[g_ % NSLOT][:], wst[:, g_ * CHW:(g_ + 1) * CHW], W=[("ws", g_ % NSLOT)], q="pool")
                c.dma(wbf[g_], ws[g_ % NSLOT][:], R=[("ws", g_ % NSLOT)], W=[("wbf", g_)], q="sp")
            else:
                c.dma(ws[g_ % NSLOT][:], wbf[g_ % NCH], R=[("wbf", g_ % NCH)], W=[("ws", g_ % NSLOT)], q="sp")
            wstate["issued"] += 1

    def W_(ti, key, width=128):
        b = BIDX[key]
        G = ti * NCH + b // CH
        ws_ensure(G)
        off = (b % CH) * 128
        slot = G % NSLOT
        return ws[slot][:, off:off + width], ("ws", slot)

    def cast_stream(si, n, dcs=None):
        for dc in range(DC):
            if dc % 2 == 0:
                c.op("act", lambda e, dc=dc: e.copy(out=Sb[:, dc, :n], in_=S[si][:, dc, :n]), R=[("S", si, dc)], W=[("Sb", dc)])
            else:
                c.op("dve", lambda e, dc=dc: e.tensor_copy(out=Sb[:, dc, :n], in_=S[si][:, dc, :n]), R=[("S", si, dc)], W=[("Sb", dc)])

    def stat_begin():
        pm, km = c.bank()
        pe2, ke = c.bank()
        c.bank_rot = [b for b in range(8) if ("pb", b) not in (km, ke)]
        return {"pm": pm, "km": km, "pe2": pe2, "ke": ke}

    def stat_add(st, dc, n, sq_x=False):
        sq, sqk = xbuf(dc, n) if sq_x else (t8[:, dc, :n], ("t8", dc))
        c.mm([(lambda e: e.matmul(st["pm"][:, :n], lhsT=onesb, rhs=Sb[:, dc, :n], start=(dc == 0), stop=(dc == DC - 1)), ["cb", ("Sb", dc)])],
             W=[st["km"]])
        c.mm([(lambda e: e.matmul(st["pe2"][:, :n], lhsT=onesb, rhs=sq, start=(dc == 0), stop=(dc == DC - 1)), ["cb", sqk])],
             W=[st["ke"]])

    def preload(func):
        c.op("act", lambda e: e.activation(out=dmy[:, 0:1], in_=epsc[:, 0:1], func=func), R=["epsc"], W=["dmy"])

    def ffn(ti, si, n, pre, alt=False):
        for fc in range(FC):
            pa, ka = c.bank()
            pb_, kb = c.bank()
            ia, ib = [], []
            for dc in range(DC):
                wa, wk = W_(ti, (pre + "w1", fc, dc))
                ia.append((lambda e, wa=wa, dc=dc, pa=pa: e.matmul(pa[:, :n], lhsT=wa, rhs=Sb[:, dc, :n], start=(dc == 0), stop=(dc == DC - 1)),
                           [wk, ("Sb", dc)]))
            c.mm(ia, W=[ka])
            for dc in range(DC):
                wb, wk = W_(ti, (pre + "w3", fc, dc))
                ib.append((lambda e, wb=wb, dc=dc, pb_=pb_: e.matmul(pb_[:, :n], lhsT=wb, rhs=Sb[:, dc, :n], start=(dc == 0), stop=(dc == DC - 1)),
                           [wk, ("Sb", dc)]))
            c.mm(ib, W=[kb])
            sb_i = cnt["sil"] % 2
            cnt["sil"] += 1
            c.op("act", lambda e, pa=pa, sb_i=sb_i: e.activation(out=sil[sb_i][:, :n], in_=pa[:, :n], func=AF.Silu), R=[ka], W=[("sil", sb_i)])
            c.op("dve", lambda e, pb_=pb_, sb_i=sb_i, fc=fc: e.tensor_tensor(out=bufs['h1'][:, fc, :n], in0=pb_[:, :n], in1=sil[sb_i][:, :n], op=ALU.mult),
                 R=[kb, ("sil", sb_i)], W=[("h1", fc)])
            if ln_pending:
                ln_pending.pop(0)()
        while ln_pending:
            ln_pending.pop(0)()
        preload(AF.Ln)
        stt = None if alt else stat_begin()
        for oc in range(DC):
            po, ko = c.bank()
            items = []
            for fc in range(FC):
                w2, wk = W_(ti, (pre + "w2", oc, fc))
                items.append((lambda e, w2=w2, fc=fc, po=po: e.matmul(po[:, :n], lhsT=w2, rhs=bufs['h1'][:, fc, :n], start=(fc == 0), stop=(fc == FC - 1)),
                              [wk, ("h1", fc)]))
            c.mm(items, W=[ko])
            if stt is not None and oc >= 1:
                stat_add(stt, oc - 1, n)
            resid(si, n, oc, po, ko, 0.5 / ALPHA, alt)
        if stt is not None:
            stat_add(stt, DC - 1, n)
        return stt

    def xbuf(dc, n):
        return (uT[:, dc, :n], ("uT", dc)) if dc < 4 else (yb[:, dc - 4, :n], ("yb", dc - 4))

    def resid(si, n, oc, po, ko, coef, alt=False, sq_x=False):
        c.op("dve", lambda e: e.scalar_tensor_tensor(out=S[si][:, oc, :n], in0=po[:, :n], scalar=coef, in1=S[si][:, oc, :n],
                                                      op0=ALU.mult, op1=ALU.add), R=[ko, ("S", si, oc)], W=[("S", si, oc)])
        dst, dkey = xbuf(oc, n) if alt else (Sb[:, oc, :n], ("Sb", oc))
        c.op("act", lambda e: e.copy(out=dst, in_=S[si][:, oc, :n]), R=[("S", si, oc)], W=[dkey])
        sq, sqk = xbuf(oc, n) if sq_x else (t8[:, oc, :n], ("t8", oc))
        c.op("act", lambda e: e.activation(out=sq, in_=S[si][:, oc, :n], func=AF.Square), R=[("S", si, oc)], W=[sqk])

    def layer_norm(si, n, which, need_bf16=True, defer=None, pre=None):
        g0, b0 = PLN[which]
        npool = 0
        st = {}

        def stats():
            if pre is not None:
                pm, km, pe2, ke = pre["pm"], pre["km"], pre["pe2"], pre["ke"]
                st["pm"], st["km"] = pm, km
            else:
                pm, km = c.bank()
                pe2, ke = c.bank()
                st["pm"], st["km"] = pm, km
                srcs = [xbuf(dc, n) if defer is not None else (Sb[:, dc, :n], ("Sb", dc)) for dc in range(DC)]
                c.mm([(lambda e, dc=dc: e.matmul(pm[:, :n], lhsT=onesb, rhs=srcs[dc][0], start=(dc == 0), stop=(dc == DC - 1)), ["cb", srcs[dc][1]])
                      for dc in range(DC)], W=[km])
                c.mm([(lambda e, dc=dc: e.matmul(pe2[:, :n], lhsT=onesb, rhs=t8[:, dc, :n], start=(dc == 0), stop=(dc == DC - 1)), ["cb", ("t8", dc)])
                      for dc in range(DC)], W=[ke])
            c.op("act", lambda e: e.activation(out=rstd[:, :n], in_=pm[:, :n], func=AF.Square), R=[km], W=["rstd"])
            if defer is not None or npool:
                c.op("act", lambda e: e.copy(out=mean_s[:, :n], in_=pm[:, :n]), R=[km], W=[("m1", 0)])
            c.op("dve", lambda e: e.tensor_tensor(out=rstd[:, :n], in0=pe2[:, :n], in1=rstd[:, :n], op=ALU.subtract), R=[ke, "rstd"], W=["rstd"])
            c.op("act", lambda e: e.activation(out=rstd[:, :n], in_=rstd[:, :n], func=AF.Ln, bias=epsc[:, 0:1]), R=["rstd", "epsc"], W=["rstd"])
            c.op("act", lambda e: e.activation(out=rstd[:, :n], in_=rstd[:, :n], func=AF.Exp, scale=-0.5), R=["rstd"], W=["rstd"])

        def csub(dc, eng="dve"):
            if defer is not None or eng == "pool":
                c.op(eng, lambda e: e.tensor_tensor(out=S[si][:, dc, :n], in0=S[si][:, dc, :n], in1=mean_s[:, :n], op=ALU.subtract),
                     R=[("S", si, dc), ("m1", 0)], W=[("S", si, dc)])
            else:
                c.op(eng, lambda e: e.tensor_tensor(out=S[si][:, dc, :n], in0=S[si][:, dc, :n], in1=st["pm"][:, :n], op=ALU.subtract),
                     R=[("S", si, dc), st["km"]], W=[("S", si, dc)])

        def cmul(dc, eng="dve"):
            c.op(eng, lambda e: e.tensor_tensor(out=S[si][:, dc, :n], in0=S[si][:, dc, :n], in1=rstd[:, :n], op=ALU.mult),
                 R=[("S", si, dc), "rstd"], W=[("S", si, dc)])

        def chunk(dc, eng="dve"):
            csub(dc, eng)
            cmul(dc, eng)

        def tobf(dc):
            c.op("act", lambda e: e.activation(out=Sb[:, dc, :n], in_=S[si][:, dc, :n], func=AF.Identity,
                                               scale=pp[:, g0 + dc:g0 + dc + 1], bias=pp[:, b0 + dc:b0 + dc + 1]),
                 R=[("S", si, dc), "pp"], W=[("Sb", dc)])

        def affine(dc):
            c.op("act", lambda e: e.activation(out=S[si][:, dc, :n], in_=S[si][:, dc, :n], func=AF.Identity,
                                               scale=pp[:, g0 + dc:g0 + dc + 1], bias=pp[:, b0 + dc:b0 + dc + 1]),
                 R=[("S", si, dc), "pp"], W=[("S", si, dc)])

        if defer is None:
            stats()
            for dc in range(4):
                csub(dc)
            for dc in range(DC):
                cmul(dc)
                if need_bf16:
                    tobf(dc)
                if dc + 4 < DC:
                    csub(dc + 4)
            for dc in range(DC):
                affine(dc)
            if pre is not None:
                c.bank_rot = list(range(8))
        else:
            ln_pending.append(stats)
            for dc in range(DC):
                ln_pending.append(lambda dc=dc: (chunk(dc), affine(dc)))
            ln_pending.append(defer)

    def proj_fm(ti, n, key, idx, evac):
        p, k = c.bank()
        items = []
        for dc in range(DC):
            w, wk = W_(ti, (key, idx, dc))
            items.append((lambda e, w=w, dc=dc: e.matmul(p[:, :n], lhsT=w, rhs=Sb[:, dc, :n], start=(dc == 0), stop=(dc == DC - 1)), [wk, ("Sb", dc)]))
        c.mm(items, W=[k])
        evac(p, k)

    def sgu_rows(p, k, m, tb, sample):
        vi = tb
        v_ = vt[vi]
        c.op("dve", lambda e: e.tensor_tensor(out=v_[:m], in0=p[:m, :], in1=bt[:m, BKV + 1536:BKV + 2048], op=ALU.add), R=[k, "bt"], W=[("vt", vi)])
        c.op("act", lambda e: e.activation(out=v_[:m], in_=v_[:m], func=AF.Gelu), R=[("vt", vi)], W=[("vt", vi)])
        steps = [
            lambda: c.op("dve", lambda e: e.bn_stats(out=st6[:m, tb, :], in_=v_[:m]), R=[("vt", vi)], W=[("st6", tb)]),
            lambda: c.op("dve", lambda e: e.bn_aggr(out=mv[:m, tb, 0:2], in_=st6[:m, tb, :]), R=[("st6", tb)], W=[("mv", tb)]),
            lambda: c.op("act", lambda e: e.activation(out=mv[:m, tb, 2:3], in_=mv[:m, tb, 1:2], func=AF.Ln, bias=epsc[:m, 1:2]),
                         R=[("mv", tb), "epsc"], W=[("mv2", tb)]),
            lambda: c.op("act", lambda e: e.activation(out=mv[:m, tb, 3:4], in_=mv[:m, tb, 2:3], func=AF.Exp, scale=-0.5), R=[("mv2", tb)], W=[("mv3", tb)]),
            lambda: c.op("dve", lambda e: e.tensor_scalar(out=v_[:m], in0=v_[:m], scalar1=mv[:m, tb, 0:1], scalar2=mv[:m, tb, 3:4],
                                                          op0=ALU.subtract, op1=ALU.mult), R=[("vt", vi), ("mv", tb), ("mv3", tb)], W=[("vt", vi)]),
            lambda: c.op("dve", lambda e: e.tensor_tensor(out=v_[:m], in0=v_[:m], in1=bt[:m, BSG:BSG + 512], op=ALU.mult), R=[("vt", vi), "bt"], W=[("vt", vi)]),
            lambda: c.op("dve", lambda e: e.tensor_tensor(out=v_[:m], in0=v_[:m], in1=bt[:m, BSB:BSB + 512], op=ALU.add), R=[("vt", vi), "bt"], W=[("vt", vi)]),
            lambda: c.op("act", lambda e: e.copy(out=vn[:m, tb, :], in_=v_[:m]), R=[("vt", vi)], W=[("vn", tb)]),
        ]
        if sample:
            steps.append(lambda: c.dma(sv[:, :], v_[:m], R=[("vt", vi)]))
        return steps

    def sgu_mix(n, sample):
        for gg in range(4):
            p, k = c.bank()
            items = []
            if not sample:
                for tb in range(4):
                    items.append((lambda e, tb=tb: e.matmul(p[:, tb * 128:(tb + 1) * 128], lhsT=vn[:, tb, gg * 128:(gg + 1) * 128], rhs=wm[:, gg, :],
                                                            start=True, stop=True), [("vn", tb), "wm"]))
            else:
                items.append((lambda e: e.matmul(p[:, 0:64], lhsT=vn[0:64, 0, gg * 128:(gg + 1) * 128], rhs=wms[0:64, gg, :], start=True, stop=True),
                              [("vn", 0), "wms"]))
            c.mm(items, W=[k])
            vi = cnt["vt"] % 4
            cnt["vt"] += 1
            v_ = vt[vi]
            if not sample:
                bia = bt[:, BSGB + gg * 128:BSGB + (gg + 1) * 128].unsqueeze(1).to_broadcast([128, 4, 128])
                c.op("dve", lambda e: e.tensor_tensor(out=v_[:, :].rearrange("p (a b) -> p a b", a=4), in0=p[:, :].rearrange("p (a b) -> p a b", a=4),
                                                      in1=bia, op=ALU.add), R=[k, "bt"], W=[("vt", vi)])
            else:
                c.op("dve", lambda e: e.tensor_tensor(out=v_[:, :n], in0=p[:, :n], in1=bufs["bts"][:, BSGBS + gg * 64:BSGBS + (gg + 1) * 64], op=ALU.add),
                     R=[k, "bts"], W=[("vt", vi)])
            c.op("dve", lambda e: e.tensor_tensor(out=yb[:, gg, :n], in0=v_[:, :n], in1=uT[:, gg, :n], op=ALU.mult),
                 R=[("vt", vi), ("uT", gg)], W=[("yb", gg)])

    def mix_stage(ti, si, n):
        for oc in range(DC):
            def ev_ga(p, k, oc=oc):
                i = cnt["sg"] % 2
                cnt["sg"] += 1
                c.op("act", lambda e: e.activation(out=sg[i][:, :n], in_=p[:, :n], func=AF.Sigmoid, bias=pp[:, PBGA + oc:PBGA + oc + 1]),
                     R=[k, "pp"], W=[("sg", i)])
                ev_ga.i = i
            proj_fm(ti, n, "ga", oc, ev_ga)
            pA, kA = c.bank()
            items = []
            for kc in range(2):
                w, wk = W_(ti, ("oa", oc, kc))
                items.append((lambda e, w=w, kc=kc: e.matmul(pA[:, :n], lhsT=w, rhs=attb[:, kc, :n], start=(kc == 0), stop=(kc == 1)), [wk, ("attb", kc)]))
            c.mm(items, W=[kA])
            ia = ev_ga.i
            mi = 0
            c.op("dve", lambda e: e.tensor_tensor(out=m1[mi][:, :n], in0=pA[:, :n], in1=sg[ia][:, :n], op=ALU.mult), R=[kA, ("sg", ia)], W=[("m1", mi)])

            def ev_gb(p, k, oc=oc):
                i = cnt["sg"] % 2
                cnt["sg"] += 1
                c.op("act", lambda e: e.activation(out=sg[i][:, :n], in_=p[:, :n], func=AF.Sigmoid, bias=pp[:, PBGB + oc:PBGB + oc + 1]),
                     R=[k, "pp"], W=[("sg", i)])
                ev_gb.i = i
            proj_fm(ti, n, "gb", oc, ev_gb)
            pB, kB = c.bank()
            items = []
            for kc in range(4):
                w, wk = W_(ti, ("ob", oc, kc))
                items.append((lambda e, w=w, kc=kc: e.matmul(pB[:, :n], lhsT=w, rhs=yb[:, kc, :n], start=(kc == 0), stop=(kc == 3)), [wk, ("yb", kc)]))
            c.mm(items, W=[kB])
            ib = ev_gb.i
            c.op("dve", lambda e: e.tensor_tensor(out=sg[ib][:, :n], in0=pB[:, :n], in1=sg[ib][:, :n], op=ALU.mult), R=[kB, ("sg", ib)], W=[("sg", ib)])
            c.op("dve", lambda e: e.tensor_tensor(out=t8[:, oc, :n], in0=m1[mi][:, :n], in1=sg[ib][:, :n], op=ALU.add),
                 R=[("m1", mi), ("sg", ib)], W=[("t8", oc)])
        preload(AF.Ln)
        stt = stat_begin()
        for oc in range(DC):
            po, ko = c.bank()
            items = []
            for kc in range(DC):
                w, wk = W_(ti, ("wo", oc, kc))
                items.append((lambda e, w=w, kc=kc, po=po: e.matmul(po[:, :n], lhsT=w, rhs=t8[:, kc, :n], start=(kc == 0), stop=(kc == DC - 1)), [wk, ("t8", kc)]))
            c.mm(items, W=[ko])
            if oc >= 1:
                stat_add(stt, oc - 1, n, sq_x=True)
            resid(si, n, oc, po, ko, 1.0 / ALPHA, sq_x=True)
        stat_add(stt, DC - 1, n, sq_x=True)
        return stt

    mix_stage.pend = []

    mix_stage.pend = []

    def prompt_attention(j):
        pr = j % 2
        c.bank_rot = [0, 1, 2, 3]
        NUM = [(c.banks[4], ("pb", 4)), (c.banks[5], ("pb", 5))]
        LB = [(c.banks[6], ("pb", 6)), (c.banks[7], ("pb", 7))]
        started = set()

        def exp_mul(p, k, eb_ap, m, lo, hi, view):
            i = cnt["te"] % 2
            cnt["te"] += 1
            c.op("act", lambda e: e.activation(out=tmpE[i][:m, lo:hi], in_=p[:m, lo:hi], func=AF.Exp, scale=SCALE), R=[k], W=[("tmpE", i)])
            c.op("dve", lambda e: e.tensor_tensor(out=view(PT[i][:m, lo:hi]), in0=view(tmpE[i][:m, lo:hi]), in1=eb_ap, op=ALU.mult),
                 R=[("tmpE", i), "EB"], W=[("PT", i)])
            return PT[i], ("PT", i)

        def numl(items_spec, ptk, h):
            hc, hr = h // 2, (h % 2) * 64
            items = []
            for vap, vkey, rhs, osel, m in items_spec:
                for kind in (0, 1):
                    tgt, tk = (NUM if kind == 0 else LB)[hc]
                    st = (kind, h) not in started
                    started.add((kind, h))
                    lhs = vap if kind == 0 else ones64[:m, :]
                    items.append((lambda e, lhs=lhs, rhs=rhs, tgt=tgt, st=st, osel=osel: e.matmul(osel(tgt[hr:hr + 64, :]), lhsT=lhs, rhs=rhs, start=st,
                                                                                                  stop=False, skip_group_check=True),
                                  [vkey, ptk, "cb"]))
            c.mm(items, W=[NUM[hc][1], LB[hc][1]])

        jobs = []
        for g in range(2):
            for h in range(4):
                for which in (0, 1):
                    if which == 1 and j == 0 and g == 1:
                        continue

                    def st_fn(g=g, h=h, which=which):
                        kT, Vc = kT01[g], Vc01[g]
                        hc, hr = h // 2, (h % 2) * 64
                        qc = g * 2 + hc
                        p, k = c.bank()
                        items = []
                        specs = []
                        lo = 128 if (which == 1 and j == 0) else 0
                        for sbk in range(4):
                            if g == 0:
                                qcols = slice(sbk * 128, (sbk + 1) * 128)
                                if which == 0:
                                    kap = kT[hr:hr + 64, hc, pr, sbk * 128:(sbk + 1) * 128]
                                    kkey = ("kT", 0, hc, pr)
                                    vap, vkey = Vc[:, pr, sbk, h * 64:(h + 1) * 64], ("Vc", 0, pr, sbk)
                                elif sbk > 0:
                                    kap = kT[hr:hr + 64, hc, pr, (sbk - 1) * 128:sbk * 128]
                                    kkey = ("kT", 0, hc, pr)
                                    vap, vkey = Vc[:, pr, sbk - 1, h * 64:(h + 1) * 64], ("Vc", 0, pr, sbk - 1)
                                else:
                                    if j == 0:
                                        continue
                                    kap = kT[hr:hr + 64, hc, 1 - pr, 384:512]
                                    kkey = ("kT", 0, hc, 1 - pr)
                                    vap, vkey = Vc[:, 1 - pr, 3, h * 64:(h + 1) * 64], ("Vc", 0, 1 - pr, 3)
                            else:
                                qcols = slice(sbk, T, 4)
                                pp_ = pr if which == 0 else 1 - pr
                                kap = kT[hr:hr + 64, hc, pp_, sbk:T:4]
                                kkey = ("kT", 1, hc, pp_)
                                vap, vkey = Vc[:, pp_, sbk, h * 64:(h + 1) * 64], ("Vc", 1, pp_, sbk)
                            osel = (lambda t_, qcols=qcols: t_[:, qcols])
                            items.append((lambda e, kap=kap, qcols=qcols, sbk=sbk: e.matmul(p[:, sbk * 128:(sbk + 1) * 128], lhsT=kap,
                                                                                            rhs=qT[hr:hr + 64, qc, qcols], start=True, stop=True),
                                          [kkey, ("qT", qc)]))
                            specs.append((vap, vkey, sbk, osel))
                        c.mm(items, W=[k])
                        return p, k, specs, lo

                    def post_fn(state, g=g, h=h, which=which):
                        p, k, specs, lo = state
                        eb = EB[:, g * 4 + h, :]
                        nb_ = (512 - lo) // 128
                        ebs = eb[:, which * 128:(which + 1) * 128].unsqueeze(1).to_broadcast([128, nb_, 128])
                        ptb, ptk = exp_mul(p, k, ebs, 128, lo, 512, lambda a, nb_=nb_: a.rearrange("p (a b) -> p a b", a=nb_))
                        numl([(vap, vkey, ptb[:, sbk * 128:(sbk + 1) * 128], osel, 128) for vap, vkey, sbk, osel in specs], ptk, h)

                    jobs.append((st_fn, post_fn))
        M = 32 * (j + 1)
        for h in range(4):
            def st_fn(h=h):
                hc, hr = h // 2, (h % 2) * 64
                qc = 4 + hc
                p, k = c.bank()
                items = []
                for r in range(16):
                    items.append((lambda e, r=r: e.matmul(p[:M, r * 32:(r + 1) * 32], lhsT=kT2[hr:hr + 64, hc, r:512 * (j + 1):16],
                                                          rhs=qT[hr:hr + 64, qc, r:T:16], start=True, stop=True),
                                  [("kT", 2, hc, jj) for jj in range(j + 1)] + [("qT", qc)]))
                c.mm(items, W=[k])
                return p, k

            def post_fn(state, h=h):
                p, k = state
                ebs = EB[:M, 8 + h, 32 * j:32 * (j + 1)].unsqueeze(1).to_broadcast([M, 16, 32])
                ptb, ptk = exp_mul(p, k, ebs, M, 0, 512, lambda a: a.rearrange("p (a b) -> p a b", a=16))
                numl([(Vc2[:M, r, h * 64:(h + 1) * 64], ("Vc", 2, r), ptb[:M, r * 32:(r + 1) * 32], (lambda t_, r=r: t_[:, r:T:16]), M)
                      for r in range(16)], ptk, h)

            jobs.append((st_fn, post_fn))
        states = [None] * len(jobs)
        states[0] = jobs[0][0]()
        for i in range(len(jobs)):
            if i + 1 < len(jobs):
                states[i + 1] = jobs[i + 1][0]()
            jobs[i][1](states[i])
            if sgu_pending:
                sgu_pending.pop(0)()
        for hc in range(2):
            c.op("dve", lambda e, hc=hc: e.reciprocal(out=rL[:, :], in_=LB[hc][0][:, :]), R=[LB[hc][1]], W=["rL"])
            c.op("dve", lambda e, hc=hc: e.tensor_tensor(out=attb[:, hc, :], in0=NUM[hc][0][:, :], in1=rL[:, :], op=ALU.mult),
                 R=[NUM[hc][1], "rL"], W=[("attb", hc)])
        c.bank_rot = list(range(8))
        c.bank_i = 0

    def prompt_tile(ti, s, j):
        si = ti % 2
        n = T
        pr = j % 2
        cast_stream(si, n)
        st1 = ffn(ti, si, n, "f1")
        if ti + 1 < len(tiles):
            load_x(ti + 1)
        if debug and ti == 0:
            c.dma(dbg["r1"][:, :, :], S[si][:, :, :], R=[("S", si, dc) for dc in range(DC)])
        layer_norm(si, n, 1, pre=st1)
        if debug and ti == 0:
            c.dma(dbg["h"][:, :, :], S[si][:, :, :], R=[("S", si, dc) for dc in range(DC)])
        for cc in range(16):
            if cc < 6:
                def ev(p, k, cc=cc):
                    c.op("act", lambda e: e.activation(out=qT[:, cc, :], in_=p[:, :], func=AF.Identity, bias=pp[:, PBQ + cc:PBQ + cc + 1]),
                         R=[k, "pp"], W=[("qT", cc)])
            elif cc < 12:
                def ev(p, k, cc=cc):
                    g, hc = (cc - 6) // 2, (cc - 6) % 2
                    if g < 2:
                        dst, key = kT01[g][:, hc, pr, :], ("kT", g, hc, pr)
                    else:
                        dst, key = kT2[:, hc, j * T:(j + 1) * T], ("kT", 2, hc, j)
                    c.op("act", lambda e: e.activation(out=dst, in_=p[:, :], func=AF.Identity, bias=pp[:, PBK + cc - 6:PBK + cc - 5]),
                         R=[k, "pp"], W=[key])
            else:
                def ev(p, k, cc=cc):
                    c.op("act", lambda e: e.activation(out=uT[:, cc - 12, :], in_=p[:, :], func=AF.Gelu, bias=pp[:, PBU + cc - 12:PBU + cc - 11]),
                         R=[k, "pp"], W=[("uT", cc - 12)])
            proj_fm(ti, n, "wq", cc, ev)
        for cg in (3, 0, 1, 2):
            tb_steps = []
            for mb in range(4):
                if cg == 1:
                    lcols = lambda dc, mb=mb: Sb[:, dc, mb:T:4]
                else:
                    lcols = lambda dc, mb=mb: Sb[:, dc, mb * 128:(mb + 1) * 128]
                p, k = c.bank()
                items = []
                need_k = cg >= 2 or (cg == 1 and j == NTL - 1) or (cg == 0 and j == NTL - 1 and mb == 3)
                c0 = 0 if need_k else 256
                for dc in range(DC):
                    w, wk = W_(ti, ("wtm", cg, dc, 0), 512)
                    items.append((lambda e, w=w, dc=dc, lcols=lcols, c0=c0: e.matmul(p[:, c0:512], lhsT=lcols(dc), rhs=w[:, c0:512],
                                                                                     start=(dc == 0), stop=(dc == DC - 1)),
                                  [wk, ("Sb", dc)]))
                c.mm(items, W=[k])
                if cg == 3:
                    tb_steps.append(sgu_rows(p, k, 128, mb, False))
                    continue
                g = cg
                ki = cnt["kv"] % 4
                cnt["kv"] += 1
                kv = kvst[ki]
                c.op("dve", lambda e: e.tensor_tensor(out=kv[:, c0:512], in0=p[:, c0:512], in1=bt[:, BKV + g * 512 + c0:BKV + (g + 1) * 512], op=ALU.add),
                     R=[k, "bt"], W=[("kvst", ki)])
                if g < 2:
                    c.op("act", lambda e: e.copy(out=Vc01[g][:, pr, mb, :], in_=kv[:, 256:512]), R=[("kvst", ki)], W=[("Vc", g, pr, mb)])
                if g == 0 and j == NTL - 1 and mb == 3:
                    c.dma(pw[0][s, :, :], kv[:], R=[("kvst", ki)])
                if g == 1 and j == NTL - 1:
                    c.dma(pw[1][s, mb:512:4, :], kv[:], R=[("kvst", ki)])
                if g == 2:
                    c.dma(pw[2][s, T * j + mb * 128:T * j + (mb + 1) * 128, :], kv[:], R=[("kvst", ki)], W=[("pw2", mb)])
            if cg == 2:
                src = bass.AP(tensor=pw[2].tensor, offset=(s * SEQ + T * j) * 512 + 256, ap=[[16 * 512, 32], [512, 16], [1, 256]])
                c.dma(Vc2[32 * j:32 * (j + 1), :, :], src, R=[("pw2", mb) for mb in range(4)], W=[("Vc", 2, r) for r in range(16)])
            if cg == 3:
                preload(AF.Exp)
                for st_i in range(len(tb_steps[0])):
                    sgu_pending.append(lambda st_i=st_i, tbs=tb_steps: [tbs[tb][st_i]() for tb in range(4)])
        prompt_attention(j)
        while sgu_pending:
            sgu_pending.pop(0)()
        sgu_mix(n, False)
        st2 = mix_stage(ti, si, n)
        if debug and ti == 0:
            c.dma(dbg["r2"][:, :, :], S[si][:, :, :], R=[("S", si, dc) for dc in range(DC)])
        layer_norm(si, n, 2, pre=st2)
        if debug and ti == 0:
            c.dma(dbg["h2"][:, :, :], S[si][:, :, :], R=[("S", si, dc) for dc in range(DC)])
        ffn(ti, si, n, "f2", alt=True)
        layer_norm(si, n, 3, need_bf16=False,
                   defer=lambda: c.dma(yp[s, :, :, j * T:(j + 1) * T], S[si][:, :, :], R=[("S", si, dc) for dc in range(DC)]))

    tiles = [(s, j) for s in range(nseq) for j in range(NTL)]

    def load_x(ti):
        s, j = tiles[ti]
        c.dma(S[ti % 2][:, :, :], xp[s, :, :, j * T:(j + 1) * T], W=[("S", ti % 2, dc) for dc in range(DC)])

    load_x(0)
    for ti, (s, j) in enumerate(tiles):
        prompt_tile(ti, s, j)
    while ln_pending:
        ln_pending.pop(0)()

    if do_sample:
        c.barrier()
        pm.close()
        bufs["h1"] = c.sb("h1s", [128, FC, TS], BF16)
        qbs = [c.sb("qb%d" % i, [128, 4, 768], BF16) for i in range(2)]
        KVa = [c.sb("KVa0", [128, 1, 512], BF16), c.sb("KVa1", [128, 4, 512], BF16), c.sb("KVa2", [128, 4, 512], BF16)]
        prod = c.sb("prod", [128, 4, 256], BF16)
        PVt = c.sb("PVt", [128, 4, 260], BF16)
        prodB = c.sb("prodB", [128, 4, 256], F32)
        PVB = c.sb("PVB", [128, 4, 260], F32)
        sA = c.sb("sA", [128, 16], F32)
        biasA = c.sb("biasA", [128, 48], F32)
        biasB = c.sb("biasB", [128, 48], F32)
        KVn = c.sb("KVn", [128, 3, 512], F32)
        qB4 = c.sb("qB4", [128, 4, 768], F32)
        qtok = c.sb("qtok", [128, 768], F32)
        att = c.sb("att", [128, 256], F32)
        racc = c.sb("racc", [128, 4], F32)
        ohA = c.sb("ohA", [32, 12 * 128], F32)
        ohB = c.sb("ohB", [32, 12 * 64], F32)
        mkA = c.sb("mkA", [128, 48], F32)
        mkB = c.sb("mkB", [64, 48], F32)
        bts = c.sb("bts", [128, 1024], F32)
        bufs["bts"] = bts
        c.dma(bts[:], bts_d[:, :], W=["bts"])
        c.dma(ohA[:], ohA_d[:, :], W=["ohA"])
        c.dma(ohB[:], ohB_d[:, :], W=["ohB"])
        c.dma(mkA[:], mkA_d[:, :], W=["mkA"])
        c.dma(mkB[:], mkB_d[:, :], W=["mkB"])
        pA, kA = c.bank()
        c.mm([(lambda e, i=i: e.matmul(pA[:, 4 * i:4 * i + 4], lhsT=ohA[:, 128 * i:128 * (i + 1)], rhs=relb[:, 4 * (i // 4):4 * (i // 4) + 4],
                                       start=True, stop=True), ["ohA", "relb"]) for i in range(12)], W=[kA])
        c.op("dve", lambda e: e.tensor_tensor(out=biasA[:, :], in0=pA[:, 0:48], in1=mkA[:, :], op=ALU.add), R=[kA, "mkA"], W=["biasA"])
        pB, kB = c.bank()
        c.mm([(lambda e, i=i: e.matmul(pB[0:64, 4 * i:4 * i + 4], lhsT=ohB[:, 64 * i:64 * (i + 1)], rhs=relb[:, 4 * (i // 4):4 * (i // 4) + 4],
                                       start=True, stop=True), ["ohB", "relb"]) for i in range(12)], W=[kB])
        c.op("dve", lambda e: e.tensor_tensor(out=biasB[0:64, :], in0=pB[0:64, 0:48], in1=mkB[:, :], op=ALU.add), R=[kB, "mkB"], W=["biasB"])

        ti = len(tiles)
        si, n = 0, TS
        c.dma(S[0][:, :, :n], xs[:, :, :], W=[("S", 0, dc) for dc in range(DC)])
        cast_stream(si, n)
        st1 = ffn(ti, si, n, "f1")
        layer_norm(si, n, 1, pre=st1)
        pq = [c.bank(), c.bank()]
        for cc in range(6):
            p, k = pq[cc // 4]
            items = []
            for dc in range(DC):
                w, wk = W_(ti, ("wq", cc, dc))
                items.append((lambda e, w=w, dc=dc, cc=cc, p=p: e.matmul(p[0:n, (cc % 4) * 128:(cc % 4 + 1) * 128], lhsT=Sb[:, dc, 0:n], rhs=w,
                                                                       start=(dc == 0), stop=(dc == DC - 1)), [wk, ("Sb", dc)]))
            c.mm(items, W=[k])
        c.op("dve", lambda e: e.tensor_tensor(out=qtok[0:n, 0:512], in0=pq[0][0][0:n, :], in1=bts[0:n, BQ:BQ + 512], op=ALU.add), R=[pq[0][1], "bts"], W=["qtok"])
        c.op("dve", lambda e: e.tensor_tensor(out=qtok[0:n, 512:768], in0=pq[1][0][0:n, 0:256], in1=bts[0:n, BQ + 512:BQ + 768], op=ALU.add),
             R=[pq[1][1], "bts", "qtok"], W=["qtok"])
        c.dma(qsc_t.ap()[:, :], qtok[0:n, :], R=["qtok"], W=["qsc"])
        for cc in range(12, 16):
            def ev(p, k, cc=cc):
                c.op("act", lambda e: e.activation(out=uT[:, cc - 12, :n], in_=p[:, :n], func=AF.Gelu, bias=pp[:, PBU + cc - 12:PBU + cc - 11]),
                     R=[k, "pp"], W=[("uT", cc - 12)])
            proj_fm(ti, n, "wq", cc, ev)
        for cg in (3, 0, 1, 2):
            p, k = c.bank()
            items = []
            for dc in range(DC):
                w, wk = W_(ti, ("wtm", cg, dc, 0), 512)
                items.append((lambda e, w=w, dc=dc, p=p: e.matmul(p[0:n, :], lhsT=Sb[:, dc, 0:n], rhs=w, start=(dc == 0), stop=(dc == DC - 1)),
                              [wk, ("Sb", dc)]))
            c.mm(items, W=[k])
            if cg == 3:
                for st_ in sgu_rows(p, k, n, 0, True):
                    st_()
                continue
            g = cg
            c.op("dve", lambda e: e.tensor_tensor(out=KVn[0:n, g, :], in0=p[0:n, :], in1=bt[0:n, BKV + g * 512:BKV + (g + 1) * 512], op=ALU.add),
                 R=[k, "bt"], W=[("KVn", g)])
            for b in range(SB):
                c.dma(sw[g][b, WIN[g] - 4:WIN[g], :], KVn[4 * b:4 * b + 4, g, :], R=[("KVn", g)])
        acc, kacc = c.bank()
        first = [True]

        def att_block(m, Kap, Vap, qap, bias_ap, pv, sa, sel_fn, keysR, prod=prod):
            c.op("dve", lambda e: e.tensor_tensor(out=prod[:m], in0=Kap, in1=qap, op=ALU.mult), R=keysR, W=["prod"])
            c.op("dve", lambda e: e.tensor_reduce(out=sa[:m, :], in_=prod[:m].rearrange("p t (h e) -> p (t h) e", e=64), axis=mybir.AxisListType.X,
                                                  op=ALU.add), R=["prod"], W=["sA"])
            c.op("dve", lambda e: e.scalar_tensor_tensor(out=sa[:m, :], in0=sa[:m, :], scalar=SCALE, in1=bias_ap, op0=ALU.mult, op1=ALU.add),
                 R=["sA", "biasA", "biasB"], W=["sA"])
            c.op("act", lambda e: e.activation(out=pv[:m, :, 256:260], in_=sa[:m, :].rearrange("p (t h) -> p t h", t=4), func=AF.Exp),
                 R=["sA"], W=["PVp"])
            c.op("dve", lambda e: e.tensor_tensor(out=pv[:m, :, 0:256].rearrange("p t (h e) -> p t h e", e=64), in0=Vap,
                                                  in1=pv[:m, :, 256:260].unsqueeze(3).to_broadcast([m, 4, 4, 64]), op=ALU.mult),
                 R=keysR + ["PVp"], W=["PVv"])
            items = []
            for t in range(4):
                st = first[0]
                first[0] = False
                items.append((lambda e, t=t, st=st: e.matmul(acc[0:64, 0:260], lhsT=sel_fn(t), rhs=pv[:m, t, :], start=st, stop=False,
                                                             skip_group_check=True), ["PVp", "PVv", "cf"]))
            c.mm(items, W=[kacc])

        for b in range(SB):
            src = bass.AP(tensor=qsc_t, offset=b * 3072, ap=[[0, 128], [1, 3072]])
            qb = qbs[b % 2]
            c.dma(qb[:, :, :], src, R=["qsc"], W=[("qb", b % 2)])
            for g in range(3):
                nT = 1 if g == 0 else 4
                if g == 0:
                    c.dma(KVa[0][:, 0, :], st_d[0][b, :, :], W=[("KVa", 0)])
                else:
                    src = bass.AP(tensor=st_d[g].tensor, offset=b * WIN[g] * 512, ap=[[DIL[g] * 512, 128], [512, 4], [1, 512]])
                    c.dma(KVa[g][:, :, :], src, W=[("KVa", g)])
                if g == 0:
                    Kap = KVa[0][:, 0:1, 0:256].to_broadcast([128, 4, 256])
                    Vap = KVa[0][:, 0:1, 256:512].to_broadcast([128, 4, 256]).rearrange("p t (h e) -> p t h e", e=64)
                else:
                    Kap = KVa[g][:, :, 0:256]
                    Vap = KVa[g][:, :, 256:512].rearrange("p t (h e) -> p t h e", e=64)
                att_block(128, Kap, Vap, qb[:, :, g * 256:(g + 1) * 256], biasA[:, g * 16:(g + 1) * 16], PVt, sA,
                          lambda t, b=b: zselb[:, 63 - (4 * b + t):127 - (4 * b + t)], [("KVa", g), ("qb", b % 2)])
        for b in range(SB):
            src = bass.AP(tensor=qsc_t, offset=b * 3072, ap=[[0, 4], [1, 3072]])
            c.dma(qB4[4 * b:4 * b + 4, :, :], src, R=["qsc"], W=[("qB4", b)])
        for g in range(3):
            Kap = KVn[0:64, g:g + 1, 0:256].to_broadcast([64, 4, 256])
            Vap = KVn[0:64, g:g + 1, 256:512].to_broadcast([64, 4, 256]).rearrange("p t (h e) -> p t h e", e=64)
            att_block(64, Kap, Vap, qB4[0:64, :, g * 256:(g + 1) * 256], biasB[0:64, g * 16:(g + 1) * 16], PVB, sA,
                      lambda t: gsel[0:64, t * 64:(t + 1) * 64], [("KVn", g)] + [("qB4", b) for b in range(SB)], prod=prodB)
        c.op("dve", lambda e: e.reciprocal(out=racc[0:64, :], in_=acc[0:64, 256:260]), R=[kacc], W=["racc"])
        c.op("dve", lambda e: e.tensor_tensor(out=att[0:64, :].rearrange("p (h e) -> p h e", e=64), in0=acc[0:64, 0:256].rearrange("p (h e) -> p h e", e=64),
                                              in1=racc[0:64, :].unsqueeze(2).to_broadcast([64, 4, 64]), op=ALU.mult), R=[kacc, "racc"], W=["att"])
        for hc in range(2):
            ptr, ktr = c.bank()
            c.mm([(lambda e, hc=hc, ptr=ptr: e.transpose(ptr[:, 0:64], att[0:64, hc * 128:(hc + 1) * 128], identf[0:64, 0:64]), ["att", "cf"])], W=[ktr])
            c.op("act", lambda e, hc=hc, ptr=ptr: e.copy(out=attb[:, hc, 0:64], in_=ptr[:, 0:64]), R=[ktr], W=[("attb", hc)])
        sgu_mix(n, True)
        st2 = mix_stage(ti, si, n)
        layer_norm(si, n, 2, pre=st2)
        st3 = ffn(ti, si, n, "f2")
        layer_norm(si, n, 3, need_bf16=False, pre=st3)
        c.dma(ys[:, :, :], S[si][:, :, :n], R=[("S", si, dc) for dc in range(DC)])

    c.finish()
    c.es.close()
    return nc, c


def _host_consts():
    cf = np.zeros((128, 128 + 128 + 64 + 127 + 256), np.float32)
    cf[:, 0:128] = np.eye(128, dtype=np.float32)
    cf[:, 128:256] = np.triu(np.ones((128, 128), np.float32))
    m = np.zeros((64, 64), np.float32)
    for b in range(16):
        m[4 * b:4 * b + 4, 4 * b:4 * b + 4] = np.triu(np.ones((4, 4), np.float32))
    cf[0:64, 256:320] = m
    cf[:, 320 + 63] = 1.0
    gs = np.zeros((64, 4, 64), np.float32)
    for b in range(16):
        for t in range(4):
            gs[4 * b:4 * b + 4, t, 4 * b + t] = 1.0
    cf[0:64, 447:703] = gs.reshape(64, 256)
    cb = np.zeros((128, 447), np.float32)
    cb[:, 320 + 63] = 1.0
    cb[:, 0:128] = 1.0 / 1024.0
    cb[:, 128:192] = 1.0
    cb[:, 192:320] = np.eye(128, dtype=np.float32)[::-1]
    rel = np.arange(383) - 127
    oh = np.zeros((32, 3, 383), np.float32)
    mka = np.zeros((4, 3, 383), np.float32)
    for g in range(3):
        bk = _t5_bucket(np.clip(rel, 0, 128) * DIL[g])
        oh[bk, g, np.arange(383)] = 1.0
        mka[:, g, :] = np.where((rel >= 0) & (rel <= 128), 0.0, NEGB)[None]
    return cf, cb.astype(ml_dtypes.bfloat16), oh.reshape(32, 3 * 383), mka.reshape(4, 3 * 383)


def _fm(vec, n):
    return np.ascontiguousarray(np.asarray(vec, np.float32).reshape(n, 128).T)


def kernel(x_prompt, x_sample, state_win0, state_win1, state_win2, rel_bias,
           ln1_g, ln1_b, f1_w1, f1_w3, f1_w2, w_in, b_in,
           sgu_ln_g, sgu_ln_b, sgu_ws, sgu_b, w_oa, w_ob, w_out,
           ln2_g, ln2_b, f2_w1, f2_w3, f2_w2, ln3_g, ln3_b, _nseq=NSEQ, _ncores=NCORES, _do_sample=True, _debug=False):
    f32 = np.float32
    Wd = {"f1_w1": f1_w1[0], "f1_w3": f1_w3[0], "f1_w2": f1_w2[0], "w_in": w_in[0], "w_oa": w_oa[0], "w_ob": w_ob[0],
          "w_out": w_out[0], "f2_w1": f2_w1[0], "f2_w3": f2_w3[0], "f2_w2": f2_w2[0]}
    wst = np.zeros((128, NCH * CHW), f32)
    for i, blk in enumerate(BLOCKS):
        if blk is not None:
            name, r0, c0 = blk
            wst[:, i * 128:(i + 1) * 128] = Wd[name][r0:r0 + 128, c0:c0 + 128]
    bi = np.asarray(b_in[0], f32)
    pp = np.zeros((128, 80), f32)
    for i, v in enumerate((ln1_g, ln1_b, ln2_g, ln2_b, ln3_g, ln3_b)):
        pp[:, 8 * i:8 * i + 8] = _fm(v[0], 8)
    pp[:, 48:54] = _fm(bi[OQ:OQ + 768], 6)
    pp[:, 54:60] = _fm(bi[OK_:OK_ + 768], 6)
    pp[:, 60:64] = _fm(bi[OU:OU + 512], 4)
    pp[:, 64:72] = _fm(bi[OGA:OGA + 1024], 8)
    pp[:, 72:80] = _fm(bi[OGB:OGB + 1024], 8)
    rows = []
    for g in range(3):
        rows += [bi[OK_ + g * 256:OK_ + (g + 1) * 256], bi[OV + g * 256:OV + (g + 1) * 256]]
    rows += [bi[OVV:OVV + 512], sgu_ln_g[0], sgu_ln_b[0], np.asarray(sgu_b[0], f32).reshape(512)]
    btrow = np.concatenate([np.asarray(r, f32).reshape(-1) for r in rows])
    bt = np.ascontiguousarray(np.broadcast_to(btrow[None, :], (128, btrow.size)))
    btsrow = np.concatenate([bi[OQ:OQ + 768], np.tile(np.asarray(sgu_b[0], f32)[:, None, 0:4], (1, 16, 1)).reshape(256)])
    bts = np.ascontiguousarray(np.broadcast_to(btsrow[None, :], (128, 1024)))
    wsT = np.ascontiguousarray(np.transpose(np.asarray(sgu_ws[0], f32), (2, 0, 1)))
    wsTs = np.zeros((128, 4, 64), f32)
    wsTs[0:64] = np.tile(np.transpose(np.asarray(sgu_ws[0], f32)[:, 0:4, 0:4], (2, 0, 1)), (16, 1, 16))
    cf, cb, oh, mka = _host_consts()
    ohA = np.zeros((32, 12, 128), f32)
    mkA = np.zeros((128, 3, 4, 4), f32)
    ohB = np.zeros((32, 12, 64), f32)
    mkB = np.zeros((64, 3, 4, 4), f32)
    pidx = np.arange(128)
    for g in range(3):
        for t in range(4):
            if g == 0:
                dist = 128 + t - pidx
                valid = dist <= 128
            else:
                dist = (128 - pidx) * DIL[g]
                valid = np.ones(128, bool)
            ohA[_t5_bucket(np.clip(dist, 0, 128 * DIL[g])), g * 4 + t, pidx] = 1.0
            mkA[:, g, t, :] = np.where(valid, 0.0, NEGB)[:, None]
            tp = np.arange(64) % 4
            dB = t - tp
            vB = (dB >= 0) if g == 0 else (dB == 0)
            ohB[_t5_bucket(np.clip(dB, 0, 3) * DIL[g]), g * 4 + t, np.arange(64)] = 1.0
            mkB[:, g, t, :] = np.where(vB, 0.0, NEGB)[:, None]
    common = {"wst": wst, "pp": pp, "bt": bt, "bts": bts, "wsT": wsT, "wsTs": wsTs, "relb": np.asarray(rel_bias, f32), "cf": cf, "cb": cb,
              "oh": oh, "mka": mka,
              "ohA": ohA.reshape(32, 12 * 128), "mkA": mkA.reshape(128, 48), "ohB": ohB.reshape(32, 12 * 64), "mkB": mkB.reshape(64, 48)}
    xpT = np.asarray(x_prompt, f32).reshape(-1, SEQ, DC, 128)
    xsT = np.asarray(x_sample, f32).reshape(-1, 4, DC, 128)
    sts = [np.asarray(a, f32)[0].reshape(a.shape[1], a.shape[2], 512) for a in (state_win0, state_win1, state_win2)]
    in_maps = []
    for cid in range(_ncores):
        m = dict(common)
        m["xp"] = np.ascontiguousarray(np.transpose(xpT[cid * _nseq:(cid + 1) * _nseq], (0, 3, 2, 1)))
        m["xs"] = np.ascontiguousarray(np.transpose(xsT[cid * SB:(cid + 1) * SB].reshape(TS, DC, 128), (2, 1, 0)))
        for g in range(3):
            m["st%d" % g] = np.ascontiguousarray(sts[g][cid * SB:(cid + 1) * SB])
        in_maps.append(m)
    nc, _ = build(_nseq, _do_sample, _debug)
    res = run_bass_kernel_spmd(nc, in_maps, core_ids=list(range(_ncores)))
    R = res.results
    yp = np.concatenate([np.transpose(r["yp"], (0, 3, 2, 1)).reshape(_nseq, SEQ, D) for r in R], 0)
    ys = np.concatenate([np.transpose(r["ys"], (2, 1, 0)).reshape(SB, 4, D) for r in R], 0)
    outs = [yp, ys]
    for g in range(3):
        outs.append(np.concatenate([r["pw%d" % g].reshape(_nseq, WIN[g], 2, 4, 64) for r in R], 0)[None])
    for g in range(3):
        outs.append(np.concatenate([r["sw%d" % g].reshape(SB, WIN[g], 2, 4, 64) for r in R], 0)[None])
    outs.append(np.concatenate([r["sv"].reshape(SB, 4, 512) for r in R], 0)[None])
    return tuple(np.ascontiguousarray(o, dtype=np.float32) for o in outs)
```

```python
import math
from contextlib import ExitStack

import numpy as np
import ml_dtypes
import concourse.bass as bass
import concourse.mybir as mybir
from concourse.bass_utils import run_bass_kernel_spmd

F32 = mybir.dt.float32
BF16 = mybir.dt.bfloat16
AF = mybir.ActivationFunctionType
ALU = mybir.AluOpType

D = 1024
DC = 8
DFF = 2816
FC = 22
SEQ = 2048
NCORES = 8
NSEQ = 4
T = 512
NTL = SEQ // T
SB = 16
TS = 64
WIN = (128, 512, 2048)
DIL = (1, 4, 16)
ALPHA = 2.0 ** 0.25
LN_EPS = 1e-5
EPS2 = LN_EPS / (ALPHA * ALPHA)
SCALE = 0.125
NEGB = -30000.0
CH = 16
CHW = CH * 128
NSLOT = 6
NPOOL = 24

OQ, OK_, OV, OU, OVV, OGA, OGB = 0, 768, 1536, 2304, 2816, 3328, 4352


def _stream_plan():
    blocks = []
    idx = {}

    def add(key, name, r0, c0):
        idx[key] = len(blocks)
        blocks.append((name, r0, c0))

    for pre, (w1, w3, w2) in (("f1", ("f1_w1", "f1_w3", "f1_w2")),):
        for fc in range(FC):
            for dc in range(DC):
                add((pre + "w1", fc, dc), w1, dc * 128, fc * 128)
            for dc in range(DC):
                add((pre + "w3", fc, dc), w3, dc * 128, fc * 128)
        for oc in range(DC):
            for fc in range(FC):
                add((pre + "w2", oc, fc), w2, fc * 128, oc * 128)
    fm_cols = [OQ + i * 128 for i in range(6)] + [OK_ + i * 128 for i in range(6)] + [OU + i * 128 for i in range(4)]
    for cc, c0 in enumerate(fm_cols):
        for dc in range(DC):
            add(("wq", cc, dc), "w_in", dc * 128, c0)
    while len(blocks) % 32:
        blocks.append(None)
    for cg in (3, 0, 1, 2):
        if cg < 3:
            cols = [OK_ + cg * 256, OK_ + cg * 256 + 128, OV + cg * 256, OV + cg * 256 + 128]
        else:
            cols = [OVV + i * 128 for i in range(4)]
        for dc in range(DC):
            for i, c0 in enumerate(cols):
                add(("wtm", cg, dc, i), "w_in", dc * 128, c0)
    for oc in range(DC):
        for dc in range(DC):
            add(("ga", oc, dc), "w_in", dc * 128, OGA + oc * 128)
        for kc in range(2):
            add(("oa", oc, kc), "w_oa", kc * 128, oc * 128)
        for dc in range(DC):
            add(("gb", oc, dc), "w_in", dc * 128, OGB + oc * 128)
        for kc in range(4):
            add(("ob", oc, kc), "w_ob", kc * 128, oc * 128)
    for oc in range(DC):
        for kc in range(DC):
            add(("wo", oc, kc), "w_out", kc * 128, oc * 128)
    for pre, (w1, w3, w2) in (("f2", ("f2_w1", "f2_w3", "f2_w2")),):
        for fc in range(FC):
            for dc in range(DC):
                add((pre + "w1", fc, dc), w1, dc * 128, fc * 128)
            for dc in range(DC):
                add((pre + "w3", fc, dc), w3, dc * 128, fc * 128)
        for oc in range(DC):
            for fc in range(FC):
                add((pre + "w2", oc, fc), w2, fc * 128, oc * 128)
    return blocks, idx


BLOCKS, BIDX = _stream_plan()
NBLK = len(BLOCKS)
NCH = (NBLK + CH - 1) // CH


def _t5_bucket(dist):
    dist = np.asarray(dist, np.int64)
    max_exact = 16
    large = max_exact + (np.log(np.maximum(dist, max_exact) / max_exact) / np.log(2048 / max_exact) * 16).astype(np.int64)
    large = np.minimum(large, 31)
    return np.where(dist < max_exact, dist, large).astype(np.int64)


class _Res:
    __slots__ = ("w", "rd")

    def __init__(self):
        self.w = None
        self.rd = {}


class Ctx:
    def __init__(self, nc):
        self.nc = nc
        self.es = ExitStack()
        self.h = {"pe": nc.tensor, "act": nc.scalar, "dve": nc.vector, "pool": nc.gpsimd, "sp": nc.sync}
        self.sems = {}
        self.cnt = {}
        for name in self.h:
            self.sems[name] = self.es.enter_context(nc.semaphore("s_" + name))
            self.cnt[name] = 0
        self.pool = []
        for i in range(NPOOL):
            k = "d%d" % i
            self.sems[k] = self.es.enter_context(nc.semaphore("s_" + k))
            self.cnt[k] = 0
            self.pool.append(k)
        self.pool_i = 0
        self.res = {}
        self.seen = {name: {} for name in self.h}
        self.nins = {name: 0 for name in self.h}
        self.banks = []
        self.bank_rot = list(range(8))
        self.bank_i = 0

    def sb(self, name, shape, dt, es=None):
        return (es or self.es).enter_context(self.nc.sbuf_tensor("sb_" + name, shape, dt))

    def _wait(self, eng, tok):
        k, v = tok
        if self.seen[eng].get(k, 0) < v:
            self.h[eng].wait_ge(self.sems[k], v)
            self.seen[eng][k] = v

    def _deps(self, eng, R, W):
        need = {}

        def add(tok, raw):
            k, v = tok
            if k == eng and (not raw or eng == "pe"):
                return
            if need.get(k, 0) < v:
                need[k] = v

        for key in R:
            r = self.res.get(key)
            if r is not None and r.w is not None:
                add(r.w, True)
        for key in W:
            r = self.res.get(key)
            if r is not None:
                if r.w is not None:
                    add(r.w, False)
                for k, v in r.rd.items():
                    add((k, v), False)
        for k, v in need.items():
            self._wait(eng, (k, v))

    def _record(self, tok, R, W):
        k, v = tok
        for key in R:
            r = self.res.get(key)
            if r is None:
                r = self.res[key] = _Res()
            if r.rd.get(k, 0) < v:
                r.rd[k] = v
        for key in W:
            r = self.res.get(key)
            if r is None:
                r = self.res[key] = _Res()
            r.w = tok
            r.rd = {}

    def op(self, eng, fn, R=(), W=()):
        self._deps(eng, R, W)
        ins = fn(self.h[eng])
        self.cnt[eng] += 1
        ins.then_inc(self.sems[eng], 1)
        tok = (eng, self.cnt[eng])
        self._record(tok, R, W)
        self.nins[eng] += 1
        return tok

    def mm(self, items, W=()):
        allR = []
        ins = None
        first = True
        for fn, R in items:
            self._deps("pe", R, W if first else ())
            first = False
            ins = fn(self.h["pe"])
            self.nins["pe"] += 1
            for k in R:
                if k not in allR:
                    allR.append(k)
        self.cnt["pe"] += 1
        ins.then_inc(self.sems["pe"], 1)
        tok = ("pe", self.cnt["pe"])
        self._record(tok, allR, W)
        return tok

    def dma(self, out, in_, R=(), W=(), q="pool"):
        ch = self.pool[self.pool_i]
        self.pool_i = (self.pool_i + 1) % NPOOL
        if self.cnt[ch] > 0:
            self._wait(q, (ch, self.cnt[ch]))
        self._deps(q, R, W)
        ins = self.h[q].dma_start(out=out, in_=in_)
        self.cnt[ch] += 16
        ins.then_inc(self.sems[ch], 16)
        tok = (ch, self.cnt[ch])
        self._record(tok, R, W)
        self.nins[q] += 1
        return tok

    def barrier(self):
        for e in self.h:
            for k in list(self.cnt):
                if k != e and self.cnt[k] > 0:
                    self._wait(e, (k, self.cnt[k]))

    def finish(self, q="sp"):
        for k in self.cnt:
            if k != q and self.cnt[k] > 0:
                self._wait(q, (k, self.cnt[k]))

    def bank(self):
        i = self.bank_rot[self.bank_i % len(self.bank_rot)]
        self.bank_i += 1
        return self.banks[i], ("pb", i)


def build(nseq=NSEQ, do_sample=True, debug=False):
    nc = bass.Bass("TRN2", target_bir_lowering=False)
    c = Ctx(nc)
    dbg = {}
    if debug:
        for nm in ("r1", "h", "r2", "h2"):
            dbg[nm] = nc.dram_tensor("dbg_" + nm, [128, DC, T], F32, kind="ExternalOutput").ap()

    def din(name, shape, dt=F32):
        return nc.dram_tensor(name, list(shape), dt, kind="ExternalInput").ap()

    def dout(name, shape):
        return nc.dram_tensor(name, list(shape), F32, kind="ExternalOutput").ap()

    xp = din("xp", [nseq, 128, DC, SEQ])
    xs = din("xs", [128, DC, TS])
    wst = din("wst", [128, NCH * CHW])
    pp_d = din("pp", [128, 80])
    NBT = 2048 + 512 + 512 + 512
    bt_d = din("bt", [128, NBT])
    bts_d = din("bts", [128, 1024])
    wsT_d = din("wsT", [128, 4, 128])
    wsTs_d = din("wsTs", [128, 4, 64])
    relb_d = din("relb", [32, 12])
    cf_d = din("cf", [128, 128 + 128 + 64 + 127 + 256])
    cb_d = din("cb", [128, 128 + 64 + 128 + 127], BF16)
    oh_d = din("oh", [32, 3 * 383])
    mka_d = din("mka", [4, 3 * 383])
    ohA_d = din("ohA", [32, 12 * 128])
    mkA_d = din("mkA", [128, 48])
    ohB_d = din("ohB", [32, 12 * 64])
    mkB_d = din("mkB", [64, 48])
    st_d = [din("st%d" % g, [SB, WIN[g], 512]) for g in range(3)]

    yp = dout("yp", [nseq, 128, DC, SEQ])
    ys = dout("ys", [128, DC, TS])
    pw = [dout("pw%d" % g, [nseq, WIN[g], 512]) for g in range(3)]
    sw = [dout("sw%d" % g, [SB, WIN[g], 512]) for g in range(3)]
    sv = dout("sv", [TS, 512])

    wbf_t = nc.dram_tensor("wbf", [NCH, 128, CHW], BF16, kind="Internal")
    wbf = wbf_t.ap()
    ebs_t = nc.dram_tensor("ebs", [12, 383], F32, kind="Internal")
    qsc_t = nc.dram_tensor("qsc", [TS, 768], F32, kind="Internal")

    for i in range(8):
        c.banks.append(c.es.enter_context(nc.psum_tensor("pb%d" % i, [128, 512], F32)))
    ws = [c.sb("ws%d" % i, [128, CHW], BF16) for i in range(NSLOT)]
    pp = c.sb("pp", [128, 80], F32)
    bt = c.sb("bt", [128, NBT], F32)
    cf = c.sb("cf", [128, 128 + 128 + 64 + 127 + 256], F32)
    cb = c.sb("cb", [128, 128 + 64 + 128 + 127], BF16)
    EB = c.sb("EB", [128, 12, 256], BF16)
    wm = c.sb("wm", [128, 4, 128], BF16)
    wms = c.sb("wms", [128, 4, 64], BF16)
    relb = c.sb("relb", [32, 12], F32)
    epsc = c.sb("epsc", [128, 2], F32)
    dmy = c.sb("dmy", [128, 2], F32)

    identf = cf[:, 0:128]
    triu = cf[:, 128:256]
    masks = cf[:, 256:320]
    zsel = cf[:, 320:447]
    gsel = cf[:, 447:703]
    onesb = cb[:, 0:128]
    ones64 = cb[:, 128:192]
    Jb = cb[:, 192:320]
    zselb = cb[:, 320:447]
    PLN = {1: (0, 8), 2: (16, 24), 3: (32, 40)}
    PBQ, PBK, PBU, PBGA, PBGB = 48, 54, 60, 64, 72
    BKV, BSG, BSB, BSGB = 0, 2048, 2560, 3072
    BQ, BSGBS = 0, 768

    c.dma(pp[:], pp_d[:, :], W=["pp"])
    c.dma(bt[:], bt_d[:, :], W=["bt"])
    c.dma(cf[:], cf_d[:, :], W=["cf"])
    c.dma(cb[:], cb_d[:, :], W=["cb"])
    c.dma(relb[:], relb_d[:, :], W=["relb"])
    c.op("dve", lambda e: e.memset(epsc[:, 0:1], EPS2), W=["epsc"])
    c.op("dve", lambda e: e.memset(epsc[:, 1:2], LN_EPS), R=["epsc"], W=["epsc"])

    sw_pending = []
    if do_sample:
        for g in range(3):
            n = (WIN[g] - 4) * 512
            for b in range(SB):
                src = bass.AP(tensor=st_d[g].tensor, offset=b * WIN[g] * 512 + 4 * 512, ap=[[n // 16, 16], [1, n // 16]])
                dst = bass.AP(tensor=sw[g].tensor, offset=b * WIN[g] * 512, ap=[[n // 16, 16], [1, n // 16]])
                sw_pending.append(lambda dst=dst, src=src, g=g, b=b: c.dma(dst, src, W=[("swcopy", g, b)]))

    with ExitStack() as p0:
        oh = c.sb("oh", [32, 3 * 383], F32, p0)
        mka = c.sb("mka", [4, 3 * 383], F32, p0)
        ext = c.sb("ext", [4, 3 * 383], F32, p0)
        hank = c.sb("hank", [128, 12, 256], F32, p0)
        hankb = c.sb("hankb", [128, 12, 256], BF16, p0)
        wsT = c.sb("wsT", [128, 4, 128], F32, p0)
        wsTs = c.sb("wsTs", [128, 4, 64], F32, p0)
        c.dma(oh[:], oh_d[:, :], W=["oh"])
        c.dma(mka[:], mka_d[:, :], W=["mka"])
        c.dma(wsT[:], wsT_d[:, :, :], W=["wsT"])
        c.dma(wsTs[:], wsTs_d[:, :, :], W=["wsTs"])
        for g in range(3):
            pb_, pk = c.bank()
            c.mm([(lambda e, g=g, pb_=pb_: e.matmul(pb_[0:4, 0:383], lhsT=relb[:, 4 * g:4 * g + 4], rhs=oh[:, g * 383:(g + 1) * 383],
                                                  start=True, stop=True), ["relb", "oh"])], W=[pk])
            c.op("dve", lambda e, g=g, pb_=pb_: e.tensor_tensor(out=ext[:, g * 383:(g + 1) * 383], in0=pb_[0:4, 0:383],
                                                             in1=mka[:, g * 383:(g + 1) * 383], op=ALU.add), R=[pk, "mka"], W=[("ext", g)])
            c.op("act", lambda e, g=g: e.activation(out=ext[:, g * 383:(g + 1) * 383], in_=ext[:, g * 383:(g + 1) * 383], func=AF.Exp),
                 R=[("ext", g)], W=[("ext", g)])
            c.dma(ebs_t.ap()[4 * g:4 * g + 4, :], ext[:, g * 383:(g + 1) * 383], R=[("ext", g)], W=[("ebs", g)])
        src = bass.AP(tensor=ebs_t, offset=0, ap=[[1, 128], [383, 12], [1, 256]])
        c.dma(hank[:, :, :], src, R=[("ebs", g) for g in range(3)], W=["hank"])
        c.op("dve", lambda e: e.tensor_copy(out=hankb[:], in_=hank[:]), R=["hank"], W=["hankb"])
        for hh in range(12):
            pb_, pk = c.bank()
            c.mm([(lambda e, hh=hh, pb_=pb_: e.matmul(pb_[:, 0:256], lhsT=Jb, rhs=hankb[:, hh, :], start=True, stop=True), ["cb", "hankb"])], W=[pk])
            c.op("act", lambda e, hh=hh, pb_=pb_: e.copy(out=EB[:, hh, :], in_=pb_[:, 0:256]), R=[pk], W=["EB"])
        c.op("dve", lambda e: e.tensor_tensor(out=wm[:], in0=wsT[:], in1=triu.unsqueeze(1).to_broadcast([128, 4, 128]), op=ALU.mult),
             R=["wsT", "cf"], W=["wm"])
        c.op("dve", lambda e: e.tensor_tensor(out=wms[0:64], in0=wsTs[0:64], in1=masks[0:64].unsqueeze(1).to_broadcast([64, 4, 64]), op=ALU.mult),
             R=["wsTs", "cf"], W=["wms"])
        c.barrier()

    S = [c.sb("S0", [128, DC, T], F32), None]
    Sb = c.sb("Sb", [128, DC, T], BF16)
    t8 = c.sb("t8", [128, DC, T], BF16)
    uT = c.sb("uT", [128, 4, T], BF16)
    vn = c.sb("vn", [128, 4, 512], BF16)
    vt = [c.sb("vt%d" % i, [128, 512], F32) for i in range(4)]
    attb = c.sb("attb", [128, 2, T], BF16)
    rL = c.sb("rL", [128, T], F32)
    yb = c.sb("yb", [128, 4, T], BF16)
    rstd = c.sb("rstd", [128, T], F32)
    sil = [c.sb("sil%d" % i, [128, T], F32) for i in range(2)]
    sg = [c.sb("sg%d" % i, [128, T], F32) for i in range(2)]
    m1 = [c.sb("m1_0", [128, T], F32)]
    mean_s = m1[0]
    st6 = c.sb("st6", [128, 4, 6], F32)
    mv = c.sb("mv", [128, 4, 4], F32)
    ln_pending = []
    sgu_pending = []
    pm = ExitStack()
    S[1] = c.sb("S1", [128, DC, T], F32, pm)
    bufs = {"h1": c.sb("h1", [128, FC, T], BF16, pm)}
    qT = c.sb("qT", [128, 6, T], BF16, pm)
    kT01 = [c.sb("kT%d" % g, [128, 2, 2, T], BF16, pm) for g in range(2)]
    kT2 = c.sb("kT2", [128, 2, SEQ], BF16, pm)
    Vc01 = [c.sb("Vc%d" % g, [128, 2, 4, 256], BF16, pm) for g in range(2)]
    Vc2 = c.sb("Vc2", [128, 16, 256], BF16, pm)
    kvst = [c.sb("kvst%d" % i, [128, 512], F32, pm) for i in range(4)]
    tmpE = [c.sb("tmpE%d" % i, [128, 512], F32, pm) for i in range(2)]
    PT = [c.sb("PT%d" % i, [128, 512], BF16, pm) for i in range(2)]

    cnt = {"kv": 0, "vt": 0, "te": 0, "sil": 0, "sg": 0}

    wstate = {"issued": 0}
    total_tiles = nseq * NTL + (1 if do_sample else 0)

    def ws_ensure(G):
        lim = min(G + NSLOT - 3, total_tiles * NCH - 1)
        while wstate["issued"] <= lim:
            g_ = wstate["issued"]
            if g_ < NCH:
                c.dma(ws[g_ % NSLOT][:], wst[:, g_ * CHW:(g_ + 1) * CHW], W=[("ws", g_ % NSLOT)], q="pool")
                c.dma(wbf[g_], ws[g_ % NSLOT][:], R=[("ws", g_ % NSLOT)], W=[("wbf", g_)], q="sp")
            else:
                c.dma(ws[g_ % NSLOT][:], wbf[g_ % NCH], R=[("wbf", g_ % NCH)], W=[("ws", g_ % NSLOT)], q="sp")
            wstate["issued"] += 1

    def W_(ti, key, width=128):
        b = BIDX[key]
        G = ti * NCH + b // CH
        ws_ensure(G)
        off = (b % CH) * 128
        slot = G % NSLOT
        return ws[slot][:, off:off + width], ("ws", slot)

    def cast_stream(si, n, dcs=None):
        for dc in range(DC):
            if dc % 2 == 0:
                c.op("act", lambda e, dc=dc: e.copy(out=Sb[:, dc, :n], in_=S[si][:, dc, :n]), R=[("S", si, dc)], W=[("Sb", dc)])
            else:
                c.op("dve", lambda e, dc=dc: e.tensor_copy(out=Sb[:, dc, :n], in_=S[si][:, dc, :n]), R=[("S", si, dc)], W=[("Sb", dc)])

    def stat_begin():
        pm, km = c.bank()
        pe2, ke = c.bank()
        c.bank_rot = [b for b in range(8) if ("pb", b) not in (km, ke)]
        return {"pm": pm, "km": km, "pe2": pe2, "ke": ke}

    def stat_add(st, dc, n, sq_x=False):
        sq, sqk = xbuf(dc, n) if sq_x else (t8[:, dc, :n], ("t8", dc))
        c.mm([(lambda e: e.matmul(st["pm"][:, :n], lhsT=onesb, rhs=Sb[:, dc, :n], start=(dc == 0), stop=(dc == DC - 1)), ["cb", ("Sb", dc)])],
             W=[st["km"]])
        c.mm([(lambda e: e.matmul(st["pe2"][:, :n], lhsT=onesb, rhs=sq, start=(dc == 0), stop=(dc == DC - 1)), ["cb", sqk])],
             W=[st["ke"]])

    def preload(func):
        c.op("act", lambda e: e.activation(out=dmy[:, 0:1], in_=epsc[:, 0:1], func=func), R=["epsc"], W=["dmy"])

    def ffn(ti, si, n, pre, alt=False):
        for fc in range(FC):
            pa, ka = c.bank()
            pb_, kb = c.bank()
            ia, ib = [], []
            for dc in range(DC):
                wa, wk = W_(ti, (pre + "w1", fc, dc))
                ia.append((lambda e, wa=wa, dc=dc, pa=pa: e.matmul(pa[:, :n], lhsT=wa, rhs=Sb[:, dc, :n], start=(dc == 0), stop=(dc == DC - 1)),
                           [wk, ("Sb", dc)]))
            c.mm(ia, W=[ka])
            for dc in range(DC):
                wb, wk = W_(ti, (pre + "w3", fc, dc))
                ib.append((lambda e, wb=wb, dc=dc, pb_=pb_: e.matmul(pb_[:, :n], lhsT=wb, rhs=Sb[:, dc, :n], start=(dc == 0), stop=(dc == DC - 1)),
                           [wk, ("Sb", dc)]))
            c.mm(ib, W=[kb])
            sb_i = cnt["sil"] % 2
            cnt["sil"] += 1
            c.op("act", lambda e, pa=pa, sb_i=sb_i: e.activation(out=sil[sb_i][:, :n], in_=pa[:, :n], func=AF.Silu), R=[ka], W=[("sil", sb_i)])
            c.op("dve", lambda e, pb_=pb_, sb_i=sb_i, fc=fc: e.tensor_tensor(out=bufs['h1'][:, fc, :n], in0=pb_[:, :n], in1=sil[sb_i][:, :n], op=ALU.mult),
                 R=[kb, ("sil", sb_i)], W=[("h1", fc)])
            if ln_pending:
                ln_pending.pop(0)()
        while ln_pending:
            ln_pending.pop(0)()
        preload(AF.Ln)
        stt = None if alt else stat_begin()
        for oc in range(DC):
            po, ko = c.bank()
            items = []
            for fc in range(FC):
                w2, wk = W_(ti, (pre + "w2", oc, fc))
                items.append((lambda e, w2=w2, fc=fc, po=po: e.matmul(po[:, :n], lhsT=w2, rhs=bufs['h1'][:, fc, :n], start=(fc == 0), stop=(fc == FC - 1)),
                              [wk, ("h1", fc)]))
            c.mm(items, W=[ko])
            if stt is not None and oc >= 1:
                stat_add(stt, oc - 1, n)
            resid(si, n, oc, po, ko, 0.5 / ALPHA, alt)
        if stt is not None:
            stat_add(stt, DC - 1, n)
        return stt

    def xbuf(dc, n):
        return (uT[:, dc, :n], ("uT", dc)) if dc < 4 else (yb[:, dc - 4, :n], ("yb", dc - 4))

    def resid(si, n, oc, po, ko, coef, alt=False, sq_x=False):
        c.op("dve", lambda e: e.scalar_tensor_tensor(out=S[si][:, oc, :n], in0=po[:, :n], scalar=coef, in1=S[si][:, oc, :n],
                                                      op0=ALU.mult, op1=ALU.add), R=[ko, ("S", si, oc)], W=[("S", si, oc)])
        dst, dkey = xbuf(oc, n) if alt else (Sb[:, oc, :n], ("Sb", oc))
        c.op("act", lambda e: e.copy(out=dst, in_=S[si][:, oc, :n]), R=[("S", si, oc)], W=[dkey])
        sq, sqk = xbuf(oc, n) if sq_x else (t8[:, oc, :n], ("t8", oc))
        c.op("act", lambda e: e.activation(out=sq, in_=S[si][:, oc, :n], func=AF.Square), R=[("S", si, oc)], W=[sqk])

    def layer_norm(si, n, which, need_bf16=True, defer=None, pre=None):
        g0, b0 = PLN[which]
        npool = 0
        st = {}

        def stats():
            if pre is not None:
                pm, km, pe2, ke = pre["pm"], pre["km"], pre["pe2"], pre["ke"]
                st["pm"], st["km"] = pm, km
            else:
                pm, km = c.bank()
                pe2, ke = c.bank()
                st["pm"], st["km"] = pm, km
                srcs = [xbuf(dc, n) if defer is not None else (Sb[:, dc, :n], ("Sb", dc)) for dc in range(DC)]
                c.mm([(lambda e, dc=dc: e.matmul(pm[:, :n], lhsT=onesb, rhs=srcs[dc][0], start=(dc == 0), stop=(dc == DC - 1)), ["cb", srcs[dc][1]])
                      for dc in range(DC)], W=[km])
                c.mm([(lambda e, dc=dc: e.matmul(pe2[:, :n], lhsT=onesb, rhs=t8[:, dc, :n], start=(dc == 0), stop=(dc == DC - 1)), ["cb", ("t8", dc)])
                      for dc in range(DC)], W=[ke])
            c.op("act", lambda e: e.activation(out=rstd[:, :n], in_=pm[:, :n], func=AF.Square), R=[km], W=["rstd"])
            if defer is not None or npool:
                c.op("act", lambda e: e.copy(out=mean_s[:, :n], in_=pm[:, :n]), R=[km], W=[("m1", 0)])
            c.op("dve", lambda e: e.tensor_tensor(out=rstd[:, :n], in0=pe2[:, :n], in1=rstd[:, :n], op=ALU.subtract), R=[ke, "rstd"], W=["rstd"])
            c.op("act", lambda e: e.activation(out=rstd[:, :n], in_=rstd[:, :n], func=AF.Ln, bias=epsc[:, 0:1]), R=["rstd", "epsc"], W=["rstd"])
            c.op("act", lambda e: e.activation(out=rstd[:, :n], in_=rstd[:, :n], func=AF.Exp, scale=-0.5), R=["rstd"], W=["rstd"])

        def csub(dc, eng="dve"):
            if defer is not None or eng == "pool":
                c.op(eng, lambda e: e.tensor_tensor(out=S[si][:, dc, :n], in0=S[si][:, dc, :n], in1=mean_s[:, :n], op=ALU.subtract),
                     R=[("S", si, dc), ("m1", 0)], W=[("S", si, dc)])
            else:
                c.op(eng, lambda e: e.tensor_tensor(out=S[si][:, dc, :n], in0=S[si][:, dc, :n], in1=st["pm"][:, :n], op=ALU.subtract),
                     R=[("S", si, dc), st["km"]], W=[("S", si, dc)])

        def cmul(dc, eng="dve"):
            c.op(eng, lambda e: e.tensor_tensor(out=S[si][:, dc, :n], in0=S[si][:, dc, :n], in1=rstd[:, :n], op=ALU.mult),
                 R=[("S", si, dc), "rstd"], W=[("S", si, dc)])

        def chunk(dc, eng="dve"):
            csub(dc, eng)
            cmul(dc, eng)

        def tobf(dc):
            c.op("act", lambda e: e.activation(out=Sb[:, dc, :n], in_=S[si][:, dc, :n], func=AF.Identity,
                                               scale=pp[:, g0 + dc:g0 + dc + 1], bias=pp[:, b0 + dc:b0 + dc + 1]),
                 R=[("S", si, dc), "pp"], W=[("Sb", dc)])

        def affine(dc):
            c.op("act", lambda e: e.activation(out=S[si][:, dc, :n], in_=S[si][:, dc, :n], func=AF.Identity,
                                               scale=pp[:, g0 + dc:g0 + dc + 1], bias=pp[:, b0 + dc:b0 + dc + 1]),
                 R=[("S", si, dc), "pp"], W=[("S", si, dc)])

        if defer is None:
            stats()
            for dc in range(4):
                csub(dc)
            for dc in range(DC):
                cmul(dc)
                if need_bf16:
                    tobf(dc)
                if dc + 4 < DC:
                    csub(dc + 4)
            for dc in range(DC):
                affine(dc)
            if pre is not None:
                c.bank_rot = list(range(8))
        else:
            ln_pending.append(stats)
            for dc in range(DC):
                ln_pending.append(lambda dc=dc: (chunk(dc), affine(dc)))
            ln_pending.append(defer)

    def proj_fm(ti, n, key, idx, evac):
        p, k = c.bank()
        items = []
        for dc in range(DC):
            w, wk = W_(ti, (key, idx, dc))
            items.append((lambda e, w=w, dc=dc: e.matmul(p[:, :n], lhsT=w, rhs=Sb[:, dc, :n], start=(dc == 0), stop=(dc == DC - 1)), [wk, ("Sb", dc)]))
        c.mm(items, W=[k])
        evac(p, k)

    def sgu_rows(p, k, m, tb, sample):
        vi = tb
        v_ = vt[vi]
        c.op("dve", lambda e: e.tensor_tensor(out=v_[:m], in0=p[:m, :], in1=bt[:m, BKV + 1536:BKV + 2048], op=ALU.add), R=[k, "bt"], W=[("vt", vi)])
        c.op("act", lambda e: e.activation(out=v_[:m], in_=v_[:m], func=AF.Gelu), R=[("vt", vi)], W=[("vt", vi)])
        steps = [
            lambda: c.op("dve", lambda e: e.bn_stats(out=st6[:m, tb, :], in_=v_[:m]), R=[("vt", vi)], W=[("st6", tb)]),
            lambda: c.op("dve", lambda e: e.bn_aggr(out=mv[:m, tb, 0:2], in_=st6[:m, tb, :]), R=[("st6", tb)], W=[("mv", tb)]),
            lambda: c.op("act", lambda e: e.activation(out=mv[:m, tb, 2:3], in_=mv[:m, tb, 1:2], func=AF.Ln, bias=epsc[:m, 1:2]),
                         R=[("mv", tb), "epsc"], W=[("mv2", tb)]),
            lambda: c.op("act", lambda e: e.activation(out=mv[:m, tb, 3:4], in_=mv[:m, tb, 2:3], func=AF.Exp, scale=-0.5), R=[("mv2", tb)], W=[("mv3", tb)]),
            lambda: c.op("dve", lambda e: e.tensor_scalar(out=v_[:m], in0=v_[:m], scalar1=mv[:m, tb, 0:1], scalar2=mv[:m, tb, 3:4],
                                                          op0=ALU.subtract, op1=ALU.mult), R=[("vt", vi), ("mv", tb), ("mv3", tb)], W=[("vt", vi)]),
            lambda: c.op("dve", lambda e: e.tensor_tensor(out=v_[:m], in0=v_[:m], in1=bt[:m, BSG:BSG + 512], op=ALU.mult), R=[("vt", vi), "bt"], W=[("vt", vi)]),
            lambda: c.op("dve", lambda e: e.tensor_tensor(out=v_[:m], in0=v_[:m], in1=bt[:m, BSB:BSB + 512], op=ALU.add), R=[("vt", vi), "bt"], W=[("vt", vi)]),
            lambda: c.op("act", lambda e: e.copy(out=vn[:m, tb, :], in_=v_[:m]), R=[("vt", vi)], W=[("vn", tb)]),
        ]
        if sample:
            steps.append(lambda: c.dma(sv[:, :], v_[:m], R=[("vt", vi)]))
        return steps

    def sgu_mix(n, sample):
        for gg in range(4):
            p, k = c.bank()
            items = []
            if not sample:
                for tb in range(4):
                    items.append((lambda e, tb=tb: e.matmul(p[:, tb * 128:(tb + 1) * 128], lhsT=vn[:, tb, gg * 128:(gg + 1) * 128], rhs=wm[:, gg, :],
                                                            start=True, stop=True), [("vn", tb), "wm"]))
            else:
                items.append((lambda e: e.matmul(p[:, 0:64], lhsT=vn[0:64, 0, gg * 128:(gg + 1) * 128], rhs=wms[0:64, gg, :], start=True, stop=True),
                              [("vn", 0), "wms"]))
            c.mm(items, W=[k])
            vi = cnt["vt"] % 4
            cnt["vt"] += 1
            v_ = vt[vi]
            if not sample:
                bia = bt[:, BSGB + gg * 128:BSGB + (gg + 1) * 128].unsqueeze(1).to_broadcast([128, 4, 128])
                c.op("dve", lambda e: e.tensor_tensor(out=v_[:, :].rearrange("p (a b) -> p a b", a=4), in0=p[:, :].rearrange("p (a b) -> p a b", a=4),
                                                      in1=bia, op=ALU.add), R=[k, "bt"], W=[("vt", vi)])
            else:
                c.op("dve", lambda e: e.tensor_tensor(out=v_[:, :n], in0=p[:, :n], in1=bufs["bts"][:, BSGBS + gg * 64:BSGBS + (gg + 1) * 64], op=ALU.add),
                     R=[k, "bts"], W=[("vt", vi)])
            c.op("dve", lambda e: e.tensor_tensor(out=yb[:, gg, :n], in0=v_[:, :n], in1=uT[:, gg, :n], op=ALU.mult),
                 R=[("vt", vi), ("uT", gg)], W=[("yb", gg)])

    def mix_stage(ti, si, n):
        for oc in range(DC):
            def ev_ga(p, k, oc=oc):
                i = cnt["sg"] % 2
                cnt["sg"] += 1
                c.op("act", lambda e: e.activation(out=sg[i][:, :n], in_=p[:, :n], func=AF.Sigmoid, bias=pp[:, PBGA + oc:PBGA + oc + 1]),
                     R=[k, "pp"], W=[("sg", i)])
                ev_ga.i = i
            proj_fm(ti, n, "ga", oc, ev_ga)
            pA, kA = c.bank()
            items = []
            for kc in range(2):
                w, wk = W_(ti, ("oa", oc, kc))
                items.append((lambda e, w=w, kc=kc: e.matmul(pA[:, :n], lhsT=w, rhs=attb[:, kc, :n], start=(kc == 0), stop=(kc == 1)), [wk, ("attb", kc)]))
            c.mm(items, W=[kA])
            ia = ev_ga.i
            mi = 0
            c.op("dve", lambda e: e.tensor_tensor(out=m1[mi][:, :n], in0=pA[:, :n], in1=sg[ia][:, :n], op=ALU.mult), R=[kA, ("sg", ia)], W=[("m1", mi)])

            def ev_gb(p, k, oc=oc):
                i = cnt["sg"] % 2
                cnt["sg"] += 1
                c.op("act", lambda e: e.activation(out=sg[i][:, :n], in_=p[:, :n], func=AF.Sigmoid, bias=pp[:, PBGB + oc:PBGB + oc + 1]),
                     R=[k, "pp"], W=[("sg", i)])
                ev_gb.i = i
            proj_fm(ti, n, "gb", oc, ev_gb)
            pB, kB = c.bank()
            items = []
            for kc in range(4):
                w, wk = W_(ti, ("ob", oc, kc))
                items.append((lambda e, w=w, kc=kc: e.matmul(pB[:, :n], lhsT=w, rhs=yb[:, kc, :n], start=(kc == 0), stop=(kc == 3)), [wk, ("yb", kc)]))
            c.mm(items, W=[kB])
            ib = ev_gb.i
            c.op("dve", lambda e: e.tensor_tensor(out=sg[ib][:, :n], in0=pB[:, :n], in1=sg[ib][:, :n], op=ALU.mult), R=[kB, ("sg", ib)], W=[("sg", ib)])
            c.op("dve", lambda e: e.tensor_tensor(out=t8[:, oc, :n], in0=m1[mi][:, :n], in1=sg[ib][:, :n], op=ALU.add),
                 R=[("m1", mi), ("sg", ib)], W=[("t8", oc)])
        preload(AF.Ln)
        stt = stat_begin()
        for oc in range(DC):
            po, ko = c.bank()
            items = []
            for kc in range(DC):
                w, wk = W_(ti, ("wo", oc, kc))
                items.append((lambda e, w=w, kc=kc, po=po: e.matmul(po[:, :n], lhsT=w, rhs=t8[:, kc, :n], start=(kc == 0), stop=(kc == DC - 1)), [wk, ("t8", kc)]))
            c.mm(items, W=[ko])
            if oc >= 1:
                stat_add(stt, oc - 1, n, sq_x=True)
            resid(si, n, oc, po, ko, 1.0 / ALPHA, sq_x=True)
        stat_add(stt, DC - 1, n, sq_x=True)
        return stt

    mix_stage.pend = []

    mix_stage.pend = []

    def prompt_attention(j):
        pr = j % 2
        c.bank_rot = [0, 1, 2, 3]
        NUM = [(c.banks[4], ("pb", 4)), (c.banks[5], ("pb", 5))]
        LB = [(c.banks[6], ("pb", 6)), (c.banks[7], ("pb", 7))]
        started = set()

        def exp_mul(p, k, eb_ap, m, lo, hi, view):
            i = cnt["te"] % 2
            cnt["te"] += 1
            c.op("act", lambda e: e.activation(out=tmpE[i][:m, lo:hi], in_=p[:m, lo:hi], func=AF.Exp, scale=SCALE), R=[k], W=[("tmpE", i)])
            c.op("dve", lambda e: e.tensor_tensor(out=view(PT[i][:m, lo:hi]), in0=view(tmpE[i][:m, lo:hi]), in1=eb_ap, op=ALU.mult),
                 R=[("tmpE", i), "EB"], W=[("PT", i)])
            return PT[i], ("PT", i)

        def numl(items_spec, ptk, h):
            hc, hr = h // 2, (h % 2) * 64
            items = []
            for vap, vkey, rhs, osel, m in items_spec:
                for kind in (0, 1):
                    tgt, tk = (NUM if kind == 0 else LB)[hc]
                    st = (kind, h) not in started
                    started.add((kind, h))
                    lhs = vap if kind == 0 else ones64[:m, :]
                    items.append((lambda e, lhs=lhs, rhs=rhs, tgt=tgt, st=st, osel=osel: e.matmul(osel(tgt[hr:hr + 64, :]), lhsT=lhs, rhs=rhs, start=st,
                                                                                                  stop=False, skip_group_check=True),
                                  [vkey, ptk, "cb"]))
            c.mm(items, W=[NUM[hc][1], LB[hc][1]])

        jobs = []
        for g in range(2):
            for h in range(4):
                for which in (0, 1):
                    if which == 1 and j == 0 and g == 1:
                        continue

                    def st_fn(g=g, h=h, which=which):
                        kT, Vc = kT01[g], Vc01[g]
                        hc, hr = h // 2, (h % 2) * 64
                        qc = g * 2 + hc
                        p, k = c.bank()
                        items = []
                        specs = []
                        lo = 128 if (which == 1 and j == 0) else 0
                        for sbk in range(4):
                            if g == 0:
                                qcols = slice(sbk * 128, (sbk + 1) * 128)
                                if which == 0:
                                    kap = kT[hr:hr + 64, hc, pr, sbk * 128:(sbk + 1) * 128]
                                    kkey = ("kT", 0, hc, pr)
                                    vap, vkey = Vc[:, pr, sbk, h * 64:(h + 1) * 64], ("Vc", 0, pr, sbk)
                                elif sbk > 0:
                                    kap = kT[hr:hr + 64, hc, pr, (sbk - 1) * 128:sbk * 128]
                                    kkey = ("kT", 0, hc, pr)
                                    vap, vkey = Vc[:, pr, sbk - 1, h * 64:(h + 1) * 64], ("Vc", 0, pr, sbk - 1)
                                else:
                                    if j == 0:
                                        continue
                                    kap = kT[hr:hr + 64, hc, 1 - pr, 384:512]
                                    kkey = ("kT", 0, hc, 1 - pr)
                                    vap, vkey = Vc[:, 1 - pr, 3, h * 64:(h + 1) * 64], ("Vc", 0, 1 - pr, 3)
                            else:
                                qcols = slice(sbk, T, 4)
                                pp_ = pr if which == 0 else 1 - pr
                                kap = kT[hr:hr + 64, hc, pp_, sbk:T:4]
                                kkey = ("kT", 1, hc, pp_)
                                vap, vkey = Vc[:, pp_, sbk, h * 64:(h + 1) * 64], ("Vc", 1, pp_, sbk)
                            osel = (lambda t_, qcols=qcols: t_[:, qcols])
                            items.append((lambda e, kap=kap, qcols=qcols, sbk=sbk: e.matmul(p[:, sbk * 128:(sbk + 1) * 128], lhsT=kap,
                                                                                            rhs=qT[hr:hr + 64, qc, qcols], start=True, stop=True),
                                          [kkey, ("qT", qc)]))
                            specs.append((vap, vkey, sbk, osel))
                        c.mm(items, W=[k])
                        return p, k, specs, lo

                    def post_fn(state, g=g, h=h, which=which):
                        p, k, specs, lo = state
                        eb = EB[:, g * 4 + h, :]
                        nb_ = (512 - lo) // 128
                        ebs = eb[:, which * 128:(which + 1) * 128].unsqueeze(1).to_broadcast([128, nb_, 128])
                        ptb, ptk = exp_mul(p, k, ebs, 128, lo, 512, lambda a, nb_=nb_: a.rearrange("p (a b) -> p a b", a=nb_))
                        numl([(vap, vkey, ptb[:, sbk * 128:(sbk + 1) * 128], osel, 128) for vap, vkey, sbk, osel in specs], ptk, h)

                    jobs.append((st_fn, post_fn))
        M = 32 * (j + 1)
        for h in range(4):
            def st_fn(h=h):
                hc, hr = h // 2, (h % 2) * 64
                qc = 4 + hc
                p, k = c.bank()
                items = []
                for r in range(16):
                    items.append((lambda e, r=r: e.matmul(p[:M, r * 32:(r + 1) * 32], lhsT=kT2[hr:hr + 64, hc, r:512 * (j + 1):16],
                                                          rhs=qT[hr:hr + 64, qc, r:T:16], start=True, stop=True),
                                  [("kT", 2, hc, jj) for jj in range(j + 1)] + [("qT", qc)]))
                c.mm(items, W=[k])
                return p, k

            def post_fn(state, h=h):
                p, k = state
                ebs = EB[:M, 8 + h, 32 * j:32 * (j + 1)].unsqueeze(1).to_broadcast([M, 16, 32])
                ptb, ptk = exp_mul(p, k, ebs, M, 0, 512, lambda a: a.rearrange("p (a b) -> p a b", a=16))
                numl([(Vc2[:M, r, h * 64:(h + 1) * 64], ("Vc", 2, r), ptb[:M, r * 32:(r + 1) * 32], (lambda t_, r=r: t_[:, r:T:16]), M)
                      for r in range(16)], ptk, h)

            jobs.append((st_fn, post_fn))
        states = [None] * len(jobs)
        states[0] = jobs[0][0]()
        for i in range(len(jobs)):
            if i + 1 < len(jobs):
                states[i + 1] = jobs[i + 1][0]()
            jobs[i][1](states[i])
            if sgu_pending:
                sgu_pending.pop(0)()
        for hc in range(2):
            c.op("dve", lambda e, hc=hc: e.reciprocal(out=rL[:, :], in_=LB[hc][0][:, :]), R=[LB[hc][1]], W=["rL"])
            c.op("dve", lambda e, hc=hc: e.tensor_tensor(out=attb[:, hc, :], in0=NUM[hc][0][:, :], in1=rL[:, :], op=ALU.mult),
                 R=[NUM[hc][1], "rL"], W=[("attb", hc)])
        c.bank_rot = list(range(8))
        c.bank_i = 0

    def prompt_tile(ti, s, j):
        if ti >= 1:
            for _ in range(4):
                if sw_pending:
                    sw_pending.pop(0)()
        si = ti % 2
        n = T
        pr = j % 2
        cast_stream(si, n)
        st1 = ffn(ti, si, n, "f1")
        if ti + 1 < len(tiles):
            load_x(ti + 1)
        if debug and ti == 0:
            c.dma(dbg["r1"][:, :, :], S[si][:, :, :], R=[("S", si, dc) for dc in range(DC)])
        layer_norm(si, n, 1, pre=st1)
        if debug and ti == 0:
            c.dma(dbg["h"][:, :, :], S[si][:, :, :], R=[("S", si, dc) for dc in range(DC)])
        for cc in range(16):
            if cc < 6:
                def ev(p, k, cc=cc):
                    c.op("act", lambda e: e.activation(out=qT[:, cc, :], in_=p[:, :], func=AF.Identity, bias=pp[:, PBQ + cc:PBQ + cc + 1]),
                         R=[k, "pp"], W=[("qT", cc)])
            elif cc < 12:
                def ev(p, k, cc=cc):
                    g, hc = (cc - 6) // 2, (cc - 6) % 2
                    if g < 2:
                        dst, key = kT01[g][:, hc, pr, :], ("kT", g, hc, pr)
                    else:
                        dst, key = kT2[:, hc, j * T:(j + 1) * T], ("kT", 2, hc, j)
                    c.op("act", lambda e: e.activation(out=dst, in_=p[:, :], func=AF.Identity, bias=pp[:, PBK + cc - 6:PBK + cc - 5]),
                         R=[k, "pp"], W=[key])
            else:
                def ev(p, k, cc=cc):
                    c.op("act", lambda e: e.activation(out=uT[:, cc - 12, :], in_=p[:, :], func=AF.Gelu, bias=pp[:, PBU + cc - 12:PBU + cc - 11]),
                         R=[k, "pp"], W=[("uT", cc - 12)])
            proj_fm(ti, n, "wq", cc, ev)
        for cg in (3, 0, 1, 2):
            tb_steps = []
            for mb in range(4):
                if cg == 1:
                    lcols = lambda dc, mb=mb: Sb[:, dc, mb:T:4]
                else:
                    lcols = lambda dc, mb=mb: Sb[:, dc, mb * 128:(mb + 1) * 128]
                p, k = c.bank()
                items = []
                need_k = cg >= 2 or (cg == 1 and j == NTL - 1) or (cg == 0 and j == NTL - 1 and mb == 3)
                c0 = 0 if need_k else 256
                for dc in range(DC):
                    w, wk = W_(ti, ("wtm", cg, dc, 0), 512)
                    items.append((lambda e, w=w, dc=dc, lcols=lcols, c0=c0: e.matmul(p[:, c0:512], lhsT=lcols(dc), rhs=w[:, c0:512],
                                                                                     start=(dc == 0), stop=(dc == DC - 1)),
                                  [wk, ("Sb", dc)]))
                c.mm(items, W=[k])
                if cg == 3:
                    tb_steps.append(sgu_rows(p, k, 128, mb, False))
                    continue
                g = cg
                ki = cnt["kv"] % 4
                cnt["kv"] += 1
                kv = kvst[ki]
                c.op("dve", lambda e: e.tensor_tensor(out=kv[:, c0:512], in0=p[:, c0:512], in1=bt[:, BKV + g * 512 + c0:BKV + (g + 1) * 512], op=ALU.add),
                     R=[k, "bt"], W=[("kvst", ki)])
                if g < 2:
                    c.op("act", lambda e: e.copy(out=Vc01[g][:, pr, mb, :], in_=kv[:, 256:512]), R=[("kvst", ki)], W=[("Vc", g, pr, mb)])
                if g == 0 and j == NTL - 1 and mb == 3:
                    c.dma(pw[0][s, :, :], kv[:], R=[("kvst", ki)])
                if g == 1 and j == NTL - 1:
                    c.dma(pw[1][s, mb:512:4, :], kv[:], R=[("kvst", ki)])
                if g == 2:
                    c.dma(pw[2][s, T * j + mb * 128:T * j + (mb + 1) * 128, :], kv[:], R=[("kvst", ki)], W=[("pw2", mb)])
            if cg == 2:
                src = bass.AP(tensor=pw[2].tensor, offset=(s * SEQ + T * j) * 512 + 256, ap=[[16 * 512, 32], [512, 16], [1, 256]])
                c.dma(Vc2[32 * j:32 * (j + 1), :, :], src, R=[("pw2", mb) for mb in range(4)], W=[("Vc", 2, r) for r in range(16)])
            if cg == 3:
                preload(AF.Exp)
                for st_i in range(len(tb_steps[0])):
                    sgu_pending.append(lambda st_i=st_i, tbs=tb_steps: [tbs[tb][st_i]() for tb in range(4)])
        prompt_attention(j)
        while sgu_pending:
            sgu_pending.pop(0)()
        sgu_mix(n, False)
        st2 = mix_stage(ti, si, n)
        if debug and ti == 0:
            c.dma(dbg["r2"][:, :, :], S[si][:, :, :], R=[("S", si, dc) for dc in range(DC)])
        layer_norm(si, n, 2, pre=st2)
        if debug and ti == 0:
            c.dma(dbg["h2"][:, :, :], S[si][:, :, :], R=[("S", si, dc) for dc in range(DC)])
        ffn(ti, si, n, "f2", alt=True)
        layer_norm(si, n, 3, need_bf16=False,
                   defer=lambda: c.dma(yp[s, :, :, j * T:(j + 1) * T], S[si][:, :, :], R=[("S", si, dc) for dc in range(DC)]))

    tiles = [(s, j) for s in range(nseq) for j in range(NTL)]

    def load_x(ti):
        s, j = tiles[ti]
        c.dma(S[ti % 2][:, :, :], xp[s, :, :, j * T:(j + 1) * T], W=[("S", ti % 2, dc) for dc in range(DC)])

    load_x(0)
    for ti, (s, j) in enumerate(tiles):
        prompt_tile(ti, s, j)
    while ln_pending:
        ln_pending.pop(0)()
    while sw_pending:
        sw_pending.pop(0)()

    if do_sample:
        c.barrier()
        pm.close()
        bufs["h1"] = c.sb("h1s", [128, FC, TS], BF16)
        qbs = [c.sb("qb%d" % i, [128, 4, 768], BF16) for i in range(2)]
        KVa = [c.sb("KVa0", [128, 1, 512], BF16), c.sb("KVa1", [128, 4, 512], BF16), c.sb("KVa2", [128, 4, 512], BF16)]
        prod = c.sb("prod", [128, 4, 256], BF16)
        PVt = c.sb("PVt", [128, 4, 260], BF16)
        prodB = c.sb("prodB", [128, 4, 256], F32)
        PVB = c.sb("PVB", [128, 4, 260], F32)
        sA = c.sb("sA", [128, 16], F32)
        biasA = c.sb("biasA", [128, 48], F32)
        biasB = c.sb("biasB", [128, 48], F32)
        KVn = c.sb("KVn", [128, 3, 512], F32)
        qB4 = c.sb("qB4", [128, 4, 768], F32)
        qtok = c.sb("qtok", [128, 768], F32)
        att = c.sb("att", [128, 256], F32)
        racc = c.sb("racc", [128, 4], F32)
        ohA = c.sb("ohA", [32, 12 * 128], F32)
        ohB = c.sb("ohB", [32, 12 * 64], F32)
        mkA = c.sb("mkA", [128, 48], F32)
        mkB = c.sb("mkB", [64, 48], F32)
        bts = c.sb("bts", [128, 1024], F32)
        bufs["bts"] = bts
        c.dma(bts[:], bts_d[:, :], W=["bts"])
        c.dma(ohA[:], ohA_d[:, :], W=["ohA"])
        c.dma(ohB[:], ohB_d[:, :], W=["ohB"])
        c.dma(mkA[:], mkA_d[:, :], W=["mkA"])
        c.dma(mkB[:], mkB_d[:, :], W=["mkB"])
        pA, kA = c.bank()
        c.mm([(lambda e, i=i: e.matmul(pA[:, 4 * i:4 * i + 4], lhsT=ohA[:, 128 * i:128 * (i + 1)], rhs=relb[:, 4 * (i // 4):4 * (i // 4) + 4],
                                       start=True, stop=True), ["ohA", "relb"]) for i in range(12)], W=[kA])
        c.op("dve", lambda e: e.tensor_tensor(out=biasA[:, :], in0=pA[:, 0:48], in1=mkA[:, :], op=ALU.add), R=[kA, "mkA"], W=["biasA"])
        pB, kB = c.bank()
        c.mm([(lambda e, i=i: e.matmul(pB[0:64, 4 * i:4 * i + 4], lhsT=ohB[:, 64 * i:64 * (i + 1)], rhs=relb[:, 4 * (i // 4):4 * (i // 4) + 4],
                                       start=True, stop=True), ["ohB", "relb"]) for i in range(12)], W=[kB])
        c.op("dve", lambda e: e.tensor_tensor(out=biasB[0:64, :], in0=pB[0:64, 0:48], in1=mkB[:, :], op=ALU.add), R=[kB, "mkB"], W=["biasB"])

        ti = len(tiles)
        si, n = 0, TS
        c.dma(S[0][:, :, :n], xs[:, :, :], W=[("S", 0, dc) for dc in range(DC)])
        cast_stream(si, n)
        st1 = ffn(ti, si, n, "f1")
        layer_norm(si, n, 1, pre=st1)
        pq = [c.bank(), c.bank()]
        for cc in range(6):
            p, k = pq[cc // 4]
            items = []
            for dc in range(DC):
                w, wk = W_(ti, ("wq", cc, dc))
                items.append((lambda e, w=w, dc=dc, cc=cc, p=p: e.matmul(p[0:n, (cc % 4) * 128:(cc % 4 + 1) * 128], lhsT=Sb[:, dc, 0:n], rhs=w,
                                                                       start=(dc == 0), stop=(dc == DC - 1)), [wk, ("Sb", dc)]))
            c.mm(items, W=[k])
        c.op("dve", lambda e: e.tensor_tensor(out=qtok[0:n, 0:512], in0=pq[0][0][0:n, :], in1=bts[0:n, BQ:BQ + 512], op=ALU.add), R=[pq[0][1], "bts"], W=["qtok"])
        c.op("dve", lambda e: e.tensor_tensor(out=qtok[0:n, 512:768], in0=pq[1][0][0:n, 0:256], in1=bts[0:n, BQ + 512:BQ + 768], op=ALU.add),
             R=[pq[1][1], "bts", "qtok"], W=["qtok"])
        c.dma(qsc_t.ap()[:, :], qtok[0:n, :], R=["qtok"], W=["qsc"])
        for cc in range(12, 16):
            def ev(p, k, cc=cc):
                c.op("act", lambda e: e.activation(out=uT[:, cc - 12, :n], in_=p[:, :n], func=AF.Gelu, bias=pp[:, PBU + cc - 12:PBU + cc - 11]),
                     R=[k, "pp"], W=[("uT", cc - 12)])
            proj_fm(ti, n, "wq", cc, ev)
        for cg in (3, 0, 1, 2):
            p, k = c.bank()
            items = []
            for dc in range(DC):
                w, wk = W_(ti, ("wtm", cg, dc, 0), 512)
                items.append((lambda e, w=w, dc=dc, p=p: e.matmul(p[0:n, :], lhsT=Sb[:, dc, 0:n], rhs=w, start=(dc == 0), stop=(dc == DC - 1)),
                              [wk, ("Sb", dc)]))
            c.mm(items, W=[k])
            if cg == 3:
                for st_ in sgu_rows(p, k, n, 0, True):
                    st_()
                continue
            g = cg
            c.op("dve", lambda e: e.tensor_tensor(out=KVn[0:n, g, :], in0=p[0:n, :], in1=bt[0:n, BKV + g * 512:BKV + (g + 1) * 512], op=ALU.add),
                 R=[k, "bt"], W=[("KVn", g)])
            for b in range(SB):
                c.dma(sw[g][b, WIN[g] - 4:WIN[g], :], KVn[4 * b:4 * b + 4, g, :], R=[("KVn", g)])
        acc, kacc = c.bank()
        first = [True]

        def att_block(m, Kap, Vap, qap, bias_ap, pv, sa, sel_fn, keysR, prod=prod):
            c.op("dve", lambda e: e.tensor_tensor(out=prod[:m], in0=Kap, in1=qap, op=ALU.mult), R=keysR, W=["prod"])
            c.op("dve", lambda e: e.tensor_reduce(out=sa[:m, :], in_=prod[:m].rearrange("p t (h e) -> p (t h) e", e=64), axis=mybir.AxisListType.X,
                                                  op=ALU.add), R=["prod"], W=["sA"])
            c.op("dve", lambda e: e.scalar_tensor_tensor(out=sa[:m, :], in0=sa[:m, :], scalar=SCALE, in1=bias_ap, op0=ALU.mult, op1=ALU.add),
                 R=["sA", "biasA", "biasB"], W=["sA"])
            c.op("act", lambda e: e.activation(out=pv[:m, :, 256:260], in_=sa[:m, :].rearrange("p (t h) -> p t h", t=4), func=AF.Exp),
                 R=["sA"], W=["PVp"])
            c.op("dve", lambda e: e.tensor_tensor(out=pv[:m, :, 0:256].rearrange("p t (h e) -> p t h e", e=64), in0=Vap,
                                                  in1=pv[:m, :, 256:260].unsqueeze(3).to_broadcast([m, 4, 4, 64]), op=ALU.mult),
                 R=keysR + ["PVp"], W=["PVv"])
            items = []
            for t in range(4):
                st = first[0]
                first[0] = False
                items.append((lambda e, t=t, st=st: e.matmul(acc[0:64, 0:260], lhsT=sel_fn(t), rhs=pv[:m, t, :], start=st, stop=False,
                                                             skip_group_check=True), ["PVp", "PVv", "cf"]))
            c.mm(items, W=[kacc])

        for b in range(SB):
            src = bass.AP(tensor=qsc_t, offset=b * 3072, ap=[[0, 128], [1, 3072]])
            qb = qbs[b % 2]
            c.dma(qb[:, :, :], src, R=["qsc"], W=[("qb", b % 2)])
            for g in range(3):
                nT = 1 if g == 0 else 4
                if g == 0:
                    c.dma(KVa[0][:, 0, :], st_d[0][b, :, :], W=[("KVa", 0)])
                else:
                    src = bass.AP(tensor=st_d[g].tensor, offset=b * WIN[g] * 512, ap=[[DIL[g] * 512, 128], [512, 4], [1, 512]])
                    c.dma(KVa[g][:, :, :], src, W=[("KVa", g)])
                if g == 0:
                    Kap = KVa[0][:, 0:1, 0:256].to_broadcast([128, 4, 256])
                    Vap = KVa[0][:, 0:1, 256:512].to_broadcast([128, 4, 256]).rearrange("p t (h e) -> p t h e", e=64)
                else:
                    Kap = KVa[g][:, :, 0:256]
                    Vap = KVa[g][:, :, 256:512].rearrange("p t (h e) -> p t h e", e=64)
                att_block(128, Kap, Vap, qb[:, :, g * 256:(g + 1) * 256], biasA[:, g * 16:(g + 1) * 16], PVt, sA,
                          lambda t, b=b: zselb[:, 63 - (4 * b + t):127 - (4 * b + t)], [("KVa", g), ("qb", b % 2)])
        for b in range(SB):
            src = bass.AP(tensor=qsc_t, offset=b * 3072, ap=[[0, 4], [1, 3072]])
            c.dma(qB4[4 * b:4 * b + 4, :, :], src, R=["qsc"], W=[("qB4", b)])
        for g in range(3):
            Kap = KVn[0:64, g:g + 1, 0:256].to_broadcast([64, 4, 256])
            Vap = KVn[0:64, g:g + 1, 256:512].to_broadcast([64, 4, 256]).rearrange("p t (h e) -> p t h e", e=64)
            att_block(64, Kap, Vap, qB4[0:64, :, g * 256:(g + 1) * 256], biasB[0:64, g * 16:(g + 1) * 16], PVB, sA,
                      lambda t: gsel[0:64, t * 64:(t + 1) * 64], [("KVn", g)] + [("qB4", b) for b in range(SB)], prod=prodB)
        c.op("dve", lambda e: e.reciprocal(out=racc[0:64, :], in_=acc[0:64, 256:260]), R=[kacc], W=["racc"])
        c.op("dve", lambda e: e.tensor_tensor(out=att[0:64, :].rearrange("p (h e) -> p h e", e=64), in0=acc[0:64, 0:256].rearrange("p (h e) -> p h e", e=64),
                                              in1=racc[0:64, :].unsqueeze(2).to_broadcast([64, 4, 64]), op=ALU.mult), R=[kacc, "racc"], W=["att"])
        for hc in range(2):
            ptr, ktr = c.bank()
            c.mm([(lambda e, hc=hc, ptr=ptr: e.transpose(ptr[:, 0:64], att[0:64, hc * 128:(hc + 1) * 128], identf[0:64, 0:64]), ["att", "cf"])], W=[ktr])
            c.op("act", lambda e, hc=hc, ptr=ptr: e.copy(out=attb[:, hc, 0:64], in_=ptr[:, 0:64]), R=[ktr], W=[("attb", hc)])
        sgu_mix(n, True)
        st2 = mix_stage(ti, si, n)
        layer_norm(si, n, 2, pre=st2)
        st3 = ffn(ti, si, n, "f2")
        layer_norm(si, n, 3, need_bf16=False, pre=st3)
        c.dma(ys[:, :, :], S[si][:, :, :n], R=[("S", si, dc) for dc in range(DC)])

    c.finish()
    c.es.close()
    return nc, c


def _host_consts():
    cf = np.zeros((128, 128 + 128 + 64 + 127 + 256), np.float32)
    cf[:, 0:128] = np.eye(128, dtype=np.float32)
    cf[:, 128:256] = np.triu(np.ones((128, 128), np.float32))
    m = np.zeros((64, 64), np.float32)
    for b in range(16):
        m[4 * b:4 * b + 4, 4 * b:4 * b + 4] = np.triu(np.ones((4, 4), np.float32))
    cf[0:64, 256:320] = m
    cf[:, 320 + 63] = 1.0
    gs = np.zeros((64, 4, 64), np.float32)
    for b in range(16):
        for t in range(4):
            gs[4 * b:4 * b + 4, t, 4 * b + t] = 1.0
    cf[0:64, 447:703] = gs.reshape(64, 256)
    cb = np.zeros((128, 447), np.float32)
    cb[:, 320 + 63] = 1.0
    cb[:, 0:128] = 1.0 / 1024.0
    cb[:, 128:192] = 1.0
    cb[:, 192:320] = np.eye(128, dtype=np.float32)[::-1]
    rel = np.arange(383) - 127
    oh = np.zeros((32, 3, 383), np.float32)
    mka = np.zeros((4, 3, 383), np.float32)
    for g in range(3):
        bk = _t5_bucket(np.clip(rel, 0, 128) * DIL[g])
        oh[bk, g, np.arange(383)] = 1.0
        mka[:, g, :] = np.where((rel >= 0) & (rel <= 128), 0.0, NEGB)[None]
    return cf, cb.astype(ml_dtypes.bfloat16), oh.reshape(32, 3 * 383), mka.reshape(4, 3 * 383)


def _fm(vec, n):
    return np.ascontiguousarray(np.asarray(vec, np.float32).reshape(n, 128).T)


def kernel(x_prompt, x_sample, state_win0, state_win1, state_win2, rel_bias,
           ln1_g, ln1_b, f1_w1, f1_w3, f1_w2, w_in, b_in,
           sgu_ln_g, sgu_ln_b, sgu_ws, sgu_b, w_oa, w_ob, w_out,
           ln2_g, ln2_b, f2_w1, f2_w3, f2_w2, ln3_g, ln3_b, _nseq=NSEQ, _ncores=NCORES, _do_sample=True, _debug=False):
    f32 = np.float32
    Wd = {"f1_w1": f1_w1[0], "f1_w3": f1_w3[0], "f1_w2": f1_w2[0], "w_in": w_in[0], "w_oa": w_oa[0], "w_ob": w_ob[0],
          "w_out": w_out[0], "f2_w1": f2_w1[0], "f2_w3": f2_w3[0], "f2_w2": f2_w2[0]}
    wst = np.zeros((128, NCH * CHW), f32)
    for i, blk in enumerate(BLOCKS):
        if blk is not None:
            name, r0, c0 = blk
            wst[:, i * 128:(i + 1) * 128] = Wd[name][r0:r0 + 128, c0:c0 + 128]
    bi = np.asarray(b_in[0], f32)
    pp = np.zeros((128, 80), f32)
    for i, v in enumerate((ln1_g, ln1_b, ln2_g, ln2_b, ln3_g, ln3_b)):
        pp[:, 8 * i:8 * i + 8] = _fm(v[0], 8)
    pp[:, 48:54] = _fm(bi[OQ:OQ + 768], 6)
    pp[:, 54:60] = _fm(bi[OK_:OK_ + 768], 6)
    pp[:, 60:64] = _fm(bi[OU:OU + 512], 4)
    pp[:, 64:72] = _fm(bi[OGA:OGA + 1024], 8)
    pp[:, 72:80] = _fm(bi[OGB:OGB + 1024], 8)
    rows = []
    for g in range(3):
        rows += [bi[OK_ + g * 256:OK_ + (g + 1) * 256], bi[OV + g * 256:OV + (g + 1) * 256]]
    rows += [bi[OVV:OVV + 512], sgu_ln_g[0], sgu_ln_b[0], np.asarray(sgu_b[0], f32).reshape(512)]
    btrow = np.concatenate([np.asarray(r, f32).reshape(-1) for r in rows])
    bt = np.ascontiguousarray(np.broadcast_to(btrow[None, :], (128, btrow.size)))
    btsrow = np.concatenate([bi[OQ:OQ + 768], np.tile(np.asarray(sgu_b[0], f32)[:, None, 0:4], (1, 16, 1)).reshape(256)])
    bts = np.ascontiguousarray(np.broadcast_to(btsrow[None, :], (128, 1024)))
    wsT = np.ascontiguousarray(np.transpose(np.asarray(sgu_ws[0], f32), (2, 0, 1)))
    wsTs = np.zeros((128, 4, 64), f32)
    wsTs[0:64] = np.tile(np.transpose(np.asarray(sgu_ws[0], f32)[:, 0:4, 0:4], (2, 0, 1)), (16, 1, 16))
    cf, cb, oh, mka = _host_consts()
    ohA = np.zeros((32, 12, 128), f32)
    mkA = np.zeros((128, 3, 4, 4), f32)
    ohB = np.zeros((32, 12, 64), f32)
    mkB = np.zeros((64, 3, 4, 4), f32)
    pidx = np.arange(128)
    for g in range(3):
        for t in range(4):
            if g == 0:
                dist = 128 + t - pidx
                valid = dist <= 128
            else:
                dist = (128 - pidx) * DIL[g]
                valid = np.ones(128, bool)
            ohA[_t5_bucket(np.clip(dist, 0, 128 * DIL[g])), g * 4 + t, pidx] = 1.0
            mkA[:, g, t, :] = np.where(valid, 0.0, NEGB)[:, None]
            tp = np.arange(64) % 4
            dB = t - tp
            vB = (dB >= 0) if g == 0 else (dB == 0)
            ohB[_t5_bucket(np.clip(dB, 0, 3) * DIL[g]), g * 4 + t, np.arange(64)] = 1.0
            mkB[:, g, t, :] = np.where(vB, 0.0, NEGB)[:, None]
    common = {"wst": wst, "pp": pp, "bt": bt, "bts": bts, "wsT": wsT, "wsTs": wsTs, "relb": np.asarray(rel_bias, f32), "cf": cf, "cb": cb,
              "oh": oh, "mka": mka,
              "ohA": ohA.reshape(32, 12 * 128), "mkA": mkA.reshape(128, 48), "ohB": ohB.reshape(32, 12 * 64), "mkB": mkB.reshape(64, 48)}
    xpT = np.asarray(x_prompt, f32).reshape(-1, SEQ, DC, 128)
    xsT = np.asarray(x_sample, f32).reshape(-1, 4, DC, 128)
    sts = [np.asarray(a, f32)[0].reshape(a.shape[1], a.shape[2], 512) for a in (state_win0, state_win1, state_win2)]
    in_maps = []
    for cid in range(_ncores):
        m = dict(common)
        m["xp"] = np.ascontiguousarray(np.transpose(xpT[cid * _nseq:(cid + 1) * _nseq], (0, 3, 2, 1)))
        m["xs"] = np.ascontiguousarray(np.transpose(xsT[cid * SB:(cid + 1) * SB].reshape(TS, DC, 128), (2, 1, 0)))
        for g in range(3):
            m["st%d" % g] = np.ascontiguousarray(sts[g][cid * SB:(cid + 1) * SB])
        in_maps.append(m)
    nc, _ = build(_nseq, _do_sample, _debug)
    res = run_bass_kernel_spmd(nc, in_maps, core_ids=list(range(_ncores)))
    R = res.results
    yp = np.concatenate([np.transpose(r["yp"], (0, 3, 2, 1)).reshape(_nseq, SEQ, D) for r in R], 0)
    ys = np.concatenate([np.transpose(r["ys"], (2, 1, 0)).reshape(SB, 4, D) for r in R], 0)
    outs = [yp, ys]
    for g in range(3):
        outs.append(np.concatenate([r["pw%d" % g].reshape(_nseq, WIN[g], 2, 4, 64) for r in R], 0)[None])
    for g in range(3):
        outs.append(np.concatenate([r["sw%d" % g].reshape(SB, WIN[g], 2, 4, 64) for r in R], 0)[None])
    outs.append(np.concatenate([r["sv"].reshape(SB, 4, 512) for r in R], 0)[None])
    return tuple(np.ascontiguousarray(o, dtype=np.float32) for o in outs)
```
